# Optimizing a Trainium2 kernel written in Bass

```python
import jax
import jax.numpy as jnp
from jax import lax
import numpy as np

D_MODEL = 4096
BATCH = 1
SEQ = 8192
DEPTH = 1

HEAD_DIM = 128
ATTN_HEADS = D_MODEL // 256
ATTN_WIDTH = ATTN_HEADS * HEAD_DIM
Q_BLOCK = 128
LRU_WIDTH = D_MODEL // 2
LRU_GROUPS = 16
LRU_GROUP_DIM = LRU_WIDTH // LRU_GROUPS
CONV_WIDTH = 4
LRU_C = 8.0
PEER_HEADS = 8
PEER_N_KEYS = 128
PEER_N_EXPERTS = PEER_N_KEYS * PEER_N_KEYS
PEER_TOPK = 16
PEER_KEY_DIM = 256
PEER_HALF = PEER_KEY_DIM // 2
PEER_CHUNK = 64
N_MOD = 6
EPS = 1e-6

Q_END = ATTN_WIDTH
K_END = Q_END + ATTN_WIDTH
V_END = K_END + ATTN_WIDTH
F_END = V_END + ATTN_HEADS
LX_END = F_END + LRU_WIDTH
LY_END = LX_END + LRU_WIDTH
GA_END = LY_END + D_MODEL
IN_WIDTH = GA_END + D_MODEL

kernel_name = "hybrid_fox_rglru_peer_block"


def _rmsnorm(x, g):
    xf = x.astype(jnp.float32)
    y = xf * lax.rsqrt(jnp.mean(xf * xf, axis=-1, keepdims=True) + EPS)
    return (y * g.astype(jnp.float32)).astype(x.dtype)


def _forgetting_attention(q, k, v, log_f):
    B, S, H, Dh = q.shape
    nb = S // Q_BLOCK
    F = jnp.cumsum(log_f, axis=1)
    Fh = jnp.transpose(F, (0, 2, 1))
    kh = jnp.transpose(k, (0, 2, 1, 3))
    vh = jnp.transpose(v, (0, 2, 1, 3))
    qb = q.reshape(B, nb, Q_BLOCK, H, Dh).transpose(1, 0, 3, 2, 4)
    Fb = Fh.reshape(B, H, nb, Q_BLOCK).transpose(2, 0, 1, 3)
    k_pos = jnp.arange(S)
    scale = Dh ** -0.5

    def block(args):
        q_blk, f_blk, blk_idx = args
        s = jnp.einsum('bhqd,bhkd->bhqk', q_blk, kh).astype(jnp.float32) * scale
        s = s + f_blk[..., :, None] - Fh[:, :, None, :]
        q_pos = blk_idx * Q_BLOCK + jnp.arange(Q_BLOCK)
        s = jnp.where(q_pos[:, None] >= k_pos[None, :], s, -jnp.inf)
        p = jax.nn.softmax(s, axis=-1)
        return jnp.einsum('bhqk,bhkd->bhqd', p.astype(vh.dtype), vh)

    out = lax.map(block, (qb, Fb, jnp.arange(nb)))
    return out.transpose(1, 0, 3, 2, 4).reshape(B, S, H * Dh)


def _causal_conv(x, w, b):
    C = x.shape[-1]
    y = lax.conv_general_dilated(
        x, w[:, None, :].astype(x.dtype), window_strides=(1,),
        padding=[(CONV_WIDTH - 1, 0)], dimension_numbers=('NWC', 'WIO', 'NWC'),
        feature_group_count=C)
    return y + b


def _rg_lru(xc, w_a, b_a, w_x, b_x, lam):
    B, S, C = xc.shape
    xg = xc.reshape(B, S, LRU_GROUPS, LRU_GROUP_DIM)
    r_gate = jax.nn.sigmoid((jnp.einsum('bsgi,gij->bsgj', xg, w_a).reshape(B, S, C) + b_a).astype(jnp.float32))
    i_gate = jax.nn.sigmoid((jnp.einsum('bsgi,gij->bsgj', xg, w_x).reshape(B, S, C) + b_x).astype(jnp.float32))
    log_a = -LRU_C * r_gate * jax.nn.softplus(-lam.astype(jnp.float32))
    a = jnp.exp(log_a)
    b = jnp.sqrt(-jnp.expm1(2.0 * log_a)) * (i_gate * xc.astype(jnp.float32))

    def combine(left, right):
        a_l, b_l = left
        a_r, b_r = right
        return a_l * a_r, a_r * b_l + b_r

    _, h = lax.associative_scan(combine, (a, b), axis=1)
    return h.astype(xc.dtype)


def _peer(h, w_q, subkeys, u, v):
    B, S, D = h.shape
    q = jnp.einsum('bsd,dk->bsk', h, w_q).reshape(B, S, PEER_HEADS, 2, PEER_HALF)
    s = jnp.einsum('bshpd,hpnd->bshpn', q, subkeys).astype(jnp.float32)
    s_top, i_top = lax.top_k(s, PEER_TOPK)
    cand = s_top[..., 0, :, None] + s_top[..., 1, None, :]
    cand_idx = i_top[..., 0, :, None] * PEER_N_KEYS + i_top[..., 1, None, :]
    cand = cand.reshape(B, S, PEER_HEADS, PEER_TOPK * PEER_TOPK)
    cand_idx = cand_idx.reshape(B, S, PEER_HEADS, PEER_TOPK * PEER_TOPK)
    best, pos = lax.top_k(cand, PEER_TOPK)
    expert_idx = jnp.take_along_axis(cand_idx, pos, axis=-1)
    gate = jax.nn.softmax(best, axis=-1).astype(h.dtype)

    nc = (B * S) // PEER_CHUNK
    hc = h.reshape(nc, PEER_CHUNK, D)
    ic = expert_idx.reshape(nc, PEER_CHUNK, PEER_HEADS, PEER_TOPK)
    gc = gate.reshape(nc, PEER_CHUNK, PEER_HEADS, PEER_TOPK)

    def chunk(args):
        xt, idx, g = args
        u_sel = u[idx]
        act = jax.nn.gelu(jnp.einsum('td,thkd->thk', xt, u_sel), approximate=False)
        v_sel = v[idx]
        return jnp.einsum('thk,thkd->td', g * act, v_sel)

    out = lax.map(chunk, (hc, ic, gc))
    return out.reshape(B, S, D)


def setup_inputs(seed: int = 0) -> dict:
    key = jax.random.key(seed)
    ks = jax.random.split(key, 26)
    L = DEPTH

    def nrm(k, shape, s):
        return jax.random.normal(k, shape, jnp.float32) * s

    x = nrm(ks[0], (BATCH, SEQ, D_MODEL), 1.0)
    c = nrm(ks[1], (BATCH, D_MODEL), 1.0)
    w_ada = nrm(ks[2], (L, D_MODEL, N_MOD * D_MODEL), D_MODEL ** -0.5)
    b_ada = nrm(ks[3], (L, N_MOD * D_MODEL), 0.02)
    norm_mix_g = 1.0 + nrm(ks[4], (L, D_MODEL), 0.02)
    norm_ffn_g = 1.0 + nrm(ks[5], (L, D_MODEL), 0.02)
    w_in = nrm(ks[6], (L, D_MODEL, IN_WIDTH), D_MODEL ** -0.5)
    b_in = nrm(ks[7], (L, IN_WIDTH), 0.02)
    forget_bias = jax.random.uniform(ks[8], (L, ATTN_HEADS), jnp.float32, 1.0, 4.0)
    b_in = b_in.at[:, V_END:F_END].add(forget_bias)
    q_norm_g = 1.0 + nrm(ks[9], (L, HEAD_DIM), 0.02)
    k_norm_g = 1.0 + nrm(ks[10], (L, HEAD_DIM), 0.02)
    conv_w = nrm(ks[11], (L, CONV_WIDTH, LRU_WIDTH), CONV_WIDTH ** -0.5)
    conv_b = nrm(ks[12], (L, LRU_WIDTH), 0.02)
    lru_wa = nrm(ks[13], (L, LRU_GROUPS, LRU_GROUP_DIM, LRU_GROUP_DIM), LRU_GROUP_DIM ** -0.5)
    lru_ba = nrm(ks[14], (L, LRU_WIDTH), 0.02)
    lru_wx = nrm(ks[15], (L, LRU_GROUPS, LRU_GROUP_DIM, LRU_GROUP_DIM), LRU_GROUP_DIM ** -0.5)
    lru_bx = nrm(ks[16], (L, LRU_WIDTH), 0.02)
    a_c = jax.random.uniform(ks[17], (L, LRU_WIDTH), jnp.float32, 0.9, 0.999)
    sig = a_c ** (1.0 / LRU_C)
    lru_lambda = jnp.log(sig) - jnp.log1p(-sig)
    w_attn_o = nrm(ks[18], (L, ATTN_WIDTH, D_MODEL), ATTN_WIDTH ** -0.5)
    w_lru_o = nrm(ks[19], (L, LRU_WIDTH, D_MODEL), LRU_WIDTH ** -0.5)
    w_out = nrm(ks[20], (L, D_MODEL, D_MODEL), D_MODEL ** -0.5)
    peer_wq = nrm(ks[21], (L, D_MODEL, PEER_HEADS * PEER_KEY_DIM), D_MODEL ** -0.5)
    peer_subkeys = nrm(ks[22], (L, PEER_HEADS, 2, PEER_N_KEYS, PEER_HALF), PEER_HALF ** -0.5)
    peer_u = nrm(ks[23], (L, PEER_N_EXPERTS, D_MODEL), D_MODEL ** -0.5)
    peer_v = nrm(ks[24], (L, PEER_N_EXPERTS, D_MODEL), 0.5)
    return {"x": x, "c": c, "w_ada": w_ada, "b_ada": b_ada,
            "norm_mix_g": norm_mix_g, "norm_ffn_g": norm_ffn_g,
            "w_in": w_in, "b_in": b_in, "q_norm_g": q_norm_g, "k_norm_g": k_norm_g,
            "conv_w": conv_w, "conv_b": conv_b, "lru_wa": lru_wa, "lru_ba": lru_ba,
            "lru_wx": lru_wx, "lru_bx": lru_bx, "lru_lambda": lru_lambda,
            "w_attn_o": w_attn_o, "w_lru_o": w_lru_o, "w_out": w_out,
            "peer_wq": peer_wq, "peer_subkeys": peer_subkeys, "peer_u": peer_u, "peer_v": peer_v}


def reference(x, c, w_ada, b_ada, norm_mix_g, norm_ffn_g, w_in, b_in, q_norm_g, k_norm_g,
              conv_w, conv_b, lru_wa, lru_ba, lru_wx, lru_bx, lru_lambda,
              w_attn_o, w_lru_o, w_out, peer_wq, peer_subkeys, peer_u, peer_v):
    B, S, D = x.shape
    for l in range(DEPTH):
        mod = jnp.einsum('bd,de->be', jax.nn.silu(c), w_ada[l]) + b_ada[l]
        shift1, scale1, gate1, shift2, scale2, gate2 = jnp.split(mod[:, None, :], N_MOD, axis=-1)

        h = _rmsnorm(x, norm_mix_g[l]) * (1.0 + scale1) + shift1
        proj = jnp.einsum('bsd,de->bse', h, w_in[l]) + b_in[l]
        q, k, v, f_logit, lru_x, lru_y, g_attn, g_lru = jnp.split(
            proj, [Q_END, K_END, V_END, F_END, LX_END, LY_END, GA_END], axis=-1)

        q = _rmsnorm(q.reshape(B, S, ATTN_HEADS, HEAD_DIM), q_norm_g[l])
        k = _rmsnorm(k.reshape(B, S, ATTN_HEADS, HEAD_DIM), k_norm_g[l])
        v = v.reshape(B, S, ATTN_HEADS, HEAD_DIM)
        log_f = jax.nn.log_sigmoid(f_logit.astype(jnp.float32))
        attn = _forgetting_attention(q, k, v, log_f)

        xc = _causal_conv(lru_x, conv_w[l], conv_b[l])
        lru = _rg_lru(xc, lru_wa[l], lru_ba[l], lru_wx[l], lru_bx[l], lru_lambda[l])
        lru = lru * jax.nn.gelu(lru_y, approximate=False)

        merged = (jax.nn.sigmoid(g_attn) * jnp.einsum('bse,ed->bsd', attn, w_attn_o[l])
                  + jax.nn.sigmoid(g_lru) * jnp.einsum('bse,ed->bsd', lru, w_lru_o[l]))
        x = x + gate1 * jnp.einsum('bsd,de->bse', merged, w_out[l])

        h2 = _rmsnorm(x, norm_ffn_g[l]) * (1.0 + scale2) + shift2
        x = x + gate2 * _peer(h2, peer_wq[l], peer_subkeys[l], peer_u[l], peer_v[l])
    return x
```

```python
import numpy as np
import ml_dtypes
import concourse.bass as bass
import concourse.mybir as mybir
from concourse.bass_utils import run_bass_kernel_spmd

F32, BF16 = mybir.dt.float32, mybir.dt.bfloat16
ALU = mybir.AluOpType
AF = mybir.ActivationFunctionType
NPBF = ml_dtypes.bfloat16

D = 4096
S = 8192
NCORE = 8
KC = D // 128
EPS = 1e-6


class Res:
    __slots__ = ("name", "w", "r", "dsem", "dcnt", "pend")

    def __init__(self, name):
        self.name = name
        self.w = {}
        self.r = {}
        self.dsem = None
        self.dcnt = 0
        self.pend = None


class KB:
    def __init__(self, nc):
        self.nc = nc
        self.eng = dict(pe=nc.tensor, act=nc.scalar, dve=nc.vector, pool=nc.gpsimd, sp=nc.sync)
        self.psem = {}
        self.pcnt = {}
        for k in ("pe", "act", "dve", "pool"):
            self.psem[k] = self.newsem("prog_" + k)
            self.pcnt[k] = 0
        self.waited = {}
        self.pend = {k: [] for k in self.psem}
        self.nres = 0

    def newsem(self, name):
        return self.nc.semaphore(name).__enter__()

    def res(self, name=None):
        self.nres += 1
        return Res(name or "r%d" % self.nres)

    def sb(self, name, shape, dt):
        return self.nc.sbuf_tensor("s_" + name, shape, dt).__enter__()

    def ps(self, name, shape, dt):
        return self.nc.psum_tensor("p_" + name, shape, dt).__enter__()

    def _wait(self, e, t):
        sem, val, key = t
        if key == "pe" and e == "pe":
            return
        if self.waited.get((e, key), 0) >= val:
            return
        self.waited[(e, key)] = val
        self.eng[e].wait_ge(sem, val)

    def _deps(self, e, reads, writes, skipkey=None):
        for r in reads:
            assert r.pend is None or r.pend == e, (r.name, r.pend, e)
            for k, t in r.w.items():
                if k != skipkey:
                    self._wait(e, t)
        for w in writes:
            assert w.pend is None or w.pend == e, (w.name, w.pend, e)
            for k, t in w.w.items():
                if k != skipkey:
                    self._wait(e, t)
            for k, t in w.r.items():
                self._wait(e, t)

    @staticmethod
    def _assign(t, reads, writes, part=False):
        key = t[2]
        for w in writes:
            if part:
                w.w[key] = t
            else:
                w.w = {key: t}
                w.r = {}
            w.pend = None
        for r in reads:
            r.r[key] = t
            r.pend = None

    def op(self, e, fn, reads=(), writes=(), sig=True):
        self._deps(e, reads, writes)
        ins = fn(self.eng[e])
        if sig:
            self.pcnt[e] += 1
            ins.then_inc(self.psem[e], 1)
            t = (self.psem[e], self.pcnt[e], e)
            for (rs, ws) in self.pend[e]:
                self._assign(t, rs, ws)
            self.pend[e] = []
            self._assign(t, reads, writes)
        else:
            self.pend[e].append((reads, writes))
            for x in list(reads) + list(writes):
                x.pend = e
        return ins

    def dma(self, q, out, in_, sres, reads=(), writes=(), part=False, **kw):
        key = "d_" + sres.name
        self._deps(q, reads, writes, skipkey=key if part else None)
        ins = self.eng[q].dma_start(out=out, in_=in_, **kw)
        if sres.dsem is None:
            sres.dsem = self.newsem("dma_" + sres.name)
        sres.dcnt += 16
        ins.then_inc(sres.dsem, 16)
        t = (sres.dsem, sres.dcnt, key)
        self._assign(t, reads, writes, part=part)
        return ins

    def finish(self, resources, e="sp"):
        for r in resources:
            for t in list(r.w.values()) + list(r.r.values()):
                self._wait(e, t)


def _new_nc():
    return bass.Bass("TRN2", target_bir_lowering=False)


L0_COLS = 6 * D // NCORE


def build_l0():
    nc = _new_nc()
    kb = KB(nc)
    cT = nc.dram_tensor("cT", [128, KC], F32, kind="ExternalInput").ap()
    w = nc.dram_tensor("w", [D, L0_COLS], F32, kind="ExternalInput").ap()
    b = nc.dram_tensor("b", [1, L0_COLS], F32, kind="ExternalInput").ap()
    mod = nc.dram_tensor("mod", [1, L0_COLS], F32, kind="ExternalOutput").ap()
    NB = L0_COLS // 512
    ct = kb.sb("ct", [128, KC], F32)
    sc = kb.sb("sc", [128, KC], F32)
    bt = kb.sb("bt", [1, L0_COLS], F32)
    ot = kb.sb("ot", [1, L0_COLS], F32)
    G = 4
    wt = [kb.sb("wt%d" % i, [128, G, L0_COLS], F32) for i in range(2)]
    r_ct, r_sc, r_bt, r_ot, r_mod = kb.res("ct"), kb.res("sc"), kb.res("bt"), kb.res("ot"), kb.res("mod")
    r_wt = [kb.res("wt0"), kb.res("wt1")]
    pss = [kb.ps("ps%d" % i, [128, 512], F32) for i in range(NB)]
    r_ps = [kb.res("ps%d" % i) for i in range(NB)]
    kb.dma("sp", ct[:], cT[:, :], r_ct, writes=[r_ct])
    kb.dma("sp", bt[:], b[:, :], r_bt, writes=[r_bt])
    kb.op("act", lambda e: e.activation(out=sc[:], in_=ct[:], func=AF.Silu), reads=[r_ct], writes=[r_sc])
    wv = w.rearrange("(k p) n -> p k n", p=128)
    ngrp = KC // G

    def load(g):
        kb.dma("sp", wt[g % 2][:], wv[:, g * G:(g + 1) * G, :], r_wt[g % 2], writes=[r_wt[g % 2]])

    load(0)
    for g in range(ngrp):
        if g + 1 < ngrp:
            load(g + 1)
        for kk in range(G):
            kc = g * G + kk
            for n in range(NB):
                last = (kc == KC - 1)
                kb.op("pe", lambda e, kc=kc, kk=kk, n=n, g=g: e.matmul(
                    pss[n][0:1, :], lhsT=sc[:, kc:kc + 1], rhs=wt[g % 2][:, kk, n * 512:(n + 1) * 512],
                    start=(kc == 0), stop=(kc == KC - 1)),
                    reads=[r_sc, r_wt[g % 2]], writes=[r_ps[n]],
                    sig=(last or (kk == G - 1 and n == NB - 1)))
    for n in range(NB):
        kb.op("dve", lambda e, n=n: e.tensor_tensor(out=ot[0:1, n * 512:(n + 1) * 512], in0=pss[n][0:1, :],
                                                    in1=bt[0:1, n * 512:(n + 1) * 512], op=ALU.add),
              reads=[r_ps[n], r_bt], writes=[r_ot])
    kb.dma("sp", mod[:, :], ot[:], r_ot, reads=[r_ot], writes=[r_mod], part=True)
    kb.finish([r_mod])
    return nc


TT = 256
NT = S // TT
NBLK = S // 128
WCOLS = 641
CF_SCALE1, CF_SHIFT1, CF_G = 0, 32, 64
CF_GQ, CF_GK = 96, 97
CF_U = 98
CF_UW = 12
CF_N = CF_U + 2 * CF_UW


def build_l1(ntiles=NT):
    nc = _new_nc()
    kb = KB(nc)
    x = nc.dram_tensor("x", [S, D], F32, kind="ExternalInput").ap()
    cf_d = nc.dram_tensor("cf", [128, CF_N], F32, kind="ExternalInput").ap()
    bv_d = nc.dram_tensor("bvrep", [128, 2 * 129], F32, kind="ExternalInput").ap()
    w_d = nc.dram_tensor("wslab", [2 * 128, KC * WCOLS], F32, kind="ExternalInput").ap()
    wax_d = nc.dram_tensor("wax", [128, 2 * 2 * 128], F32, kind="ExternalInput").ap()
    id_d = nc.dram_tensor("ident", [128, 128], BF16, kind="ExternalInput").ap()
    cm_d = nc.dram_tensor("cmask", [128, 128], F32, kind="ExternalInput").ap()
    idf_d = nc.dram_tensor("identf", [128, 128], F32, kind="ExternalInput").ap()
    tri_d = nc.dram_tensor("tri", [128, 128], F32, kind="ExternalInput").ap()
    on_d = nc.dram_tensor("ones", [128, 2 * 128], F32, kind="ExternalInput").ap()
    alT = nc.dram_tensor("alT", [4 * 128, S], BF16, kind="ExternalOutput").ap()
    r_out = kb.res("alT")

    cf = kb.sb("cf", [128, CF_N], F32); r_cf = kb.res("cf")
    bv = kb.sb("bv", [128, 2 * 129], F32); r_bv = kb.res("bv")
    ident = kb.sb("ident", [128, 128], BF16); r_id = kb.res("ident")
    maskneg = kb.sb("cmask", [128, 128], F32); r_cm = kb.res("cmask")
    cmask = maskneg
    identf = kb.sb("identf", [128, 128], F32); r_idf = kb.res("identf")
    dgn = kb.sb("dgn", [128, 128], F32); r_dgn = kb.res("dgn")
    Dm = kb.sb("Dm", [128, 128], F32); r_Dm = kb.res("Dm")
    sdg = kb.sb("sdg", [128, 128], F32); r_sdg = kb.res("sdg")
    osb = kb.sb("osb", [128, 129], F32); r_osb = kb.res("osb")
    tri = kb.sb("tri", [128, 128], F32); r_tri = kb.res("tri")
    ones = kb.sb("ones", [128, 256], F32); r_on = kb.res("ones")
    wax = kb.sb("wax", [128, 512], BF16); r_wax = kb.res("wax")
    kb.dma("sp", cf[:], cf_d[:, :], r_cf, writes=[r_cf])
    kb.dma("sp", bv[:], bv_d[:, :], r_bv, writes=[r_bv])
    kb.dma("sp", ident[:], id_d[:, :], r_id, writes=[r_id])
    kb.dma("sp", cmask[:], cm_d[:, :], r_cm, writes=[r_cm])
    kb.dma("sp", identf[:], idf_d[:, :], r_idf, writes=[r_idf])
    kb.dma("sp", tri[:], tri_d[:, :], r_tri, writes=[r_tri])
    kb.dma("sp", ones[:], on_d[:, :], r_on, writes=[r_on])
    kb.dma("pool", wax[:], wax_d[:, :], r_wax, writes=[r_wax])
    A1 = kb.sb("A1", [128, KC], F32); r_A1 = kb.res("A1")
    kb.op("dve", lambda e: e.scalar_tensor_tensor(out=A1[:], in0=cf[:, CF_SCALE1:CF_SCALE1 + KC], scalar=1.0,
                                                 in1=cf[:, CF_G:CF_G + KC], op0=ALU.add, op1=ALU.mult),
          reads=[r_cf], writes=[r_A1])
    c8 = kb.sb("c8", [128, 8], F32); r_c8 = kb.res("c8")
    for u in range(2):
        lam = cf[:, CF_U + u * CF_UW + 11:CF_U + u * CF_UW + 12]
        kb.op("act", lambda e, u=u, lam=lam: e.activation(out=c8[:, 4 + u:5 + u], in_=lam, func=AF.Exp, scale=-1.0),
              reads=[r_cf], writes=[r_c8])
        kb.op("act", lambda e, u=u: e.activation(out=c8[:, 6 + u:7 + u], in_=c8[:, 4 + u:5 + u], func=AF.Ln, bias=1.0, scale=1.0),
              reads=[r_c8], writes=[r_c8])
        kb.op("dve", lambda e, u=u: e.tensor_scalar(out=c8[:, u:u + 1], in0=c8[:, 6 + u:7 + u], scalar1=-8.0, scalar2=None, op0=ALU.mult),
              reads=[r_c8], writes=[r_c8])
        kb.op("dve", lambda e, u=u: e.tensor_scalar(out=c8[:, 2 + u:3 + u], in0=c8[:, 6 + u:7 + u], scalar1=-16.0, scalar2=None, op0=ALU.mult),
              reads=[r_c8], writes=[r_c8])

    wsl = kb.sb("wsl", [128, KC, WCOLS], BF16); r_wsl = kb.res("wsl")
    xb = [kb.sb("xb%d" % i, [128, 2, D], BF16) for i in range(2)]
    r_xb = [kb.res("xb0"), kb.res("xb1")]
    junk = kb.sb("junk", [128, D], BF16); r_junk = kb.res("junk")
    hT = kb.sb("hT", [128, KC, TT], BF16); r_hT = [kb.res("hT0"), kb.res("hT1")]
    KT = kb.sb("KT", [128, S], BF16); r_KT = kb.res("KT")
    V = kb.sb("V", [128, NBLK, 129], BF16); r_V = kb.res("V")
    QT = kb.sb("QT", [128, TT], BF16); r_QT = kb.res("QT")
    NFT = kb.sb("NFT", [128, NBLK], F32); r_NFT = kb.res("NFT")
    NR = kb.sb("NR", [128, NBLK + 1], F32); r_NR = kb.res("NR")
    st = kb.sb("st", [128, 8], F32); r_st = kb.res("st")
    fl = kb.sb("fl", [128, 12], F32); r_fl = kb.res("fl")
    qf = kb.sb("qf", [128, TT], F32); r_qf = kb.res("qf")
    qs = kb.sb("qs", [128, TT], F32); r_qs = kb.res("qs")
    qr = qs; r_qr = r_qs
    Bj = [kb.sb("Bj%d" % i, [128, NBLK], F32) for i in range(2)]; r_Bj = [kb.res("Bj0"), kb.res("Bj1")]
    NPT = 4
    PT = [kb.sb("PT%d" % i, [128, 128], BF16) for i in range(NPT)]; r_PT = [kb.res("PT%d" % i) for i in range(NPT)]
    rec = kb.sb("rec", [128, 2], F32); r_rec = kb.res("rec")
    ao = kb.sb("ao", [128, 128], BF16); r_ao = kb.res("ao")
    aoT = [kb.sb("aoT%d" % i, [128, TT], BF16) for i in range(2)]; r_aoT = [kb.res("aoT0"), kb.res("aoT1")]
    lx = [kb.sb("lx%d" % i, [128, TT + 3], F32) for i in range(2)]; r_lx = [kb.res("lx0"), kb.res("lx1")]
    xc = kb.sb("xc", [128, TT], F32); r_xc = kb.res("xc")
    xcb = kb.sb("xcb", [128, TT], BF16); r_xcb = kb.res("xcb")
    rg = kb.sb("rg", [128, TT], F32); r_rg = kb.res("rg")
    ig = kb.sb("ig", [128, TT], F32); r_ig = kb.res("ig")
    aa = kb.sb("aa", [128, TT], F32); r_aa = kb.res("aa")
    a2 = kb.sb("a2", [128, TT], F32); r_a2 = kb.res("a2")
    hh = [kb.sb("hh%d" % i, [128, TT], F32) for i in range(2)]; r_hh = [kb.res("hh0"), kb.res("hh1")]
    gy = kb.sb("gy", [128, TT], F32); r_gy = kb.res("gy")
    lo = [kb.sb("lo%d" % i, [128, TT], BF16) for i in range(2)]; r_lo = [kb.res("lo0"), kb.res("lo1")]
    zero1 = kb.sb("zero1", [128, 1], F32); r_z = kb.res("zero1")
    kb.op("dve", lambda e: e.memset(zero1[:], 0.0), writes=[r_z])

    ps_tr = [kb.ps("ps_tr%d" % i, [128, 1024], BF16) for i in range(2)]; r_ptr = [kb.res("ptr0"), kb.res("ptr1")]
    ps_mm = [kb.ps("ps_mm%d" % i, [128, 512], F32) for i in range(2)]; r_pmm = [kb.res("pmm0"), kb.res("pmm1")]
    ps_s = [kb.ps("ps_s%d" % i, [128, 512], F32) for i in range(2)]
    r_pss = [kb.res("pss%d" % i) for i in range(2)]
    ps_o = [kb.ps("ps_o%d" % i, [128, 512], F32) for i in range(2)]; r_po = [kb.res("po0"), kb.res("po1")]

    xv = x.rearrange("(s b p) d -> s p b d", b=2, p=128)
    SCALE = float(128 ** -0.5)
    cnt = dict(mm=0, tr=0, ev=0, s=0, pt=0, o=0, bj=0, aot=0, lo=0)

    def load_x(gs, sidx):
        kb.dma("pool", xb[gs % 2][:], xv[sidx], r_xb[gs % 2], writes=[r_xb[gs % 2]])

    for u in range(2):
        ub = CF_U + u * CF_UW
        col = lambda k: cf[:, ub + k:ub + k + 1]
        for part in range(4):
            kb.dma("pool", wsl[:, part * 8:(part + 1) * 8, :],
                   w_d[u * 128:(u + 1) * 128, part * 8 * WCOLS:(part + 1) * 8 * WCOLS].rearrange("p (k c) -> p k c", c=WCOLS),
                   r_wsl, writes=[r_wsl], part=(part > 0))
        kb.op("dve", lambda e: e.memset(NR[:, 0:1], 0.0), writes=[r_NR])
        kb.op("dve", lambda e: e.memset(V[:, :, 128:129], 1.0), writes=[r_V])
        kb.op("dve", lambda e: e.memset(lx[0][:, 0:3], 0.0), writes=[r_lx[0]])
        load_x(u * ntiles, 0)
        hprev = None
        for T in range(ntiles):
            for hf in range(1):
                sidx = T
                gs = u * ntiles + sidx
                if sidx + 1 < ntiles:
                    load_x(gs + 1, sidx + 1)
                kb.op("dve", lambda e: e.memset(st[:, 0:2], 0.0), writes=[r_st])
                xt = xb[gs % 2]; rx = r_xb[gs % 2]
                for b in range(2):
                    kb.op("act", lambda e, b=b, xt=xt: e.activation(out=junk[:], in_=xt[:, b, :], func=AF.Square,
                                                                  scale=1.0 / 64.0, accum_out=st[:, b:b + 1]),
                          reads=[rx], writes=[r_junk, r_st])
                kb.op("act", lambda e: e.activation(out=st[:, 2:4], in_=st[:, 0:2], func=AF.Sqrt, bias=EPS, scale=1.0),
                      reads=[r_st], writes=[r_st])
                kb.op("dve", lambda e: e.reciprocal(out=st[:, 4:6], in_=st[:, 2:4]), reads=[r_st], writes=[r_st])
                for b in range(2):
                    kb.op("dve", lambda e, b=b, xt=xt: e.tensor_scalar(out=xt[:, b, :], in0=xt[:, b, :], scalar1=st[:, 4 + b:5 + b],
                                                                     scalar2=None, op0=ALU.mult),
                          reads=[r_st, rx], writes=[rx])
                for g in range(KC // 4):
                    pt = cnt["tr"] % 2; cnt["tr"] += 1
                    for k4 in range(4):
                        kc = g * 4 + k4
                        for b in range(2):
                            kb.op("pe", lambda e, kc=kc, k4=k4, b=b, xt=xt, pt=pt: e.transpose(
                                out=ps_tr[pt][:, k4 * 256 + b * 128:k4 * 256 + (b + 1) * 128],
                                in_=xt[:, b, kc * 128:(kc + 1) * 128], identity=ident[:]),
                                reads=[rx, r_id], writes=[r_ptr[pt]], sig=(k4 == 3 and b == 1))
                    for k4 in range(4):
                        kc = g * 4 + k4
                        dst = hT[:, kc, :]
                        src = ps_tr[pt][:, k4 * 256:(k4 + 1) * 256]
                        if cnt["ev"] % 2 == 0:
                            kb.op("act", lambda e, dst=dst, src=src, kc=kc: e.activation(
                                out=dst, in_=src, func=AF.Identity, scale=A1[:, kc:kc + 1], bias=cf[:, CF_SHIFT1 + kc:CF_SHIFT1 + kc + 1]),
                                reads=[r_ptr[pt], r_A1, r_cf], writes=[r_hT[hf]])
                        else:
                            kb.op("dve", lambda e, dst=dst, src=src, kc=kc: e.tensor_scalar(
                                out=dst, in0=src, scalar1=A1[:, kc:kc + 1], scalar2=cf[:, CF_SHIFT1 + kc:CF_SHIFT1 + kc + 1],
                                op0=ALU.mult, op1=ALU.add),
                                reads=[r_ptr[pt], r_A1, r_cf], writes=[r_hT[hf]])
                        cnt["ev"] += 1

            def proj(si):
                pm = cnt["mm"] % 2; cnt["mm"] += 1
                for kc in range(KC):
                    kb.op("pe", lambda e, kc=kc, pm=pm: e.matmul(ps_mm[pm][:, 0:TT], lhsT=wsl[:, kc, si * 128:(si + 1) * 128], rhs=hT[:, kc, :],
                                                                 start=(kc == 0), stop=(kc == KC - 1)),
                          reads=[r_wsl, r_hT[0]], writes=[r_pmm[pm]], sig=(kc == KC - 1))
                return pm

            def qknorm(pm, bcol, gcol, dst, r_dst):
                kb.op("act", lambda e: e.activation(out=qf[:], in_=ps_mm[pm][:, 0:TT], func=AF.Identity, bias=bcol, scale=1.0),
                      reads=[r_pmm[pm], r_cf], writes=[r_qf])
                kb.op("act", lambda e: e.activation(out=qs[:], in_=qf[:], func=AF.Square), reads=[r_qf], writes=[r_qs])
                px = cnt["mm"] % 2; cnt["mm"] += 1
                kb.op("pe", lambda e: e.matmul(ps_mm[px][:, 0:TT], lhsT=ones[:, 0:128], rhs=qs[:], start=True, stop=True),
                      reads=[r_on, r_qs], writes=[r_pmm[px]])
                kb.op("act", lambda e: e.activation(out=qr[:], in_=ps_mm[px][:, 0:TT], func=AF.Sqrt, bias=EPS, scale=1.0),
                      reads=[r_pmm[px]], writes=[r_qr])
                kb.op("dve", lambda e: e.reciprocal(out=qr[:], in_=qr[:]), reads=[r_qr], writes=[r_qr])
                kb.op("dve", lambda e: e.scalar_tensor_tensor(out=dst, in0=qf[:], scalar=gcol, in1=qr[:], op0=ALU.mult, op1=ALU.mult),
                      reads=[r_qf, r_qr, r_cf], writes=[r_dst])

            pm = proj(0)
            qknorm(pm, col(0), cf[:, CF_GQ:CF_GQ + 1], QT[:], r_QT)
            pm = proj(1)
            qknorm(pm, col(1), cf[:, CF_GK:CF_GK + 1], KT[:, T * TT:(T + 1) * TT], r_KT)
            for b in range(2):
                blk = T * 2 + b
                pm = cnt["mm"] % 2; cnt["mm"] += 1
                for kc in range(KC):
                    kb.op("pe", lambda e, kc=kc, pm=pm, b=b: e.matmul(ps_mm[pm][:, 0:129], lhsT=hT[:, kc, b * 128:(b + 1) * 128],
                                                                      rhs=wsl[:, kc, 512:641], start=(kc == 0), stop=(kc == KC - 1)),
                          reads=[r_wsl, r_hT[0]], writes=[r_pmm[pm]], sig=(kc == KC - 1))
                kb.op("dve", lambda e, pm=pm, blk=blk: e.tensor_tensor(out=V[:, blk, 0:128], in0=ps_mm[pm][:, 0:128],
                                                                       in1=bv[:, u * 129:u * 129 + 128], op=ALU.add),
                      reads=[r_pmm[pm], r_bv], writes=[r_V])
                kb.op("dve", lambda e, pm=pm, b=b: e.tensor_tensor(out=fl[:, b:b + 1], in0=ps_mm[pm][:, 128:129],
                                                                   in1=bv[:, u * 129 + 128:u * 129 + 129], op=ALU.add),
                      reads=[r_pmm[pm], r_bv], writes=[r_fl])
            kb.op("act", lambda e: e.activation(out=fl[:, 4:6], in_=fl[:, 0:2], func=AF.Exp, scale=-1.0), reads=[r_fl], writes=[r_fl])
            kb.op("act", lambda e: e.activation(out=fl[:, 8:10], in_=fl[:, 4:6], func=AF.Ln, bias=1.0, scale=1.0), reads=[r_fl], writes=[r_fl])
            pxc = cnt["mm"] % 2; cnt["mm"] += 1
            ps_x = ps_mm[pxc]; r_px = r_pmm[pxc]
            kb.op("pe", lambda e: e.matmul(ps_x[:, 0:2], lhsT=tri[:], rhs=fl[:, 8:10], start=True, stop=True),
                  reads=[r_tri, r_fl], writes=[r_px], sig=False)
            kb.op("pe", lambda e: e.matmul(ps_x[:, 4:6], lhsT=ones[:, 128:256], rhs=fl[:, 8:10], start=True, stop=True),
                  reads=[r_on, r_fl], writes=[r_px])
            for b in range(2):
                blk = T * 2 + b
                kb.op("dve", lambda e, b=b, blk=blk: e.tensor_tensor(out=NR[:, blk + 1:blk + 2], in0=NR[:, blk:blk + 1],
                                                                     in1=ps_x[:, 4 + b:5 + b], op=ALU.add),
                      reads=[r_px, r_NR], writes=[r_NR])
            kb.op("dve", lambda e: e.tensor_tensor(out=NFT[:, T * 2:T * 2 + 2], in0=ps_x[:, 0:2], in1=NR[:, T * 2:T * 2 + 2], op=ALU.add),
                  reads=[r_px, r_NR], writes=[r_NFT])

            lxi = T % 2
            pm = proj(2)
            kb.op("act", lambda e, pm=pm: e.activation(out=lx[lxi][:, 3:TT + 3], in_=ps_mm[pm][:, 0:TT], func=AF.Identity, bias=col(2), scale=1.0),
                  reads=[r_pmm[pm], r_cf], writes=[r_lx[lxi]])
            pm = proj(3)
            kb.op("act", lambda e, pm=pm: e.activation(out=gy[:], in_=ps_mm[pm][:, 0:TT], func=AF.Gelu, bias=col(3), scale=1.0),
                  reads=[r_pmm[pm], r_cf], writes=[r_gy])

            groups = []
            for jq in range(2):
                j = T * 2 + jq
                ks = list(range(j))
                for g0 in range(0, len(ks), 4):
                    groups.append((jq, ks[g0:g0 + 4]))
            bj_of = {}
            sb_of = {}

            def emit_s(n):
                jq, ks = groups[n]
                j = T * 2 + jq
                if ks[0] == 0:
                    bi = cnt["bj"] % 2; cnt["bj"] += 1
                    bj_of[jq] = bi
                    kb.op("dve", lambda e, bi=bi, j=j: e.tensor_scalar(out=Bj[bi][:, 0:j], in0=NFT[:, 0:j], scalar1=NR[:, j:j + 1],
                                                                      scalar2=None, op0=ALU.subtract),
                          reads=[r_NFT, r_NR], writes=[r_Bj[bi]])
                sbk = cnt["s"] % 2; cnt["s"] += 1
                sb_of[n] = sbk
                for kk, i in enumerate(ks):
                    kb.op("pe", lambda e, sbk=sbk, kk=kk, i=i, jq=jq: e.matmul(ps_s[sbk][:, kk * 128:(kk + 1) * 128],
                                                                              lhsT=KT[:, i * 128:(i + 1) * 128], rhs=QT[:, jq * 128:(jq + 1) * 128],
                                                                              start=True, stop=True),
                          reads=[r_KT, r_QT], writes=[r_pss[sbk]], sig=(kk == len(ks) - 1))

            def diag_block(jq):
                j = T * 2 + jq
                oi = jq % 2
                kb.op("dve", lambda e: e.tensor_scalar(out=dgn[:], in0=identf[:], scalar1=NFT[:, j:j + 1], scalar2=None, op0=ALU.mult),
                      reads=[r_idf, r_NFT], writes=[r_dgn])
                px = cnt["mm"] % 2; cnt["mm"] += 1
                kb.op("pe", lambda e: e.matmul(ps_mm[px][:, 0:128], lhsT=ones[:, 128:256], rhs=dgn[:], start=True, stop=True),
                      reads=[r_on, r_dgn], writes=[r_pmm[px]])
                kb.op("dve", lambda e: e.tensor_scalar(out=Dm[:], in0=ps_mm[px][:, 0:128], scalar1=NFT[:, j:j + 1], scalar2=-1.0,
                                                      op0=ALU.subtract, op1=ALU.mult), reads=[r_pmm[px], r_NFT], writes=[r_Dm])
                kb.op("dve", lambda e: e.tensor_tensor(out=Dm[:], in0=Dm[:], in1=maskneg[:], op=ALU.add), reads=[r_Dm, r_cm], writes=[r_Dm])
                sbk = cnt["s"] % 2; cnt["s"] += 1
                kb.op("pe", lambda e: e.matmul(ps_s[sbk][:, 0:128], lhsT=KT[:, j * 128:(j + 1) * 128], rhs=QT[:, jq * 128:(jq + 1) * 128],
                                               start=True, stop=True), reads=[r_KT, r_QT], writes=[r_pss[sbk]])
                kb.op("dve", lambda e: e.scalar_tensor_tensor(out=sdg[:], in0=ps_s[sbk][:, 0:128], scalar=SCALE, in1=Dm[:], op0=ALU.mult, op1=ALU.add),
                      reads=[r_pss[sbk], r_Dm], writes=[r_sdg])
                pi = cnt["pt"] % NPT; cnt["pt"] += 1
                kb.op("act", lambda e: e.activation(out=PT[pi][:], in_=sdg[:], func=AF.Exp), reads=[r_sdg], writes=[r_PT[pi]])
                kb.op("pe", lambda e: e.matmul(ps_o[oi][:, 256:385], lhsT=PT[pi][:], rhs=V[:, j, :], start=True, stop=True),
                      reads=[r_PT[pi], r_V], writes=[r_po[oi]])
                if j > 0:
                    kb.op("act", lambda e: e.activation(out=rec[:, 1:2], in_=NFT[:, j:j + 1], func=AF.Exp, scale=-1.0, bias=NR[:, j:j + 1]),
                          reads=[r_NFT, r_NR], writes=[r_rec])
                    kb.op("act", lambda e: e.activation(out=osb[:], in_=ps_o[oi][:, 256:385], func=AF.Identity), reads=[r_po[oi]], writes=[r_osb])
                    kb.op("dve", lambda e: e.scalar_tensor_tensor(out=osb[:], in0=ps_o[oi][:, 0:129], scalar=rec[:, 1:2], in1=osb[:],
                                                                 op0=ALU.mult, op1=ALU.add), reads=[r_po[oi], r_rec, r_osb], writes=[r_osb])
                else:
                    kb.op("act", lambda e: e.activation(out=osb[:], in_=ps_o[oi][:, 256:385], func=AF.Identity), reads=[r_po[oi]], writes=[r_osb])
                ai = T % 2
                kb.op("dve", lambda e: e.reciprocal(out=rec[:, 0:1], in_=osb[:, 128:129]), reads=[r_osb], writes=[r_rec])
                kb.op("dve", lambda e: e.tensor_scalar(out=ao[:], in0=osb[:, 0:128], scalar1=rec[:, 0:1], scalar2=None, op0=ALU.mult),
                      reads=[r_osb, r_rec], writes=[r_ao])
                pt = cnt["tr"] % 2; cnt["tr"] += 1
                kb.op("pe", lambda e, pt=pt: e.transpose(out=ps_tr[pt][:, 0:128], in_=ao[:], identity=ident[:]),
                      reads=[r_ao, r_id], writes=[r_ptr[pt]])
                kb.op("dve", lambda e, pt=pt: e.tensor_copy(out=aoT[ai][:, jq * 128:(jq + 1) * 128], in_=ps_tr[pt][:, 0:128]),
                      reads=[r_ptr[pt]], writes=[r_aoT[ai]])

            if groups:
                emit_s(0)
            done_diag = set()
            for n, (jq, ks) in enumerate(groups):
                j = T * 2 + jq
                if n + 1 < len(groups):
                    emit_s(n + 1)
                sbk = sb_of.pop(n)
                bi = bj_of[jq]
                oi = jq % 2
                for kk, i in enumerate(ks):
                    pi = cnt["pt"] % NPT; cnt["pt"] += 1
                    kb.op("act", lambda e, sbk=sbk, kk=kk, pi=pi, bi=bi, i=i: e.activation(
                        out=PT[pi][:], in_=ps_s[sbk][:, kk * 128:(kk + 1) * 128], func=AF.Exp,
                        scale=SCALE, bias=Bj[bi][:, i:i + 1]),
                        reads=[r_pss[sbk], r_Bj[bi]], writes=[r_PT[pi]])
                    kb.op("pe", lambda e, pi=pi, i=i, oi=oi, j=j: e.matmul(ps_o[oi][:, 0:129], lhsT=PT[pi][:], rhs=V[:, i, :],
                                                                            start=(i == 0), stop=(i == j - 1)),
                          reads=[r_PT[pi], r_V], writes=[r_po[oi]])
                    if i == j - 1:
                        diag_block(jq)
                        done_diag.add(jq)
            for jq in range(2):
                if jq not in done_diag:
                    diag_block(jq)
            ai = T % 2
            kb.dma("sp", alT[u * 128:(u + 1) * 128, T * TT:(T + 1) * TT], aoT[ai][:], r_aoT[ai], reads=[r_aoT[ai]], writes=[r_out], part=True)

            lxt = lx[lxi]
            kb.op("dve", lambda e: e.tensor_scalar(out=xc[:], in0=lxt[:, 0:TT], scalar1=col(4), scalar2=col(8), op0=ALU.mult, op1=ALU.add),
                  reads=[r_lx[lxi], r_cf], writes=[r_xc])
            for k in range(1, 4):
                kb.op("dve", lambda e, k=k: e.scalar_tensor_tensor(out=xc[:], in0=lxt[:, k:k + TT], scalar=col(4 + k), in1=xc[:],
                                                                  op0=ALU.mult, op1=ALU.add),
                      reads=[r_lx[lxi], r_cf, r_xc], writes=[r_xc])
            kb.op("pool", lambda e: e.tensor_copy(out=lx[1 - lxi][:, 0:3], in_=lxt[:, TT:TT + 3]), reads=[r_lx[lxi]], writes=[r_lx[1 - lxi]])
            kb.op("act", lambda e: e.activation(out=xcb[:], in_=xc[:], func=AF.Identity), reads=[r_xc], writes=[r_xcb])
            pmr = cnt["mm"] % 2; cnt["mm"] += 1
            kb.op("pe", lambda e: e.matmul(ps_mm[pmr][:, 0:TT], lhsT=wax[:, (u * 2) * 128:(u * 2 + 1) * 128], rhs=xcb[:], start=True, stop=True),
                  reads=[r_wax, r_xcb], writes=[r_pmm[pmr]])
            kb.op("act", lambda e: e.activation(out=rg[:], in_=ps_mm[pmr][:, 0:TT], func=AF.Sigmoid, bias=col(9), scale=1.0),
                  reads=[r_pmm[pmr], r_cf], writes=[r_rg])
            pmi = cnt["mm"] % 2; cnt["mm"] += 1
            kb.op("pe", lambda e: e.matmul(ps_mm[pmi][:, 0:TT], lhsT=wax[:, (u * 2 + 1) * 128:(u * 2 + 2) * 128], rhs=xcb[:], start=True, stop=True),
                  reads=[r_wax, r_xcb], writes=[r_pmm[pmi]])
            kb.op("act", lambda e: e.activation(out=ig[:], in_=ps_mm[pmi][:, 0:TT], func=AF.Sigmoid, bias=col(10), scale=1.0),
                  reads=[r_pmm[pmi], r_cf], writes=[r_ig])
            kb.op("act", lambda e: e.activation(out=aa[:], in_=rg[:], func=AF.Exp, scale=c8[:, u:u + 1]), reads=[r_rg, r_c8], writes=[r_aa])
            kb.op("act", lambda e: e.activation(out=a2[:], in_=rg[:], func=AF.Exp, scale=c8[:, 2 + u:3 + u]), reads=[r_rg, r_c8], writes=[r_a2])
            kb.op("dve", lambda e: e.tensor_scalar(out=a2[:], in0=a2[:], scalar1=-1.0, scalar2=1.0, op0=ALU.mult, op1=ALU.add),
                  reads=[r_a2], writes=[r_a2])
            kb.op("act", lambda e: e.activation(out=a2[:], in_=a2[:], func=AF.Sqrt), reads=[r_a2], writes=[r_a2])
            kb.op("dve", lambda e: e.tensor_tensor(out=ig[:], in0=ig[:], in1=xc[:], op=ALU.mult), reads=[r_ig, r_xc], writes=[r_ig])
            kb.op("dve", lambda e: e.tensor_tensor(out=ig[:], in0=ig[:], in1=a2[:], op=ALU.mult), reads=[r_ig, r_a2], writes=[r_ig])
            hi = T % 2
            init = zero1[:, 0:1] if T == 0 else hh[1 - hi][:, TT - 1:TT]
            kb.op("dve", lambda e, init=init: e.tensor_tensor_scan(out=hh[hi][:], data0=aa[:], data1=ig[:], initial=init, op0=ALU.mult, op1=ALU.add),
                  reads=[r_aa, r_ig, r_z, r_hh[1 - hi]], writes=[r_hh[hi]])
            li = cnt["lo"] % 2; cnt["lo"] += 1
            kb.op("dve", lambda e, li=li: e.tensor_tensor(out=lo[li][:], in0=hh[hi][:], in1=gy[:], op=ALU.mult),
                  reads=[r_hh[hi], r_gy], writes=[r_lo[li]])
            kb.dma("sp", alT[(2 + u) * 128:(3 + u) * 128, T * TT:(T + 1) * TT], lo[li][:], r_lo[li], reads=[r_lo[li]], writes=[r_out], part=True)
    kb.finish([r_out])
    return nc


Q_END, K_END, V_END = 2048, 4096, 6144
F_END = V_END + 16
LX_END = F_END + 2048
LY_END = LX_END + 2048
GA_END = LY_END + D


def _colT(v):
    return np.ascontiguousarray(np.asarray(v, np.float32).reshape(-1, 128).T)


def prep_l0(inp):
    cT = _colT(inp["c"][0])
    w = inp["w_ada"][0]
    b = inp["b_ada"][0]
    maps = []
    for r in range(NCORE):
        sl = slice(r * L0_COLS, (r + 1) * L0_COLS)
        maps.append({"cT": cT, "w": np.ascontiguousarray(w[:, sl]), "b": np.ascontiguousarray(b[None, sl])})
    return maps


def l1_consts():
    tri = np.triu(np.ones((128, 128), np.float32))
    return {
        "ident": np.eye(128, dtype=np.float32).astype(NPBF),
        "cmask": ((1.0 - tri) * np.float32(-1.0e9)).astype(np.float32),
        "identf": np.eye(128, dtype=np.float32),
        "tri": tri,
        "ones": np.concatenate([np.full((128, 128), 1.0 / 128, np.float32), np.ones((128, 128), np.float32)], axis=1),
    }


def prep_l1(inp, mod):
    x = np.ascontiguousarray(inp["x"][0])
    w_in = inp["w_in"][0]
    b_in = inp["b_in"][0]
    consts = l1_consts()
    maps = []
    for r in range(NCORE):
        cf = np.zeros((128, CF_N), np.float32)
        cf[:, CF_SCALE1:CF_SCALE1 + KC] = _colT(mod[D:2 * D])
        cf[:, CF_SHIFT1:CF_SHIFT1 + KC] = _colT(mod[0:D])
        cf[:, CF_G:CF_G + KC] = _colT(inp["norm_mix_g"][0])
        cf[:, CF_GQ] = inp["q_norm_g"][0]
        cf[:, CF_GK] = inp["k_norm_g"][0]
        bvrep = np.zeros((128, 258), np.float32)
        wslab = np.zeros((256, KC * WCOLS), np.float32)
        wax = np.zeros((128, 512), np.float32)
        for u in range(2):
            h = 2 * r + u
            hs = slice(h * 128, (h + 1) * 128)
            base = CF_U + u * CF_UW
            cols = [b_in[hs], b_in[Q_END + h * 128:Q_END + (h + 1) * 128],
                    b_in[F_END + h * 128:F_END + (h + 1) * 128], b_in[LX_END + h * 128:LX_END + (h + 1) * 128],
                    inp["conv_w"][0][0, hs], inp["conv_w"][0][1, hs], inp["conv_w"][0][2, hs], inp["conv_w"][0][3, hs],
                    inp["conv_b"][0][hs], inp["lru_ba"][0][hs], inp["lru_bx"][0][hs], inp["lru_lambda"][0][hs]]
            for k, cvec in enumerate(cols):
                cf[:, base + k] = cvec
            bvrep[:, u * 129:u * 129 + 128] = b_in[None, K_END + h * 128:K_END + (h + 1) * 128]
            bvrep[:, u * 129 + 128] = b_in[V_END + h]
            wcat = np.concatenate([w_in[:, hs], w_in[:, Q_END + h * 128:Q_END + (h + 1) * 128],
                                   w_in[:, F_END + h * 128:F_END + (h + 1) * 128], w_in[:, LX_END + h * 128:LX_END + (h + 1) * 128],
                                   w_in[:, K_END + h * 128:K_END + (h + 1) * 128], w_in[:, V_END + h:V_END + h + 1]], axis=1)
            wslab[u * 128:(u + 1) * 128] = wcat.reshape(KC, 128, WCOLS).transpose(1, 0, 2).reshape(128, KC * WCOLS)
            wax[:, u * 256:u * 256 + 128] = inp["lru_wa"][0][h]
            wax[:, u * 256 + 128:u * 256 + 256] = inp["lru_wx"][0][h]
        m = {"x": x, "cf": cf, "bvrep": bvrep, "wslab": wslab, "wax": wax}
        m.update(consts)
        maps.append(m)
    return maps


TOK2 = S // NCORE
C2_SCALE1, C2_SHIFT1, C2_G1, C2_SCALE2, C2_SHIFT2, C2_G2, C2_BGA, C2_BGL = 0, 32, 64, 96, 128, 160, 192, 224
C2_N = 256
NEG = -1.0e30


def build_l2(nhalf=2, nchunk=16, fz=None):
    from contextlib import ExitStack
    nc = _new_nc()
    kb = KB(nc)
    uid = [0]

    def sbt(stack, name, shape, dt):
        uid[0] += 1
        return stack.enter_context(nc.sbuf_tensor("s_%s_%d" % (name, uid[0]), shape, dt))

    def pst(stack, name, shape, dt):
        uid[0] += 1
        return stack.enter_context(nc.psum_tensor("p_%s_%d" % (name, uid[0]), shape, dt))

    def barrier():
        for k in kb.psem:
            assert not kb.pend[k]
        tickets = [(kb.psem[k], kb.pcnt[k], k) for k in kb.psem if kb.pcnt[k] > 0]
        tickets += [(r.dsem, r.dcnt, "d_" + r.name) for r in allres if r.dsem is not None]
        for e in ("pe", "act", "dve", "pool", "sp"):
            for t in tickets:
                kb._wait(e, t)

    allres = []

    def res(name):
        r = kb.res(name + "_%d" % len(allres))
        allres.append(r)
        return r

    r_al = res("al_d")
    r_mod = res("mod_s")
    cf_d = nc.dram_tensor("cf", [128, C2_N], F32, kind="ExternalInput").ap()
    if fz is None:
        x_d = nc.dram_tensor("x", [TOK2, D], F32, kind="ExternalInput").ap()
        g1_d = nc.dram_tensor("g1rep", [128, D], F32, kind="ExternalInput").ap()
        g2_d = nc.dram_tensor("g2rep", [128, D], F32, kind="ExternalInput").ap()
        al_d = nc.dram_tensor("alT", [KC * 128, TOK2], BF16, kind="ExternalInput").ap()
    else:
        NTL = fz["ntiles"]; NOWN = 4; NU = fz["nunits"]
        SP = NTL * 256
        xp_d = nc.dram_tensor("xpad", [SP, D], F32, kind="ExternalInput").ap()
        x_d = xp_d[SP - TOK2:SP, :]
        cT_d = nc.dram_tensor("cT", [128, KC], F32, kind="ExternalInput").ap()
        wada_d = nc.dram_tensor("w_ada", [D, 6 * D], F32, kind="ExternalInput").ap()
        bada_d = nc.dram_tensor("b_ada", [1, 6 * D], F32, kind="ExternalInput").ap()
        mod_s = nc.dram_tensor("mod_s", [1, 6 * D], F32, kind=("ExternalOutput" if fz.get("dbg") else "Internal")).ap()
        g1_d = mod_s[0:1, 2 * D:3 * D].broadcast_to([128, D])
        g2_d = mod_s[0:1, 5 * D:6 * D].broadcast_to([128, D])
        al_d = nc.dram_tensor("alT", [KC * 128, TOK2], BF16, kind=("ExternalOutput" if fz.get("dbg") else "Internal")).ap()
        hT_s = nc.dram_tensor("hT_s", [NTL * 128, KC * 256], BF16, kind="Internal").ap()
        r_hTs = [res("hTs%d" % t) for t in range(NTL)]
        cfm_d = nc.dram_tensor("cfm", [128, 2 + 16 * CF_UW], F32, kind="ExternalInput").ap()
        bvm_d = nc.dram_tensor("bvrep", [128, 16 * 129], F32, kind="ExternalInput").ap()
        wsl_d = nc.dram_tensor("wslab", [16 * 128, KC * WCOLS], F32, kind="ExternalInput").ap()
        wax_d = nc.dram_tensor("wax", [128, 16 * 256], F32, kind="ExternalInput").ap()
        tm_d = nc.dram_tensor("tmask", [128, NTL], F32, kind="ExternalInput").ap()
        cm_d = nc.dram_tensor("cmask", [128, 128], F32, kind="ExternalInput").ap()
        idf_d = nc.dram_tensor("identf", [128, 128], F32, kind="ExternalInput").ap()
        tri_d = nc.dram_tensor("tri", [128, 128], F32, kind="ExternalInput").ap()
        on_d = nc.dram_tensor("ones", [128, 256], F32, kind="ExternalInput").ap()
    wcat_d = nc.dram_tensor("wcat", [KC * 128, 96 * 128], F32, kind="ExternalInput").ap()
    wout_d = nc.dram_tensor("wout", [8 * 128, KC * 512], F32, kind="ExternalInput").ap()
    wq_d = nc.dram_tensor("wq", [16 * 128, KC * 128], F32, kind="ExternalInput").ap()
    sk_d = nc.dram_tensor("skT", [128, 16 * 128], F32, kind="ExternalInput").ap()
    uT_d = nc.dram_tensor("uT", [128 * 128, KC * 128], F32, kind="ExternalInput").ap()
    v_d = nc.dram_tensor("v", [128 * 128, D], F32, kind="ExternalInput").ap()
    id_d = nc.dram_tensor("ident", [128, 128], BF16, kind="ExternalInput").ap()
    y_d = nc.dram_tensor("y", [TOK2, D], F32, kind="ExternalOutput").ap()
    r_y = [[res("y%d_%d" % (b, n)) for n in range(8)] for b in range(8)]

    top = ExitStack()
    cf = sbt(top, "cf", [128, C2_N], F32); r_cf = res("cf")
    ident = sbt(top, "ident", [128, 128], BF16); r_id = res("ident")
    grep = sbt(top, "grep", [128, D], F32); r_grep = res("grep")
    AB = sbt(top, "AB", [128, 2 * KC], F32); r_AB = res("AB")
    skT = sbt(top, "skT", [128, 16 * 128], BF16); r_sk = res("skT")
    st = sbt(top, "st", [128, 8], F32); r_st = res("st")
    kb.dma("sp", cf[:], cf_d[:, :], r_cf, writes=[r_cf])
    kb.dma("sp", ident[:], id_d[:, :], r_id, writes=[r_id])
    kb.dma("pool", skT[:], sk_d[:, :], r_sk, writes=[r_sk])
    if fz is not None:
        with ExitStack() as sm_:
            ct = sbt(sm_, "ct", [128, KC], F32); r_ct = res("ct")
            scv = sbt(sm_, "scv", [128, KC], F32); r_scv = res("scv")
            bt = sbt(sm_, "bt", [1, D], F32); r_bt = res("bt")
            ot = sbt(sm_, "otm", [1, D], F32); r_otm = res("otm")
            wt = [sbt(sm_, "wt%d" % i, [128, 2, D], F32) for i in range(2)]; r_wt = [res("wt0"), res("wt1")]
            pss = [pst(sm_, "pm%d" % i, [128, 512], F32) for i in range(8)]; r_pss_ = [res("pm%d" % i) for i in range(8)]
            kb.dma("sp", ct[:], cT_d[:, :], r_ct, writes=[r_ct])
            kb.op("act", lambda e: e.activation(out=scv[:], in_=ct[:], func=AF.Silu), reads=[r_ct], writes=[r_scv])
            wv_ = wada_d.rearrange("(k p) n -> p k n", p=128)
            seqm = [(ps_, g) for ps_ in range(6) for g in range(KC // 2)]

            def load_m(i):
                ps_, g = seqm[i]
                kb.dma("sp", wt[i % 2][:], wv_[:, g * 2:(g + 1) * 2, ps_ * D:(ps_ + 1) * D], r_wt[i % 2], writes=[r_wt[i % 2]])

            load_m(0)
            for i, (ps_, g) in enumerate(seqm):
                if i + 1 < len(seqm):
                    load_m(i + 1)
                if g == 0:
                    kb.dma("sp", bt[:], bada_d[:, ps_ * D:(ps_ + 1) * D], r_bt, writes=[r_bt])
                for kk in range(2):
                    kc = g * 2 + kk
                    for n in range(8):
                        kb.op("pe", lambda e, kc=kc, kk=kk, n=n, i=i: e.matmul(pss[n][0:1, :], lhsT=scv[:, kc:kc + 1], rhs=wt[i % 2][:, kk, n * 512:(n + 1) * 512],
                                                                              start=(kc == 0), stop=(kc == KC - 1)),
                              reads=[r_scv, r_wt[i % 2]], writes=[r_pss_[n]], sig=(kc == KC - 1 or (kk == 1 and n == 7)))
                if g == KC // 2 - 1:
                    for n in range(8):
                        kb.op("dve", lambda e, n=n: e.tensor_tensor(out=ot[0:1, n * 512:(n + 1) * 512], in0=pss[n][0:1, :], in1=bt[0:1, n * 512:(n + 1) * 512], op=ALU.add),
                              reads=[r_pss_[n], r_bt], writes=[r_otm])
                    kb.dma("sp", mod_s[:, ps_ * D:(ps_ + 1) * D], ot[:], r_otm, reads=[r_otm], writes=[r_mod], part=True)
            barrier()
        modcol = mod_s.rearrange("o (k p) -> p (o k)", p=128)
        for sec, c0 in ((0, C2_SHIFT1), (1, C2_SCALE1), (3, C2_SHIFT2), (4, C2_SCALE2)):
            kb.dma("sp", cf[:, c0:c0 + KC], modcol[:, sec * KC:(sec + 1) * KC], r_cf, reads=[r_mod], writes=[r_cf], part=True,
                   allow_slow_non_contiguous=True)
    kb.op("dve", lambda e: e.scalar_tensor_tensor(out=AB[:, 0:KC], in0=cf[:, C2_SCALE1:C2_SCALE1 + KC], scalar=1.0,
                                                 in1=cf[:, C2_G1:C2_G1 + KC], op0=ALU.add, op1=ALU.mult), reads=[r_cf], writes=[r_AB])
    kb.op("dve", lambda e: e.scalar_tensor_tensor(out=AB[:, KC:2 * KC], in0=cf[:, C2_SCALE2:C2_SCALE2 + KC], scalar=1.0,
                                                 in1=cf[:, C2_G2:C2_G2 + KC], op0=ALU.add, op1=ALU.mult), reads=[r_cf], writes=[r_AB])
    P = {}
    cnt = dict(tr=0, ev=0, w=0, wo=0, xo=0, mm=0, u=0, vs=0, ga=0, gt=0, act=0, out=0, ot=0, ce=0)

    def norm_T(xb, rx, junk, r_junk, dst_fn, r_dst, acol, bcol):
        ps_tr, r_ptr = P["tr"], P["rtr"]
        kb.op("dve", lambda e: e.memset(st[:, 0:2], 0.0), writes=[r_st])
        for b in range(2):
            kb.op("act", lambda e, b=b: e.activation(out=junk[:], in_=xb[:, b, :], func=AF.Square, scale=1.0 / 64.0,
                                                    accum_out=st[:, b:b + 1]), reads=[rx], writes=[r_junk, r_st])
        kb.op("act", lambda e: e.activation(out=st[:, 2:4], in_=st[:, 0:2], func=AF.Sqrt, bias=EPS, scale=1.0), reads=[r_st], writes=[r_st])
        kb.op("dve", lambda e: e.reciprocal(out=st[:, 4:6], in_=st[:, 2:4]), reads=[r_st], writes=[r_st])
        for b in range(2):
            kb.op("dve", lambda e, b=b: e.tensor_scalar(out=xb[:, b, :], in0=xb[:, b, :], scalar1=st[:, 4 + b:5 + b], scalar2=None,
                                                       op0=ALU.mult), reads=[r_st, rx], writes=[rx])
        for g in range(KC // 4):
            pt = cnt["tr"] % 2; cnt["tr"] += 1
            for k4 in range(4):
                kc = g * 4 + k4
                for b in range(2):
                    kb.op("pe", lambda e, kc=kc, k4=k4, b=b, pt=pt: e.transpose(
                        out=ps_tr[pt][:, k4 * 256 + b * 128:k4 * 256 + (b + 1) * 128], in_=xb[:, b, kc * 128:(kc + 1) * 128], identity=ident[:]),
                        reads=[rx, r_id], writes=[r_ptr[pt]], sig=(k4 == 3 and b == 1))
            for k4 in range(4):
                kc = g * 4 + k4
                dst = dst_fn(kc)
                src = ps_tr[pt][:, k4 * 256:(k4 + 1) * 256]
                if cnt["ev"] % 2 == 0:
                    kb.op("act", lambda e, dst=dst, src=src, kc=kc: e.activation(out=dst, in_=src, func=AF.Identity,
                                                                                scale=AB[:, acol + kc:acol + kc + 1], bias=cf[:, bcol + kc:bcol + kc + 1]),
                          reads=[r_ptr[pt], r_AB, r_cf], writes=[r_dst])
                else:
                    kb.op("dve", lambda e, dst=dst, src=src, kc=kc: e.tensor_scalar(out=dst, in0=src, scalar1=AB[:, acol + kc:acol + kc + 1],
                                                                                   scalar2=cf[:, bcol + kc:bcol + kc + 1], op0=ALU.mult, op1=ALU.add),
                          reads=[r_ptr[pt], r_AB, r_cf], writes=[r_dst])
                cnt["ev"] += 1

    xv = x_d.rearrange("(s b p) d -> s p b d", b=2, p=128)
    yv = y_d.rearrange("(s b p) d -> s p b d", b=2, p=128)
    alv = al_d.rearrange("(k p) t -> p k t", p=128)
    vv = v_d.rearrange("(i j) d -> j i d", j=128)

    if fz is not None:
        with ExitStack() as s0:
            xbs = [sbt(s0, "xb%d" % i, [128, 2, D], BF16) for i in range(2)]; r_xbs = [res("xb0"), res("xb1")]
            junk = sbt(s0, "junk", [128, D], BF16); r_junk = res("junk")
            hTt = [sbt(s0, "hTt%d" % i, [128, KC, 256], BF16) for i in range(2)]; r_hTt = [res("hTt0"), res("hTt1")]
            P["tr"] = [pst(s0, "tr%d" % i, [128, 1024], BF16) for i in range(2)]; P["rtr"] = [res("ptr0"), res("ptr1")]
            xpv = xp_d.rearrange("(s b p) d -> s p b d", b=2, p=128)
            kb.dma("pool", xbs[0][:], xpv[0], r_xbs[0], writes=[r_xbs[0]])
            for T in range(NTL):
                if T + 1 < NTL:
                    kb.dma("pool", xbs[(T + 1) % 2][:], xpv[T + 1], r_xbs[(T + 1) % 2], writes=[r_xbs[(T + 1) % 2]])
                hb = hTt[T % 2]
                norm_T(xbs[T % 2], r_xbs[T % 2], junk, r_junk, lambda kc, hb=hb: hb[:, kc, :], r_hTt[T % 2], 0, C2_SHIFT1)
                kb.dma("sp", hT_s[T * 128:(T + 1) * 128, :].rearrange("p (k t) -> p k t", t=256), hb[:], r_hTt[T % 2],
                       reads=[r_hTt[T % 2]], writes=[r_hTs[T]], part=True)
            barrier()
        with ExitStack() as s1:
            TTm = 256
            cfm = sbt(s1, "cfm", [128, 2 + 16 * CF_UW], F32); r_cfm = res("cfm")
            bv = sbt(s1, "bv", [128, 16 * 129], F32); r_bv = res("bv")
            tmask = sbt(s1, "tmask", [128, NTL], F32); r_tm = res("tmask")
            maskneg = sbt(s1, "maskneg", [128, 128], F32); r_cm = res("maskneg")
            identf = sbt(s1, "identf", [128, 128], F32); r_idf = res("identf")
            tri = sbt(s1, "tri", [128, 128], F32); r_tri = res("tri")
            ones = sbt(s1, "ones", [128, 256], F32); r_on = res("ones")
            c8 = sbt(s1, "c8", [128, 4], F32); r_c8 = res("c8")
            nbias = sbt(s1, "nbias", [128, 2], F32); r_nbias = res("nbias")
            wax = sbt(s1, "wax", [128, 256], BF16); r_wax = res("wax")
            wsl = sbt(s1, "wsl", [128, KC, WCOLS], BF16); r_wsl = res("wsl")
            hTt = [sbt(s1, "hTm%d" % i, [128, KC, 256], BF16) for i in range(2)]; r_hTt = [res("hTm0"), res("hTm1")]
            KT = sbt(s1, "KT", [128, NTL * 256], BF16); r_KT = res("KT")
            V = sbt(s1, "V", [128, NTL * 2, 129], BF16); r_V = res("V")
            QT = sbt(s1, "QT", [128, 256], BF16); r_QT = res("QT")
            NFT = sbt(s1, "NFT", [128, NTL * 2], F32); r_NFT = res("NFT")
            NR = sbt(s1, "NR", [128, NTL * 2 + 1], F32); r_NR = res("NR")
            fl = sbt(s1, "fl", [128, 12], F32); r_fl = res("fl")
            qf = sbt(s1, "qf", [128, 256], F32); r_qf = res("qf")
            qs = sbt(s1, "qs", [128, 256], F32); r_qs = res("qs")
            Bj = [sbt(s1, "Bj%d" % i, [128, NTL * 2], F32) for i in range(2)]; r_Bj = [res("Bj0"), res("Bj1")]
            NPT = 4
            PT = [sbt(s1, "PT%d" % i, [128, 128], BF16) for i in range(NPT)]; r_PT = [res("PT%d" % i) for i in range(NPT)]
            rec = sbt(s1, "rec", [128, 2], F32); r_rec = res("rec")
            ao = sbt(s1, "ao", [128, 128], BF16); r_ao = res("ao")
            aoT = [sbt(s1, "aoT%d" % i, [128, 256], BF16) for i in range(2)]; r_aoT = [res("aoT0"), res("aoT1")]
            dgn = sbt(s1, "dgn", [128, 128], F32); r_dgn = res("dgn")
            Dm = sbt(s1, "Dm", [128, 128], F32); r_Dm = res("Dm")
            sdg = sbt(s1, "sdg", [128, 128], F32); r_sdg = res("sdg")
            osb = sbt(s1, "osb", [128, 129], F32); r_osb = res("osb")
            lx = [sbt(s1, "lx%d" % i, [128, 256 + 3], F32) for i in range(2)]; r_lx = [res("lx0"), res("lx1")]
            xc = sbt(s1, "xc", [128, 256], F32); r_xc = res("xc")
            xcb = sbt(s1, "xcb", [128, 256], BF16); r_xcb = res("xcb")
            rg = sbt(s1, "rg", [128, 256], F32); r_rg = res("rg")
            ig = sbt(s1, "ig", [128, 256], F32); r_ig = res("ig")
            aa = sbt(s1, "aa", [128, 256], F32); r_aa = res("aa")
            a2 = sbt(s1, "a2", [128, 256], F32); r_a2 = res("a2")
            hh = [sbt(s1, "hh%d" % i, [128, 256], F32) for i in range(2)]; r_hh = [res("hh0"), res("hh1")]
            gy = sbt(s1, "gy", [128, 256], F32); r_gy = res("gy")
            lo = [sbt(s1, "lo%d" % i, [128, 256], BF16) for i in range(2)]; r_lo = [res("lo0"), res("lo1")]
            zero1 = sbt(s1, "zero1", [128, 1], F32); r_z = res("zero1")
            ps_tr = [pst(s1, "tr%d" % i, [128, 1024], BF16) for i in range(1)]; r_ptr = [res("ptr0")]
            ps_mm = [pst(s1, "mm%d" % i, [128, 512], F32) for i in range(3)]; r_pmm = [res("pmm0"), res("pmm1"), res("pmm2")]
            ps_s = [pst(s1, "s%d" % i, [128, 512], F32) for i in range(2)]; r_pss = [res("pss0"), res("pss1")]
            ps_o = [pst(s1, "o%d" % i, [128, 512], F32) for i in range(2)]; r_po = [res("po0"), res("po1")]
            kb.dma("sp", cfm[:], cfm_d[:, :], r_cfm, writes=[r_cfm])
            kb.dma("sp", bv[:], bvm_d[:, :], r_bv, writes=[r_bv])
            kb.dma("sp", tmask[:], tm_d[:, :], r_tm, writes=[r_tm])
            kb.dma("sp", maskneg[:], cm_d[:, :], r_cm, writes=[r_cm])
            kb.dma("sp", identf[:], idf_d[:, :], r_idf, writes=[r_idf])
            kb.dma("sp", tri[:], tri_d[:, :], r_tri, writes=[r_tri])
            kb.dma("sp", ones[:], on_d[:, :], r_on, writes=[r_on])
            kb.op("dve", lambda e: e.memset(zero1[:], 0.0), writes=[r_z])
            SCALE = float(128 ** -0.5)
            mc = dict(mm=0, tr=0, s=0, pt=0, bj=0, lo=0, h=0)
            OWN0 = NTL - NOWN
            for u in range(NU):
                ub = 2 + u * CF_UW
                col = lambda k, ub=ub: cfm[:, ub + k:ub + k + 1]
                for part in range(4):
                    kb.dma("pool", wsl[:, part * 8:(part + 1) * 8, :],
                           wsl_d[u * 128:(u + 1) * 128, part * 8 * WCOLS:(part + 1) * 8 * WCOLS].rearrange("p (k c) -> p k c", c=WCOLS),
                           r_wsl, writes=[r_wsl], part=(part > 0))
                kb.dma("pool", wax[:], wax_d[:, u * 256:(u + 1) * 256], r_wax, writes=[r_wax])
                kb.op("act", lambda e: e.activation(out=c8[:, 2:3], in_=col(11), func=AF.Exp, scale=-1.0), reads=[r_cfm], writes=[r_c8])
                kb.op("act", lambda e: e.activation(out=c8[:, 3:4], in_=c8[:, 2:3], func=AF.Ln, bias=1.0, scale=1.0), reads=[r_c8], writes=[r_c8])
                kb.op("dve", lambda e: e.tensor_scalar(out=c8[:, 0:1], in0=c8[:, 3:4], scalar1=-8.0, scalar2=None, op0=ALU.mult), reads=[r_c8], writes=[r_c8])
                kb.op("dve", lambda e: e.tensor_scalar(out=c8[:, 1:2], in0=c8[:, 3:4], scalar1=-16.0, scalar2=None, op0=ALU.mult), reads=[r_c8], writes=[r_c8])
                kb.op("dve", lambda e: e.memset(NR[:, 0:1], 0.0), writes=[r_NR])
                kb.op("dve", lambda e: e.memset(lx[0][:, 0:3], 0.0), writes=[r_lx[0]])
                kb.op("dve", lambda e: e.tensor_scalar(out=nbias[:, 0:2], in0=cfm[:, ub + 9:ub + 11], scalar1=-1.0, scalar2=None, op0=ALU.mult),
                      reads=[r_cfm], writes=[r_nbias])

                def load_h(T):
                    i_ = mc["h"] % 2; mc["h"] += 1
                    kb.dma("sp", hTt[i_][:], hT_s[T * 128:(T + 1) * 128, :].rearrange("p (k t) -> p k t", t=256), r_hTt[i_],
                           reads=[r_hTs[T]], writes=[r_hTt[i_]])
                    return i_

                h_next = load_h(0)
                for T in range(NTL):
                    own = T >= OWN0
                    hi_ = h_next
                    if T + 1 < NTL:
                        h_next = load_h(T + 1)
                    hT = hTt[hi_]; r_hT = r_hTt[hi_]
                    tmc = tmask[:, T:T + 1]

                    def proj(si):
                        pm = mc["mm"] % 3; mc["mm"] += 1
                        for kc in range(KC):
                            kb.op("pe", lambda e, kc=kc, pm=pm: e.matmul(ps_mm[pm][:, 0:256], lhsT=wsl[:, kc, si * 128:(si + 1) * 128], rhs=hT[:, kc, :],
                                                                         start=(kc == 0), stop=(kc == KC - 1)),
                                  reads=[r_wsl, r_hT], writes=[r_pmm[pm]], sig=(kc == KC - 1))
                        return pm

                    def qknorm(pm, bcol, gcol, dst, r_dst):
                        kb.op("act", lambda e: e.activation(out=qf[:], in_=ps_mm[pm][:, 0:256], func=AF.Identity, bias=bcol, scale=1.0),
                              reads=[r_pmm[pm], r_cfm], writes=[r_qf])
                        kb.op("act", lambda e: e.activation(out=qs[:], in_=qf[:], func=AF.Square), reads=[r_qf], writes=[r_qs])
                        px = mc["mm"] % 3; mc["mm"] += 1
                        kb.op("pe", lambda e: e.matmul(ps_mm[px][:, 0:256], lhsT=ones[:, 0:128], rhs=qs[:], start=True, stop=True),
                              reads=[r_on, r_qs], writes=[r_pmm[px]])
                        kb.op("act", lambda e: e.activation(out=qs[:], in_=ps_mm[px][:, 0:256], func=AF.Ln, bias=EPS, scale=1.0),
                              reads=[r_pmm[px]], writes=[r_qs])
                        kb.op("act", lambda e: e.activation(out=qs[:], in_=qs[:], func=AF.Exp, scale=-0.5), reads=[r_qs], writes=[r_qs])
                        kb.op("dve", lambda e: e.scalar_tensor_tensor(out=dst, in0=qf[:], scalar=gcol, in1=qs[:], op0=ALU.mult, op1=ALU.mult),
                              reads=[r_qf, r_qs, r_cfm], writes=[r_dst])

                    pm = proj(1)
                    kb.op("act", lambda e, pm=pm: e.activation(out=qf[:], in_=ps_mm[pm][:, 0:256], func=AF.Identity, bias=col(1), scale=1.0),
                          reads=[r_pmm[pm], r_cfm], writes=[r_qf])
                    kb.op("act", lambda e: e.activation(out=qs[:], in_=qf[:], func=AF.Square), reads=[r_qf], writes=[r_qs])
                    lxi = T % 2
                    lxt = lx[lxi]
                    pm = proj(2)
                    kb.op("act", lambda e, pm=pm: e.activation(out=lx[lxi][:, 3:259], in_=ps_mm[pm][:, 0:256], func=AF.Identity, bias=col(2), scale=1.0),
                          reads=[r_pmm[pm], r_cfm], writes=[r_lx[lxi]])
                    kb.op("dve", lambda e: e.tensor_scalar(out=lx[lxi][:, 3:259], in0=lx[lxi][:, 3:259], scalar1=tmc, scalar2=None, op0=ALU.mult),
                          reads=[r_lx[lxi], r_tm], writes=[r_lx[lxi]])
                    kb.op("dve", lambda e: e.tensor_scalar(out=xc[:], in0=lxt[:, 0:256], scalar1=col(4), scalar2=col(8), op0=ALU.mult, op1=ALU.add),
                          reads=[r_lx[lxi], r_cfm], writes=[r_xc])
                    for k in range(1, 4):
                        kb.op("dve", lambda e, k=k: e.scalar_tensor_tensor(out=xc[:], in0=lxt[:, k:k + 256], scalar=col(4 + k), in1=xc[:],
                                                                          op0=ALU.mult, op1=ALU.add),
                              reads=[r_lx[lxi], r_cfm, r_xc], writes=[r_xc])
                    kb.op("pool", lambda e: e.tensor_copy(out=lx[1 - lxi][:, 0:3], in_=lxt[:, 256:259]), reads=[r_lx[lxi]], writes=[r_lx[1 - lxi]])
                    kb.op("act", lambda e: e.activation(out=xcb[:], in_=xc[:], func=AF.Identity), reads=[r_xc], writes=[r_xcb])
                    for b in range(2):
                        blk = T * 2 + b
                        pm = mc["mm"] % 3; mc["mm"] += 1
                        for kc in range(KC):
                            kb.op("pe", lambda e, kc=kc, pm=pm, b=b: e.matmul(ps_mm[pm][:, 0:129], lhsT=hT[:, kc, b * 128:(b + 1) * 128],
                                                                              rhs=wsl[:, kc, 512:641], start=(kc == 0), stop=(kc == KC - 1)),
                                  reads=[r_wsl, r_hT], writes=[r_pmm[pm]], sig=(kc == KC - 1))
                        kb.op("dve", lambda e, pm=pm, blk=blk: e.tensor_tensor(out=V[:, blk, 0:128], in0=ps_mm[pm][:, 0:128],
                                                                               in1=bv[:, u * 129:u * 129 + 128], op=ALU.add),
                              reads=[r_pmm[pm], r_bv], writes=[r_V])
                        kb.op("dve", lambda e, blk=blk: e.tensor_scalar(out=V[:, blk, 0:128], in0=V[:, blk, 0:128], scalar1=tmc, scalar2=None, op0=ALU.mult),
                              reads=[r_V, r_tm], writes=[r_V])
                        kb.op("dve", lambda e, blk=blk: e.tensor_copy(out=V[:, blk, 128:129], in_=tmc), reads=[r_tm], writes=[r_V])
                        kb.op("dve", lambda e, pm=pm, b=b: e.tensor_tensor(out=fl[:, b:b + 1], in0=ps_mm[pm][:, 128:129],
                                                                           in1=bv[:, u * 129 + 128:u * 129 + 129], op=ALU.add),
                              reads=[r_pmm[pm], r_bv], writes=[r_fl])
                    px = mc["mm"] % 3; mc["mm"] += 1
                    kb.op("pe", lambda e: e.matmul(ps_mm[px][:, 0:256], lhsT=ones[:, 0:128], rhs=qs[:], start=True, stop=True),
                          reads=[r_on, r_qs], writes=[r_pmm[px]])
                    kb.op("act", lambda e: e.activation(out=qs[:], in_=ps_mm[px][:, 0:256], func=AF.Ln, bias=EPS, scale=1.0),
                          reads=[r_pmm[px]], writes=[r_qs])
                    kb.op("act", lambda e: e.activation(out=qs[:], in_=qs[:], func=AF.Exp, scale=-0.5), reads=[r_qs], writes=[r_qs])
                    kb.op("dve", lambda e: e.scalar_tensor_tensor(out=KT[:, T * 256:(T + 1) * 256], in0=qf[:], scalar=cfm[:, 1:2], in1=qs[:], op0=ALU.mult, op1=ALU.mult),
                          reads=[r_qf, r_qs, r_cfm], writes=[r_KT])
                    pmr = mc["mm"] % 3; mc["mm"] += 1
                    kb.op("pe", lambda e: e.matmul(ps_mm[pmr][:, 0:256], lhsT=wax[:, 0:128], rhs=xcb[:], start=True, stop=True),
                          reads=[r_wax, r_xcb], writes=[r_pmm[pmr]])
                    pmi = mc["mm"] % 3; mc["mm"] += 1
                    kb.op("pe", lambda e: e.matmul(ps_mm[pmi][:, 0:256], lhsT=wax[:, 128:256], rhs=xcb[:], start=True, stop=True),
                          reads=[r_wax, r_xcb], writes=[r_pmm[pmi]])
                    kb.op("act", lambda e: e.activation(out=fl[:, 4:6], in_=fl[:, 0:2], func=AF.Exp, scale=-1.0), reads=[r_fl], writes=[r_fl])
                    kb.op("act", lambda e: e.activation(out=fl[:, 8:10], in_=fl[:, 4:6], func=AF.Ln, bias=1.0, scale=1.0), reads=[r_fl], writes=[r_fl])
                    kb.op("act", lambda e: e.activation(out=rg[:], in_=ps_mm[pmr][:, 0:256], func=AF.Exp, bias=nbias[:, 0:1], scale=-1.0),
                          reads=[r_pmm[pmr], r_nbias], writes=[r_rg])
                    kb.op("act", lambda e: e.activation(out=ig[:], in_=ps_mm[pmi][:, 0:256], func=AF.Exp, bias=nbias[:, 1:2], scale=-1.0),
                          reads=[r_pmm[pmi], r_nbias], writes=[r_ig])
                    pxc = mc["mm"] % 3; mc["mm"] += 1
                    ps_x = ps_mm[pxc]; r_px = r_pmm[pxc]
                    kb.op("pe", lambda e: e.matmul(ps_x[:, 0:2], lhsT=tri[:], rhs=fl[:, 8:10], start=True, stop=True),
                          reads=[r_tri, r_fl], writes=[r_px], sig=False)
                    kb.op("pe", lambda e: e.matmul(ps_x[:, 4:6], lhsT=ones[:, 128:256], rhs=fl[:, 8:10], start=True, stop=True),
                          reads=[r_on, r_fl], writes=[r_px])
                    for b in range(2):
                        blk = T * 2 + b
                        kb.op("dve", lambda e, b=b, blk=blk: e.tensor_tensor(out=NR[:, blk + 1:blk + 2], in0=NR[:, blk:blk + 1],
                                                                             in1=ps_x[:, 4 + b:5 + b], op=ALU.add),
                              reads=[r_px, r_NR], writes=[r_NR])
                    kb.op("dve", lambda e: e.tensor_tensor(out=NFT[:, T * 2:T * 2 + 2], in0=ps_x[:, 0:2], in1=NR[:, T * 2:T * 2 + 2], op=ALU.add),
                          reads=[r_px, r_NR], writes=[r_NFT])
                    kb.op("act", lambda e: e.activation(out=rg[:], in_=rg[:], func=AF.Ln, bias=1.0, scale=1.0), reads=[r_rg], writes=[r_rg])
                    kb.op("act", lambda e: e.activation(out=rg[:], in_=rg[:], func=AF.Exp, scale=-1.0), reads=[r_rg], writes=[r_rg])
                    kb.op("act", lambda e: e.activation(out=ig[:], in_=ig[:], func=AF.Ln, bias=1.0, scale=1.0), reads=[r_ig], writes=[r_ig])
                    kb.op("act", lambda e: e.activation(out=ig[:], in_=ig[:], func=AF.Exp, scale=-1.0), reads=[r_ig], writes=[r_ig])
                    kb.op("act", lambda e: e.activation(out=aa[:], in_=rg[:], func=AF.Exp, scale=c8[:, 0:1]), reads=[r_rg, r_c8], writes=[r_aa])
                    kb.op("act", lambda e: e.activation(out=a2[:], in_=rg[:], func=AF.Exp, scale=c8[:, 1:2]), reads=[r_rg, r_c8], writes=[r_a2])
                    kb.op("act", lambda e: e.activation(out=a2[:], in_=a2[:], func=AF.Ln, bias=1.0, scale=-1.0), reads=[r_a2], writes=[r_a2])
                    kb.op("act", lambda e: e.activation(out=a2[:], in_=a2[:], func=AF.Exp, scale=0.5), reads=[r_a2], writes=[r_a2])
                    kb.op("dve", lambda e: e.scalar_tensor_tensor(out=ig[:], in0=ig[:], scalar=tmc, in1=xc[:], op0=ALU.mult, op1=ALU.mult),
                          reads=[r_ig, r_xc, r_tm], writes=[r_ig])
                    kb.op("dve", lambda e: e.tensor_tensor(out=ig[:], in0=ig[:], in1=a2[:], op=ALU.mult), reads=[r_ig, r_a2], writes=[r_ig])
                    hi = T % 2
                    init = zero1[:, 0:1] if T == 0 else hh[1 - hi][:, 255:256]
                    kb.op("dve", lambda e, init=init: e.tensor_tensor_scan(out=hh[hi][:], data0=aa[:], data1=ig[:], initial=init, op0=ALU.mult, op1=ALU.add),
                          reads=[r_aa, r_ig, r_z, r_hh[1 - hi]], writes=[r_hh[hi]])
                    if own:
                        pm = proj(0)
                        qknorm(pm, col(0), cfm[:, 0:1], QT[:], r_QT)
                        pm = proj(3)
                        kb.op("act", lambda e, pm=pm: e.activation(out=gy[:], in_=ps_mm[pm][:, 0:256], func=AF.Gelu, bias=col(3), scale=1.0),
                              reads=[r_pmm[pm], r_cfm], writes=[r_gy])
                        li = mc["lo"] % 2; mc["lo"] += 1
                        kb.op("dve", lambda e, li=li: e.tensor_tensor(out=lo[li][:], in0=hh[hi][:], in1=gy[:], op=ALU.mult),
                              reads=[r_hh[hi], r_gy], writes=[r_lo[li]])
                        tcol = (T - OWN0) * 256
                        kb.dma("sp", al_d[(16 + u) * 128:(17 + u) * 128, tcol:tcol + 256], lo[li][:], r_lo[li], reads=[r_lo[li]], writes=[r_al], part=True)

                    if own:
                        groups = []
                        for jq in range(2):
                            j = T * 2 + jq
                            ks = list(range(j))
                            for g0 in range(0, len(ks), 4):
                                groups.append((jq, ks[g0:g0 + 4]))
                        bj_of = {}
                        sb_of = {}

                        def emit_s(n):
                            jq, ks = groups[n]
                            j = T * 2 + jq
                            if ks[0] == 0:
                                bi = mc["bj"] % 2; mc["bj"] += 1
                                bj_of[jq] = bi
                                kb.op("dve", lambda e, bi=bi, j=j: e.tensor_scalar(out=Bj[bi][:, 0:j], in0=NFT[:, 0:j], scalar1=NR[:, j:j + 1],
                                                                                  scalar2=None, op0=ALU.subtract),
                                      reads=[r_NFT, r_NR], writes=[r_Bj[bi]])
                            sbk = mc["s"] % 2; mc["s"] += 1
                            sb_of[n] = sbk
                            for kk, i in enumerate(ks):
                                kb.op("pe", lambda e, sbk=sbk, kk=kk, i=i, jq=jq: e.matmul(ps_s[sbk][:, kk * 128:(kk + 1) * 128],
                                                                                          lhsT=KT[:, i * 128:(i + 1) * 128], rhs=QT[:, jq * 128:(jq + 1) * 128],
                                                                                          start=True, stop=True),
                                      reads=[r_KT, r_QT], writes=[r_pss[sbk]], sig=(kk == len(ks) - 1))

                        def diag_block(jq):
                            j = T * 2 + jq
                            oi = jq % 2
                            kb.op("dve", lambda e: e.tensor_scalar(out=dgn[:], in0=identf[:], scalar1=NFT[:, j:j + 1], scalar2=None, op0=ALU.mult),
                                  reads=[r_idf, r_NFT], writes=[r_dgn])
                            px = mc["mm"] % 3; mc["mm"] += 1
                            kb.op("pe", lambda e: e.matmul(ps_mm[px][:, 0:128], lhsT=ones[:, 128:256], rhs=dgn[:], start=True, stop=True),
                                  reads=[r_on, r_dgn], writes=[r_pmm[px]])
                            kb.op("dve", lambda e: e.tensor_scalar(out=Dm[:], in0=ps_mm[px][:, 0:128], scalar1=NFT[:, j:j + 1], scalar2=-1.0,
                                                                  op0=ALU.subtract, op1=ALU.mult), reads=[r_pmm[px], r_NFT], writes=[r_Dm])
                            kb.op("dve", lambda e: e.tensor_tensor(out=Dm[:], in0=Dm[:], in1=maskneg[:], op=ALU.add), reads=[r_Dm, r_cm], writes=[r_Dm])
                            sbk = mc["s"] % 2; mc["s"] += 1
                            kb.op("pe", lambda e: e.matmul(ps_s[sbk][:, 0:128], lhsT=KT[:, j * 128:(j + 1) * 128], rhs=QT[:, jq * 128:(jq + 1) * 128],
                                                           start=True, stop=True), reads=[r_KT, r_QT], writes=[r_pss[sbk]])
                            kb.op("dve", lambda e: e.scalar_tensor_tensor(out=sdg[:], in0=ps_s[sbk][:, 0:128], scalar=SCALE, in1=Dm[:], op0=ALU.mult, op1=ALU.add),
                                  reads=[r_pss[sbk], r_Dm], writes=[r_sdg])
                            pi = mc["pt"] % NPT; mc["pt"] += 1
                            kb.op("act", lambda e: e.activation(out=PT[pi][:], in_=sdg[:], func=AF.Exp), reads=[r_sdg], writes=[r_PT[pi]])
                            kb.op("pe", lambda e: e.matmul(ps_o[oi][:, 256:385], lhsT=PT[pi][:], rhs=V[:, j, :], start=True, stop=True),
                                  reads=[r_PT[pi], r_V], writes=[r_po[oi]])
                            if j > 0:
                                kb.op("act", lambda e: e.activation(out=rec[:, 1:2], in_=NFT[:, j:j + 1], func=AF.Exp, scale=-1.0, bias=NR[:, j:j + 1]),
                                      reads=[r_NFT, r_NR], writes=[r_rec])
                                kb.op("act", lambda e: e.activation(out=osb[:], in_=ps_o[oi][:, 256:385], func=AF.Identity), reads=[r_po[oi]], writes=[r_osb])
                                kb.op("dve", lambda e: e.scalar_tensor_tensor(out=osb[:], in0=ps_o[oi][:, 0:129], scalar=rec[:, 1:2], in1=osb[:],
                                                                             op0=ALU.mult, op1=ALU.add), reads=[r_po[oi], r_rec, r_osb], writes=[r_osb])
                            else:
                                kb.op("act", lambda e: e.activation(out=osb[:], in_=ps_o[oi][:, 256:385], func=AF.Identity), reads=[r_po[oi]], writes=[r_osb])
                            ai = T % 2
                            kb.op("dve", lambda e: e.reciprocal(out=rec[:, 0:1], in_=osb[:, 128:129]), reads=[r_osb], writes=[r_rec])
                            kb.op("dve", lambda e: e.tensor_scalar(out=ao[:], in0=osb[:, 0:128], scalar1=rec[:, 0:1], scalar2=None, op0=ALU.mult),
                                  reads=[r_osb, r_rec], writes=[r_ao])
                            pt = 0
                            kb.op("pe", lambda e, pt=pt: e.transpose(out=ps_tr[pt][:, 0:128], in_=ao[:], identity=ident[:]),
                                  reads=[r_ao, r_id], writes=[r_ptr[pt]])
                            kb.op("dve", lambda e, pt=pt: e.tensor_copy(out=aoT[ai][:, jq * 128:(jq + 1) * 128], in_=ps_tr[pt][:, 0:128]),
                                  reads=[r_ptr[pt]], writes=[r_aoT[ai]])

                        if groups:
                            emit_s(0)
                        done_diag = set()
                        for n, (jq, ks) in enumerate(groups):
                            j = T * 2 + jq
                            if n + 1 < len(groups):
                                emit_s(n + 1)
                            sbk = sb_of.pop(n)
                            bi = bj_of[jq]
                            oi = jq % 2
                            for kk, i in enumerate(ks):
                                pi = mc["pt"] % NPT; mc["pt"] += 1
                                kb.op("act", lambda e, sbk=sbk, kk=kk, pi=pi, bi=bi, i=i: e.activation(
                                    out=PT[pi][:], in_=ps_s[sbk][:, kk * 128:(kk + 1) * 128], func=AF.Exp,
                                    scale=SCALE, bias=Bj[bi][:, i:i + 1]),
                                    reads=[r_pss[sbk], r_Bj[bi]], writes=[r_PT[pi]])
                                kb.op("pe", lambda e, pi=pi, i=i, oi=oi, j=j: e.matmul(ps_o[oi][:, 0:129], lhsT=PT[pi][:], rhs=V[:, i, :],
                                                                                        start=(i == 0), stop=(i == j - 1)),
                                      reads=[r_PT[pi], r_V], writes=[r_po[oi]])
                                if i == j - 1:
                                    diag_block(jq)
                                    done_diag.add(jq)
                        for jq in range(2):
                            if jq not in done_diag:
                                diag_block(jq)
                        ai = T % 2
                        tcol = (T - OWN0) * 256
                        kb.dma("sp", al_d[u * 128:(u + 1) * 128, tcol:tcol + 256], aoT[ai][:], r_aoT[ai], reads=[r_aoT[ai]], writes=[r_al], part=True)

            barrier()
    for hf in range(nhalf):
        with ExitStack() as sa:
            xb = sbt(sa, "xb", [128, 2, D], BF16); r_xb = res("xb")
            junk = sbt(sa, "junk", [128, D], BF16); r_junk = res("junk")
            hT = sbt(sa, "hT", [128, KC, 256], BF16); r_hT = res("hT")
            alq = sbt(sa, "alq", [128, KC, 256], BF16); r_alq = res("alq")
            mT = sbt(sa, "mT", [128, KC, 256], BF16); r_mT = res("mT")
            NW = 4
            wp = [sbt(sa, "wp%d" % i, [128, KC, 128], BF16) for i in range(NW)]; r_wp = [res("wp%d" % i) for i in range(NW)]
            wo = [sbt(sa, "wo%d" % i, [128, 8, 512], BF16) for i in range(2)]; r_wo = [res("wo%d" % i) for i in range(2)]
            sg = sbt(sa, "sg", [128, 256], F32); r_sg = res("sg")
            t1 = sbt(sa, "t1", [128, 256], F32); r_t1 = res("t1")
            xt = [sbt(sa, "xt%d" % i, [128, 512], F32) for i in range(2)]; r_xt = [res("xt0"), res("xt1")]
            ot = [sbt(sa, "ot%d" % i, [128, 512], F32) for i in range(2)]; r_ot = [res("ot0"), res("ot1")]
            P["tr"] = [pst(sa, "tr%d" % i, [128, 1024], BF16) for i in range(2)]; P["rtr"] = [res("ptr0"), res("ptr1")]
            pacc = [pst(sa, "acc%d" % i, [128, 512], F32) for i in range(4)]; r_pacc = [res("pacc%d" % i) for i in range(4)]
            pout = [pst(sa, "out%d" % i, [128, 512], F32) for i in range(2)]; r_pout = [res("pout0"), res("pout1")]
            kb.dma("sp", grep[:], g1_d, r_grep, reads=([r_mod] if fz is not None else []), writes=[r_grep])
            for q in range(2):
                qi = hf * 2 + q
                t0 = qi * 256
                kb.dma("pool", xb[:], xv[qi], r_xb, writes=[r_xb])
                kb.dma("sp", alq[:], alv[:, :, t0:t0 + 256], r_alq, reads=[r_al], writes=[r_alq])
                norm_T(xb, r_xb, junk, r_junk, lambda kc: hT[:, kc, :], r_hT, 0, C2_SHIFT1)
                slabs = [(c, m) for c in range(KC) for m in range(3)]

                def load_slab(n):
                    c, m = slabs[n]
                    slot = cnt["w"] % NW; cnt["w"] += 1
                    src = wcat_d[c * 128:(c + 1) * 128, m * 32 * 128:(m + 1) * 32 * 128].rearrange("p (k j) -> p k j", j=128)
                    kb.dma("pool", wp[slot][:], src, r_wp[slot], writes=[r_wp[slot]])
                    return slot

                slot_of = {}
                for n in range(min(3, len(slabs))):
                    slot_of[n] = load_slab(n)
                for n, (c, m) in enumerate(slabs):
                    if n + 3 < len(slabs):
                        slot_of[n + 3] = load_slab(n + 3)
                    sl = slot_of.pop(n)
                    if m < 2:
                        for kc in range(KC):
                            kb.op("pe", lambda e, kc=kc, sl=sl, m=m: e.matmul(pacc[m][:, 0:256], lhsT=wp[sl][:, kc, :], rhs=hT[:, kc, :],
                                                                              start=(kc == 0), stop=(kc == KC - 1)),
                                  reads=[r_wp[sl], r_hT], writes=[r_pacc[m]], sig=(kc == KC - 1))
                    else:
                        for br in range(2):
                            for kk in range(16):
                                kc = br * 16 + kk
                                kb.op("pe", lambda e, kc=kc, kk=kk, sl=sl, br=br: e.matmul(pacc[2 + br][:, 0:256], lhsT=wp[sl][:, kc, :], rhs=alq[:, kc, :],
                                                                                        start=(kk == 0), stop=(kk == 15)),
                                      reads=[r_wp[sl], r_alq], writes=[r_pacc[2 + br]], sig=(kk == 15))
                        kb.op("act", lambda e, c=c: e.activation(out=sg[:], in_=pacc[0][:, 0:256], func=AF.Sigmoid,
                                                                bias=cf[:, C2_BGA + c:C2_BGA + c + 1], scale=1.0), reads=[r_pacc[0], r_cf], writes=[r_sg])
                        kb.op("dve", lambda e: e.tensor_tensor(out=t1[:], in0=sg[:], in1=pacc[2][:, 0:256], op=ALU.mult),
                              reads=[r_sg, r_pacc[2]], writes=[r_t1])
                        kb.op("act", lambda e, c=c: e.activation(out=sg[:], in_=pacc[1][:, 0:256], func=AF.Sigmoid,
                                                                bias=cf[:, C2_BGL + c:C2_BGL + c + 1], scale=1.0), reads=[r_pacc[1], r_cf], writes=[r_sg])
                        kb.op("dve", lambda e: e.tensor_tensor(out=sg[:], in0=sg[:], in1=pacc[3][:, 0:256], op=ALU.mult),
                              reads=[r_sg, r_pacc[3]], writes=[r_sg])
                        kb.op("dve", lambda e, c=c: e.tensor_tensor(out=mT[:, c, :], in0=sg[:], in1=t1[:], op=ALU.add),
                              reads=[r_sg, r_t1], writes=[r_mT])
                def load_wo(n, kg):
                    slot = cnt["wo"] % 2; cnt["wo"] += 1
                    src = wout_d[n * 128:(n + 1) * 128, kg * 8 * 512:(kg + 1) * 8 * 512].rearrange("p (k j) -> p k j", j=512)
                    kb.dma("pool", wo[slot][:], src, r_wo[slot], writes=[r_wo[slot]])
                    return slot

                seq = [(n, kg) for n in range(8) for kg in range(4)]
                nxt = load_wo(*seq[0])
                for idx, (n, kg) in enumerate(seq):
                    sl = nxt
                    if idx + 1 < len(seq):
                        nxt = load_wo(*seq[idx + 1])
                    for b in range(2):
                        for kk in range(8):
                            kc = kg * 8 + kk
                            kb.op("pe", lambda e, b=b, kk=kk, kc=kc, sl=sl: e.matmul(pout[b][:], lhsT=mT[:, kc, b * 128:(b + 1) * 128], rhs=wo[sl][:, kk, :],
                                                                                    start=(kc == 0), stop=(kc == KC - 1)),
                                  reads=[r_mT, r_wo[sl]], writes=[r_pout[b]], sig=(kk == 7))
                    if kg == 3:
                        for b in range(2):
                            blk = qi * 2 + b
                            xs = cnt["xo"] % 2; cnt["xo"] += 1
                            kb.dma("sp", xt[xs][:], x_d[blk * 128:(blk + 1) * 128, n * 512:(n + 1) * 512], r_xt[xs], writes=[r_xt[xs]])
                            kb.op("dve", lambda e, b=b, n=n, xs=xs: e.tensor_tensor(out=ot[xs][:], in0=pout[b][:], in1=grep[:, n * 512:(n + 1) * 512], op=ALU.mult),
                                  reads=[r_pout[b], r_grep], writes=[r_ot[xs]])
                            kb.op("dve", lambda e, xs=xs: e.tensor_tensor(out=ot[xs][:], in0=ot[xs][:], in1=xt[xs][:], op=ALU.add),
                                  reads=[r_ot[xs], r_xt[xs]], writes=[r_ot[xs]])
                            kb.dma("sp", y_d[blk * 128:(blk + 1) * 128, n * 512:(n + 1) * 512], ot[xs][:], r_ot[xs], reads=[r_ot[xs]],
                                   writes=[r_y[blk][n]], part=True)
            barrier()
        with ExitStack() as sb_:
            h2T = sbt(sb_, "h2T", [128, KC, 512], BF16); r_h2T = res("h2T")
            Ssc = sbt(sb_, "Ssc", [128, 4, 16, 128], F32); r_S = res("S")
            tau = sbt(sb_, "tau", [128, 32], F32); r_tau = res("tau")
            nbv = sbt(sb_, "nbv", [128, 32], F32); r_nb = res("nb")
            kb.dma("sp", grep[:], g2_d, r_grep, reads=([r_mod] if fz is not None else []), writes=[r_grep])
            with ExitStack() as sq:
                xb = sbt(sq, "xb", [128, 2, D], BF16); r_xb = res("xb")
                junk = sbt(sq, "junk", [128, D], BF16); r_junk = res("junk")
                qT = sbt(sq, "qT", [128, 16, 512], BF16); r_qT = res("qT")
                wp = [sbt(sq, "wq%d" % i, [128, KC, 128], BF16) for i in range(2)]; r_wp = [res("wq0"), res("wq1")]
                t16 = sbt(sq, "t16", [128, 2, 16], F32); r_t16 = res("t16")
                tmp = sbt(sq, "tmp", [128, 128], F32); r_tmp = res("tmp")
                cand = sbt(sq, "cand", [128, 256], F32); r_cand = res("cand")
                tmp2 = sbt(sq, "tmp2", [128, 256], F32); r_tmp2 = res("tmp2")
                b16 = sbt(sq, "b16", [128, 16], F32); r_b16 = res("b16")
                sm = sbt(sq, "sm", [128, 24], F32); r_sm = res("sm")
                P["tr"] = [pst(sq, "tr%d" % i, [128, 1024], BF16) for i in range(2)]; P["rtr"] = [res("ptr0"), res("ptr1")]
                pq = [pst(sq, "pq%d" % i, [128, 512], F32) for i in range(2)]; r_pq = [res("pq0"), res("pq1")]
                for sub in range(2):
                    qi = hf * 2 + sub
                    rys = [r_y[qi * 2 + b][n] for b in range(2) for n in range(8)]
                    kb.dma("pool", xb[:], yv[qi], r_xb, reads=rys, writes=[r_xb])
                    norm_T(xb, r_xb, junk, r_junk, lambda kc, sub=sub: h2T[:, kc, sub * 256:(sub + 1) * 256], r_h2T, KC, C2_SHIFT2)
                def load_wq(hp):
                    kb.dma("pool", wp[hp % 2][:], wq_d[hp * 128:(hp + 1) * 128, :].rearrange("p (k j) -> p k j", j=128), r_wp[hp % 2], writes=[r_wp[hp % 2]])
                load_wq(0)
                for hp in range(16):
                    if hp + 1 < 16:
                        load_wq(hp + 1)
                    for kc in range(KC):
                        kb.op("pe", lambda e, kc=kc, hp=hp: e.matmul(pq[hp % 2][:], lhsT=wp[hp % 2][:, kc, :], rhs=h2T[:, kc, :], start=(kc == 0), stop=(kc == KC - 1)),
                              reads=[r_wp[hp % 2], r_h2T], writes=[r_pq[hp % 2]], sig=(kc == KC - 1))
                    kb.op("act", lambda e, hp=hp: e.activation(out=qT[:, hp, :], in_=pq[hp % 2][:], func=AF.Identity), reads=[r_pq[hp % 2]], writes=[r_qT])
                for blk in range(4):
                    for g4 in range(4):
                        pb = cnt["mm"] % 2; cnt["mm"] += 1
                        for k in range(4):
                            hp = g4 * 4 + k
                            kb.op("pe", lambda e, hp=hp, k=k, blk=blk, pb=pb: e.matmul(pq[pb][:, k * 128:(k + 1) * 128], lhsT=qT[:, hp, blk * 128:(blk + 1) * 128],
                                                                                       rhs=skT[:, hp * 128:(hp + 1) * 128], start=True, stop=True),
                                  reads=[r_qT, r_sk], writes=[r_pq[pb]], sig=(k == 3))
                        kb.op("act", lambda e, blk=blk, g4=g4, pb=pb: e.activation(out=Ssc[:, blk, g4 * 4:(g4 + 1) * 4, :], in_=pq[pb][:].rearrange("p (a b) -> p a b", b=128),
                                                                                   func=AF.Identity), reads=[r_pq[pb]], writes=[r_S])
                for blk in range(4):
                    for h in range(8):
                        colx = blk * 8 + h
                        for p in range(2):
                            sp_ = Ssc[:, blk, 2 * h + p, :]
                            kb.op("dve", lambda e, p=p, sp_=sp_: e.max(out=t16[:, p, 0:8], in_=sp_), reads=[r_S], writes=[r_t16])
                            kb.op("dve", lambda e, p=p, sp_=sp_: e.match_replace(out=tmp[:], in_to_replace=t16[:, p, 0:8], in_values=sp_, imm_value=NEG),
                                  reads=[r_S, r_t16], writes=[r_tmp])
                            kb.op("dve", lambda e, p=p: e.max(out=t16[:, p, 8:16], in_=tmp[:]), reads=[r_tmp], writes=[r_t16])
                        kb.op("dve", lambda e: e.tensor_tensor(out=cand[:].rearrange("p (a b) -> p a b", b=16),
                                                               in0=t16[:, 0, :].unsqueeze(2).broadcast_to([128, 16, 16]),
                                                               in1=t16[:, 1, :].unsqueeze(1).broadcast_to([128, 16, 16]), op=ALU.add),
                              reads=[r_t16], writes=[r_cand])
                        kb.op("dve", lambda e: e.max(out=b16[:, 0:8], in_=cand[:]), reads=[r_cand], writes=[r_b16])
                        kb.op("dve", lambda e: e.match_replace(out=tmp2[:], in_to_replace=b16[:, 0:8], in_values=cand[:], imm_value=NEG),
                              reads=[r_cand, r_b16], writes=[r_tmp2])
                        kb.op("dve", lambda e: e.max(out=b16[:, 8:16], in_=tmp2[:]), reads=[r_tmp2], writes=[r_b16])
                        kb.op("dve", lambda e, colx=colx: e.tensor_copy(out=tau[:, colx:colx + 1], in_=b16[:, 15:16]), reads=[r_b16], writes=[r_tau])
                        kb.op("dve", lambda e: e.tensor_scalar(out=sm[:, 0:1], in0=b16[:, 0:1], scalar1=-1.0, scalar2=None, op0=ALU.mult), reads=[r_b16], writes=[r_sm])
                        kb.op("dve", lambda e: e.memset(sm[:, 1:2], 0.0), writes=[r_sm])
                        kb.op("act", lambda e: e.activation(out=sm[:, 4:20], in_=b16[:], func=AF.Exp, bias=sm[:, 0:1], scale=1.0, accum_out=sm[:, 1:2]),
                              reads=[r_b16, r_sm], writes=[r_sm])
                        kb.op("act", lambda e: e.activation(out=sm[:, 2:3], in_=sm[:, 1:2], func=AF.Ln), reads=[r_sm], writes=[r_sm])
                        kb.op("dve", lambda e, colx=colx: e.tensor_tensor(out=nbv[:, colx:colx + 1], in0=sm[:, 0:1], in1=sm[:, 2:3], op=ALU.subtract),
                              reads=[r_sm], writes=[r_nb])
                        kb.op("act", lambda e, blk=blk, h=h, colx=colx: e.activation(out=Ssc[:, blk, 2 * h, :], in_=Ssc[:, blk, 2 * h, :], func=AF.Exp,
                                                                                    bias=nbv[:, colx:colx + 1], scale=1.0), reads=[r_S, r_nb], writes=[r_S])
                        kb.op("act", lambda e, blk=blk, h=h: e.activation(out=Ssc[:, blk, 2 * h + 1, :], in_=Ssc[:, blk, 2 * h + 1, :], func=AF.Exp),
                              reads=[r_S], writes=[r_S])
                        kb.op("act", lambda e, colx=colx: e.activation(out=tau[:, colx:colx + 1], in_=tau[:, colx:colx + 1], func=AF.Exp,
                                                                      bias=nbv[:, colx:colx + 1], scale=1.0), reads=[r_tau, r_nb], writes=[r_tau])
                        kb.op("dve", lambda e, colx=colx: e.tensor_scalar(out=tau[:, colx:colx + 1], in0=tau[:, colx:colx + 1], scalar1=0.9999, scalar2=None, op0=ALU.mult),
                              reads=[r_tau], writes=[r_tau])
                barrier()
            with ExitStack() as sc_:
                uTs = [sbt(sc_, "uT%d" % i, [128, KC, 128], BF16) for i in range(2)]; r_uT = [res("uT0"), res("uT1")]
                GTs2 = [sbt(sc_, "GTs%d" % i, [128, 8, 512], BF16) for i in range(2)]; r_GTs2 = [res("GTs0"), res("GTs1")]
                Gact = [sbt(sc_, "Gact%d" % i, [128, 8, 512], BF16) for i in range(2)]; r_Gact = [res("Gact0"), res("Gact1")]
                vs = [sbt(sc_, "vs%d" % i, [128, 8, 512], BF16) for i in range(2)]; r_vs = [res("vs0"), res("vs1")]
                cc = [sbt(sc_, "cc%d" % i, [128, 8, 128], F32) for i in range(2)]; r_cc = [res("cc0"), res("cc1")]
                ww = [sbt(sc_, "ww%d" % i, [128, 1024], BF16) for i in range(4)]; r_ww = [res("ww%d" % i) for i in range(4)]
                ga = [sbt(sc_, "ga%d" % i, [128, 512], BF16) for i in range(2)]; r_ga = [res("ga0"), res("ga1")]
                oc = [sbt(sc_, "oc%d" % i, [128, 512], F32) for i in range(2)]; r_oc = [res("oc0"), res("oc1")]
                pgt = [[pst(sc_, "gt%d_%d" % (i, k), [128, 512], F32) for k in range(2)] for i in range(2)]
                r_pgt = [res("pgt0"), res("pgt1")]
                pact = [pst(sc_, "act%d" % i, [128, 512], F32) for i in range(2)]; r_pact = [res("pact0"), res("pact1")]
                pfo = [pst(sc_, "fo%d" % i, [128, 512], F32) for i in range(2)]; r_pfo = [res("pfo0"), res("pfo1")]
                uview = uT_d.rearrange("(i p) (k j) -> i p k j", p=128, j=128)

                def load_u(i):
                    s_ = cnt["u"] % 2; cnt["u"] += 1
                    kb.dma("pool", uTs[s_][:], uview[i], r_uT[s_], writes=[r_uT[s_]])
                    return s_

                def stage1(ch):
                    i0 = ch * 8
                    gt_ = ch % 2
                    steps = [(blk, h) for blk in range(4) for h in range(8)]
                    kbuf = {}
                    LAG = 3

                    def chain(k):
                        blk, h = steps[k]
                        colx = blk * 8 + h
                        k_ = cnt["ce"] % 4; cnt["ce"] += 1
                        kbuf[k] = k_
                        c_ = k_ % 2
                        kb.op("dve", lambda e: e.tensor_tensor(
                            out=cc[c_][:], in0=Ssc[:, blk, 2 * h, i0:i0 + 8].unsqueeze(2).broadcast_to([128, 8, 128]),
                            in1=Ssc[:, blk, 2 * h + 1, :].unsqueeze(1).broadcast_to([128, 8, 128]), op=ALU.mult),
                            reads=[r_S], writes=[r_cc[c_]])
                        kb.op("dve", lambda e: e.scalar_tensor_tensor(out=ww[k_][:], in0=cc[c_][:].rearrange("p a b -> p (a b)"),
                                                                     scalar=tau[:, colx:colx + 1], in1=cc[c_][:].rearrange("p a b -> p (a b)"),
                                                                     op0=ALU.is_ge, op1=ALU.mult),
                              reads=[r_cc[c_], r_tau], writes=[r_ww[k_]])

                    for k in range(min(LAG, len(steps))):
                        chain(k)
                    gb = None
                    for k, (blk, h) in enumerate(steps):
                        if h == 0:
                            gb = cnt["gt"] % 2; cnt["gt"] += 1
                        k_ = kbuf.pop(k)
                        for ii in range(8):
                            kb.op("pe", lambda e, ii=ii: e.matmul(pgt[gb][ii // 4][:, (ii % 4) * 128:(ii % 4 + 1) * 128],
                                                                  lhsT=ww[k_][:, ii * 128:(ii + 1) * 128], rhs=ident[:],
                                                                  start=(h == 0 and ii % 4 == 0), stop=(h == 7 and ii % 4 == 3)),
                                  reads=[r_ww[k_], r_id], writes=[r_pgt[gb]], sig=(ii == 7))
                        if k + LAG < len(steps):
                            chain(k + LAG)
                        yield
                        if h == 7:
                            for kq in range(2):
                                dst = GTs2[gt_][:, kq * 4:(kq + 1) * 4, blk * 128:(blk + 1) * 128]
                                src = pgt[gb][kq][:].rearrange("p (a b) -> p a b", b=128)
                                if kq == 0:
                                    kb.op("act", lambda e, dst=dst, src=src: e.activation(out=dst, in_=src, func=AF.Identity), reads=[r_pgt[gb]], writes=[r_GTs2[gt_]])
                                else:
                                    kb.op("dve", lambda e, dst=dst, src=src: e.tensor_copy(out=dst, in_=src), reads=[r_pgt[gb]], writes=[r_GTs2[gt_]])
                            yield

                useq = [(ch, ii) for ch in range(nchunk) for ii in range(8)]
                vseq = [(ch, n) for ch in range(nchunk) for n in range(8)]
                uslot = {}
                vslot = {}

                def load_u_idx(k):
                    if k < len(useq) and k not in uslot:
                        ch_, ii_ = useq[k]
                        uslot[k] = load_u(ch_ * 8 + ii_)

                def load_v_idx(k):
                    if k < len(vseq) and k not in vslot:
                        ch_, n_ = vseq[k]
                        s_ = cnt["vs"] % 2; cnt["vs"] += 1
                        kb.dma("pool", vs[s_][:], vv[:, ch_ * 8:ch_ * 8 + 8, n_ * 512:(n_ + 1) * 512], r_vs[s_], writes=[r_vs[s_]])
                        vslot[k] = s_

                def stage23(ch):
                    gsl = ch % 2
                    gt_ = ch % 2
                    for ii in range(8):
                        ku = ch * 8 + ii
                        load_u_idx(ku)
                        us = uslot.pop(ku)
                        load_u_idx(ku + 1)
                        if ii == 7:
                            load_v_idx(ch * 8)
                        pa = cnt["act"] % 2; cnt["act"] += 1
                        for kc in range(KC):
                            kb.op("pe", lambda e, kc=kc, us=us, pa=pa: e.matmul(pact[pa][:], lhsT=uTs[us][:, kc, :], rhs=h2T[:, kc, :], start=(kc == 0), stop=(kc == KC - 1)),
                                  reads=[r_uT[us], r_h2T], writes=[r_pact[pa]], sig=(kc == KC - 1))
                        g_ = cnt["ga"] % 2; cnt["ga"] += 1
                        kb.op("act", lambda e, pa=pa, g_=g_: e.activation(out=ga[g_][:], in_=pact[pa][:], func=AF.Gelu), reads=[r_pact[pa]], writes=[r_ga[g_]])
                        kb.op("pool", lambda e, ii=ii, g_=g_: e.tensor_tensor(out=Gact[gsl][:, ii, :], in0=ga[g_][:], in1=GTs2[gt_][:, ii, :], op=ALU.mult),
                              reads=[r_ga[g_], r_GTs2[gt_]], writes=[r_Gact[gsl]])
                        yield
                    for n in range(8):
                        kv = ch * 8 + n
                        load_v_idx(kv)
                        vsl = vslot.pop(kv)
                        if n + 1 < 8:
                            load_v_idx(kv + 1)
                        for blk in range(4):
                            po = cnt["out"] % 2; cnt["out"] += 1
                            for ii in range(8):
                                kb.op("pe", lambda e, ii=ii, blk=blk, vsl=vsl, po=po: e.matmul(pfo[po][:], lhsT=Gact[gsl][:, ii, blk * 128:(blk + 1) * 128], rhs=vs[vsl][:, ii, :],
                                                                                              start=(ii == 0), stop=(ii == 7)),
                                      reads=[r_Gact[gsl], r_vs[vsl]], writes=[r_pfo[po]], sig=(ii == 7))
                            o_ = cnt["ot"] % 2; cnt["ot"] += 1
                            kb.op("dve", lambda e, po=po, o_=o_, n=n: e.tensor_tensor(out=oc[o_][:], in0=pfo[po][:], in1=grep[:, n * 512:(n + 1) * 512], op=ALU.mult),
                                  reads=[r_pfo[po], r_grep], writes=[r_oc[o_]])
                            gblk = hf * 4 + blk
                            kb.dma("pool", y_d[gblk * 128:(gblk + 1) * 128, n * 512:(n + 1) * 512], oc[o_][:], r_oc[o_], reads=[r_oc[o_], r_y[gblk][n]],
                                   writes=[r_y[gblk][n]], accum_op=ALU.add)
                        yield

                load_u_idx(0)
                for _ in stage1(0):
                    pass
                for ch in range(nchunk):
                    g23 = stage23(ch)
                    g1n = stage1(ch + 1) if ch + 1 < nchunk else iter(())
                    alive1 = True
                    for _ in g23:
                        for _k in range(3):
                            if alive1:
                                try:
                                    next(g1n)
                                except StopIteration:
                                    alive1 = False
                    if alive1:
                        for _ in g1n:
                            pass
                barrier()
    kb.finish([r for row in r_y for r in row] + [r_al, r_mod])
    return nc


def prep_l2_shared(inp):
    w_in = inp["w_in"][0]
    wga = w_in[:, LY_END:LY_END + D].reshape(KC, 128, KC, 128).transpose(2, 1, 0, 3)
    wgl = w_in[:, GA_END:GA_END + D].reshape(KC, 128, KC, 128).transpose(2, 1, 0, 3)
    wao = inp["w_attn_o"][0].reshape(16, 128, KC, 128).transpose(2, 1, 0, 3)
    wlo = inp["w_lru_o"][0].reshape(16, 128, KC, 128).transpose(2, 1, 0, 3)
    wcat = np.ascontiguousarray(np.concatenate([wga, wgl, wao, wlo], axis=2)).reshape(KC * 128, 96 * 128)
    wout = np.ascontiguousarray(inp["w_out"][0].reshape(KC, 128, 8, 512).transpose(2, 1, 0, 3)).reshape(8 * 128, KC * 512)
    wq = np.ascontiguousarray(inp["peer_wq"][0].reshape(KC, 128, 16, 128).transpose(2, 1, 0, 3)).reshape(16 * 128, KC * 128)
    skT = np.ascontiguousarray(inp["peer_subkeys"][0].reshape(16, 128, 128).transpose(2, 0, 1)).reshape(128, 16 * 128)
    uT = np.ascontiguousarray(inp["peer_u"][0].reshape(128, 128, KC, 128).transpose(0, 3, 2, 1)).reshape(128 * 128, KC * 128)
    v = np.ascontiguousarray(inp["peer_v"][0])
    return {"wcat": wcat, "wout": wout, "wq": wq, "skT": skT, "uT": uT, "v": v,
            "ident": np.eye(128, dtype=np.float32).astype(NPBF)}


def prep_l2(inp, mod, al_all, shared=None):
    if shared is None:
        shared = prep_l2_shared(inp)
    x = inp["x"][0]
    b_in = inp["b_in"][0]
    cf = np.zeros((128, C2_N), np.float32)
    cf[:, C2_SHIFT1:C2_SHIFT1 + KC] = _colT(mod[0:D])
    cf[:, C2_SCALE1:C2_SCALE1 + KC] = _colT(mod[D:2 * D])
    cf[:, C2_G1:C2_G1 + KC] = _colT(inp["norm_mix_g"][0])
    cf[:, C2_SHIFT2:C2_SHIFT2 + KC] = _colT(mod[3 * D:4 * D])
    cf[:, C2_SCALE2:C2_SCALE2 + KC] = _colT(mod[4 * D:5 * D])
    cf[:, C2_G2:C2_G2 + KC] = _colT(inp["norm_ffn_g"][0])
    cf[:, C2_BGA:C2_BGA + KC] = _colT(b_in[LY_END:LY_END + D])
    cf[:, C2_BGL:C2_BGL + KC] = _colT(b_in[GA_END:GA_END + D])
    g1rep = np.ascontiguousarray(np.broadcast_to(mod[None, 2 * D:3 * D], (128, D)))
    g2rep = np.ascontiguousarray(np.broadcast_to(mod[None, 5 * D:6 * D], (128, D)))
    maps = []
    for r in range(NCORE):
        ts = slice(r * TOK2, (r + 1) * TOK2)
        al = np.empty((KC * 128, TOK2), NPBF)
        for kc in range(16):
            al[kc * 128:(kc + 1) * 128] = al_all[kc // 2, (kc % 2) * 128:(kc % 2 + 1) * 128, ts]
            al[(16 + kc) * 128:(17 + kc) * 128] = al_all[kc // 2, (2 + kc % 2) * 128:(3 + kc % 2) * 128, ts]
        m = {"x": np.ascontiguousarray(x[ts]), "cf": cf, "g1rep": g1rep, "g2rep": g2rep, "alT": al}
        m.update(shared)
        maps.append(m)
    return maps


def prep_fused_shared(inp):
    sh = prep_l2_shared(inp)
    w_in = inp["w_in"][0]
    b_in = inp["b_in"][0]
    cfm = np.zeros((128, 2 + 16 * CF_UW), np.float32)
    cfm[:, 0] = inp["q_norm_g"][0]
    cfm[:, 1] = inp["k_norm_g"][0]
    bvrep = np.zeros((128, 16 * 129), np.float32)
    wslab = np.zeros((16 * 128, KC * WCOLS), np.float32)
    wax = np.zeros((128, 16 * 256), np.float32)
    for h in range(16):
        hs = slice(h * 128, (h + 1) * 128)
        base = 2 + h * CF_UW
        cols = [b_in[hs], b_in[Q_END + h * 128:Q_END + (h + 1) * 128],
                b_in[F_END + h * 128:F_END + (h + 1) * 128], b_in[LX_END + h * 128:LX_END + (h + 1) * 128],
                inp["conv_w"][0][0, hs], inp["conv_w"][0][1, hs], inp["conv_w"][0][2, hs], inp["conv_w"][0][3, hs],
                inp["conv_b"][0][hs], inp["lru_ba"][0][hs], inp["lru_bx"][0][hs], inp["lru_lambda"][0][hs]]
        for k, cvec in enumerate(cols):
            cfm[:, base + k] = cvec
        bvrep[:, h * 129:h * 129 + 128] = b_in[None, K_END + h * 128:K_END + (h + 1) * 128]
        bvrep[:, h * 129 + 128] = b_in[V_END + h]
        wcat = np.concatenate([w_in[:, hs], w_in[:, Q_END + h * 128:Q_END + (h + 1) * 128],
                               w_in[:, F_END + h * 128:F_END + (h + 1) * 128], w_in[:, LX_END + h * 128:LX_END + (h + 1) * 128],
                               w_in[:, K_END + h * 128:K_END + (h + 1) * 128], w_in[:, V_END + h:V_END + h + 1]], axis=1)
        wslab[h * 128:(h + 1) * 128] = wcat.reshape(KC, 128, WCOLS).transpose(1, 0, 2).reshape(128, KC * WCOLS)
        wax[:, h * 256:h * 256 + 128] = inp["lru_wa"][0][h]
        wax[:, h * 256 + 128:h * 256 + 256] = inp["lru_wx"][0][h]
    cf = np.zeros((128, C2_N), np.float32)
    cf[:, C2_G1:C2_G1 + KC] = _colT(inp["norm_mix_g"][0])
    cf[:, C2_G2:C2_G2 + KC] = _colT(inp["norm_ffn_g"][0])
    cf[:, C2_BGA:C2_BGA + KC] = _colT(b_in[LY_END:LY_END + D])
    cf[:, C2_BGL:C2_BGL + KC] = _colT(b_in[GA_END:GA_END + D])
    c1 = l1_consts()
    sh.update({"cfm": cfm, "bvrep": bvrep, "wslab": wslab, "wax": wax, "cf": cf,
               "cT": _colT(inp["c"][0]), "w_ada": np.ascontiguousarray(inp["w_ada"][0]), "b_ada": np.ascontiguousarray(inp["b_ada"][0][None]),
               "cmask": c1["cmask"], "identf": c1["identf"], "tri": c1["tri"], "ones": c1["ones"]})
    return sh


def prep_fused_core(inp, shared, nreal_tiles, ntiles=NT):
    x = inp["x"][0]
    pad = ntiles - nreal_tiles
    xpad = np.zeros((ntiles * 256, D), np.float32)
    xpad[pad * 256:] = x[0:nreal_tiles * 256]
    tmask = np.zeros((128, ntiles), np.float32)
    tmask[:, pad:] = 1.0
    m = {"xpad": xpad, "tmask": tmask}
    m.update(shared)
    return m


_CACHE = {}


def kernel(**inputs):
    inp = {k: np.asarray(v) for k, v in inputs.items()}
    cores = list(range(NCORE))
    if "fused" not in _CACHE:
        _CACHE["fused"] = build_l2(2, 16, fz=dict(nunits=16, ntiles=NT))
    shared = prep_fused_shared(inp)
    maps = [prep_fused_core(inp, shared, 4 * (r + 1)) for r in cores]
    res = run_bass_kernel_spmd(_CACHE["fused"], maps, core_ids=cores)
    y = np.concatenate([np.asarray(r["y"]) for r in res.results], axis=0)
    return y[None].astype(np.float32)
```

```python
import numpy as np
import ml_dtypes
import concourse.bass as bass
import concourse.mybir as mybir
from concourse.bass_utils import run_bass_kernel_spmd

F32, BF16 = mybir.dt.float32, mybir.dt.bfloat16
ALU = mybir.AluOpType
AF = mybir.ActivationFunctionType
NPBF = ml_dtypes.bfloat16

D = 4096
S = 8192
NCORE = 8
KC = D // 128
EPS = 1e-6


class Res:
    __slots__ = ("name", "w", "r", "dsem", "dcnt", "pend")

    def __init__(self, name):
        self.name = name
        self.w = {}
        self.r = {}
        self.dsem = None
        self.dcnt = 0
        self.pend = None


class KB:
    def __init__(self, nc):
        self.nc = nc
        self.eng = dict(pe=nc.tensor, act=nc.scalar, dve=nc.vector, pool=nc.gpsimd, sp=nc.sync)
        self.psem = {}
        self.pcnt = {}
        for k in ("pe", "act", "dve", "pool"):
            self.psem[k] = self.newsem("prog_" + k)
            self.pcnt[k] = 0
        self.waited = {}
        self.pend = {k: [] for k in self.psem}
        self.nres = 0

    def newsem(self, name):
        return self.nc.semaphore(name).__enter__()

    def res(self, name=None):
        self.nres += 1
        return Res(name or "r%d" % self.nres)

    def sb(self, name, shape, dt):
        return self.nc.sbuf_tensor("s_" + name, shape, dt).__enter__()

    def ps(self, name, shape, dt):
        return self.nc.psum_tensor("p_" + name, shape, dt).__enter__()

    def _wait(self, e, t):
        sem, val, key = t
        if key == "pe" and e == "pe":
            return
        if self.waited.get((e, key), 0) >= val:
            return
        self.waited[(e, key)] = val
        self.eng[e].wait_ge(sem, val)

    def _deps(self, e, reads, writes, skipkey=None):
        for r in reads:
            assert r.pend is None or r.pend == e, (r.name, r.pend, e)
            for k, t in r.w.items():
                if k != skipkey:
                    self._wait(e, t)
        for w in writes:
            assert w.pend is None or w.pend == e, (w.name, w.pend, e)
            for k, t in w.w.items():
                if k != skipkey:
                    self._wait(e, t)
            for k, t in w.r.items():
                self._wait(e, t)

    @staticmethod
    def _assign(t, reads, writes, part=False):
        key = t[2]
        for w in writes:
            if part:
                w.w[key] = t
            else:
                w.w = {key: t}
                w.r = {}
            w.pend = None
        for r in reads:
            r.r[key] = t
            r.pend = None

    def op(self, e, fn, reads=(), writes=(), sig=True):
        self._deps(e, reads, writes)
        ins = fn(self.eng[e])
        if sig:
            self.pcnt[e] += 1
            ins.then_inc(self.psem[e], 1)
            t = (self.psem[e], self.pcnt[e], e)
            for (rs, ws) in self.pend[e]:
                self._assign(t, rs, ws)
            self.pend[e] = []
            self._assign(t, reads, writes)
        else:
            self.pend[e].append((reads, writes))
            for x in list(reads) + list(writes):
                x.pend = e
        return ins

    def dma(self, q, out, in_, sres, reads=(), writes=(), part=False, **kw):
        key = "d_" + sres.name
        self._deps(q, reads, writes, skipkey=key if part else None)
        ins = self.eng[q].dma_start(out=out, in_=in_, **kw)
        if sres.dsem is None:
            sres.dsem = self.newsem("dma_" + sres.name)
        sres.dcnt += 16
        ins.then_inc(sres.dsem, 16)
        t = (sres.dsem, sres.dcnt, key)
        self._assign(t, reads, writes, part=part)
        return ins

    def finish(self, resources, e="sp"):
        for r in resources:
            for t in list(r.w.values()) + list(r.r.values()):
                self._wait(e, t)


def _new_nc():
    return bass.Bass("TRN2", target_bir_lowering=False)


L0_COLS = 6 * D // NCORE


def build_l0():
    nc = _new_nc()
    kb = KB(nc)
    cT = nc.dram_tensor("cT", [128, KC], F32, kind="ExternalInput").ap()
    w = nc.dram_tensor("w", [D, L0_COLS], F32, kind="ExternalInput").ap()
    b = nc.dram_tensor("b", [1, L0_COLS], F32, kind="ExternalInput").ap()
    mod = nc.dram_tensor("mod", [1, L0_COLS], F32, kind="ExternalOutput").ap()
    NB = L0_COLS // 512
    ct = kb.sb("ct", [128, KC], F32)
    sc = kb.sb("sc", [128, KC], F32)
    bt = kb.sb("bt", [1, L0_COLS], F32)
    ot = kb.sb("ot", [1, L0_COLS], F32)
    G = 4
    wt = [kb.sb("wt%d" % i, [128, G, L0_COLS], F32) for i in range(2)]
    r_ct, r_sc, r_bt, r_ot, r_mod = kb.res("ct"), kb.res("sc"), kb.res("bt"), kb.res("ot"), kb.res("mod")
    r_wt = [kb.res("wt0"), kb.res("wt1")]
    pss = [kb.ps("ps%d" % i, [128, 512], F32) for i in range(NB)]
    r_ps = [kb.res("ps%d" % i) for i in range(NB)]
    kb.dma("sp", ct[:], cT[:, :], r_ct, writes=[r_ct])
    kb.dma("sp", bt[:], b[:, :], r_bt, writes=[r_bt])
    kb.op("act", lambda e: e.activation(out=sc[:], in_=ct[:], func=AF.Silu), reads=[r_ct], writes=[r_sc])
    wv = w.rearrange("(k p) n -> p k n", p=128)
    ngrp = KC // G

    def load(g):
        kb.dma("sp", wt[g % 2][:], wv[:, g * G:(g + 1) * G, :], r_wt[g % 2], writes=[r_wt[g % 2]])

    load(0)
    for g in range(ngrp):
        if g + 1 < ngrp:
            load(g + 1)
        for kk in range(G):
            kc = g * G + kk
            for n in range(NB):
                last = (kc == KC - 1)
                kb.op("pe", lambda e, kc=kc, kk=kk, n=n, g=g: e.matmul(
                    pss[n][0:1, :], lhsT=sc[:, kc:kc + 1], rhs=wt[g % 2][:, kk, n * 512:(n + 1) * 512],
                    start=(kc == 0), stop=(kc == KC - 1)),
                    reads=[r_sc, r_wt[g % 2]], writes=[r_ps[n]],
                    sig=(last or (kk == G - 1 and n == NB - 1)))
    for n in range(NB):
        kb.op("dve", lambda e, n=n: e.tensor_tensor(out=ot[0:1, n * 512:(n + 1) * 512], in0=pss[n][0:1, :],
                                                    in1=bt[0:1, n * 512:(n + 1) * 512], op=ALU.add),
              reads=[r_ps[n], r_bt], writes=[r_ot])
    kb.dma("sp", mod[:, :], ot[:], r_ot, reads=[r_ot], writes=[r_mod], part=True)
    kb.finish([r_mod])
    return nc


TT = 256
NT = S // TT
NBLK = S // 128
WCOLS = 641
CF_SCALE1, CF_SHIFT1, CF_G = 0, 32, 64
CF_GQ, CF_GK = 96, 97
CF_U = 98
CF_UW = 12
CF_N = CF_U + 2 * CF_UW


def build_l1(ntiles=NT):
    nc = _new_nc()
    kb = KB(nc)
    x = nc.dram_tensor("x", [S, D], F32, kind="ExternalInput").ap()
    cf_d = nc.dram_tensor("cf", [128, CF_N], F32, kind="ExternalInput").ap()
    bv_d = nc.dram_tensor("bvrep", [128, 2 * 129], F32, kind="ExternalInput").ap()
    w_d = nc.dram_tensor("wslab", [2 * 128, KC * WCOLS], F32, kind="ExternalInput").ap()
    wax_d = nc.dram_tensor("wax", [128, 2 * 2 * 128], F32, kind="ExternalInput").ap()
    id_d = nc.dram_tensor("ident", [128, 128], BF16, kind="ExternalInput").ap()
    cm_d = nc.dram_tensor("cmask", [128, 128], F32, kind="ExternalInput").ap()
    idf_d = nc.dram_tensor("identf", [128, 128], F32, kind="ExternalInput").ap()
    tri_d = nc.dram_tensor("tri", [128, 128], F32, kind="ExternalInput").ap()
    on_d = nc.dram_tensor("ones", [128, 2 * 128], F32, kind="ExternalInput").ap()
    alT = nc.dram_tensor("alT", [4 * 128, S], BF16, kind="ExternalOutput").ap()
    r_out = kb.res("alT")

    cf = kb.sb("cf", [128, CF_N], F32); r_cf = kb.res("cf")
    bv = kb.sb("bv", [128, 2 * 129], F32); r_bv = kb.res("bv")
    ident = kb.sb("ident", [128, 128], BF16); r_id = kb.res("ident")
    maskneg = kb.sb("cmask", [128, 128], F32); r_cm = kb.res("cmask")
    cmask = maskneg
    identf = kb.sb("identf", [128, 128], F32); r_idf = kb.res("identf")
    dgn = kb.sb("dgn", [128, 128], F32); r_dgn = kb.res("dgn")
    Dm = kb.sb("Dm", [128, 128], F32); r_Dm = kb.res("Dm")
    sdg = kb.sb("sdg", [128, 128], F32); r_sdg = kb.res("sdg")
    osb = kb.sb("osb", [128, 129], F32); r_osb = kb.res("osb")
    tri = kb.sb("tri", [128, 128], F32); r_tri = kb.res("tri")
    ones = kb.sb("ones", [128, 256], F32); r_on = kb.res("ones")
    wax = kb.sb("wax", [128, 512], BF16); r_wax = kb.res("wax")
    kb.dma("sp", cf[:], cf_d[:, :], r_cf, writes=[r_cf])
    kb.dma("sp", bv[:], bv_d[:, :], r_bv, writes=[r_bv])
    kb.dma("sp", ident[:], id_d[:, :], r_id, writes=[r_id])
    kb.dma("sp", cmask[:], cm_d[:, :], r_cm, writes=[r_cm])
    kb.dma("sp", identf[:], idf_d[:, :], r_idf, writes=[r_idf])
    kb.dma("sp", tri[:], tri_d[:, :], r_tri, writes=[r_tri])
    kb.dma("sp", ones[:], on_d[:, :], r_on, writes=[r_on])
    kb.dma("pool", wax[:], wax_d[:, :], r_wax, writes=[r_wax])
    A1 = kb.sb("A1", [128, KC], F32); r_A1 = kb.res("A1")
    kb.op("dve", lambda e: e.scalar_tensor_tensor(out=A1[:], in0=cf[:, CF_SCALE1:CF_SCALE1 + KC], scalar=1.0,
                                                 in1=cf[:, CF_G:CF_G + KC], op0=ALU.add, op1=ALU.mult),
          reads=[r_cf], writes=[r_A1])
    c8 = kb.sb("c8", [128, 8], F32); r_c8 = kb.res("c8")
    for u in range(2):
        lam = cf[:, CF_U + u * CF_UW + 11:CF_U + u * CF_UW + 12]
        kb.op("act", lambda e, u=u, lam=lam: e.activation(out=c8[:, 4 + u:5 + u], in_=lam, func=AF.Exp, scale=-1.0),
              reads=[r_cf], writes=[r_c8])
        kb.op("act", lambda e, u=u: e.activation(out=c8[:, 6 + u:7 + u], in_=c8[:, 4 + u:5 + u], func=AF.Ln, bias=1.0, scale=1.0),
              reads=[r_c8], writes=[r_c8])
        kb.op("dve", lambda e, u=u: e.tensor_scalar(out=c8[:, u:u + 1], in0=c8[:, 6 + u:7 + u], scalar1=-8.0, scalar2=None, op0=ALU.mult),
              reads=[r_c8], writes=[r_c8])
        kb.op("dve", lambda e, u=u: e.tensor_scalar(out=c8[:, 2 + u:3 + u], in0=c8[:, 6 + u:7 + u], scalar1=-16.0, scalar2=None, op0=ALU.mult),
              reads=[r_c8], writes=[r_c8])

    wsl = kb.sb("wsl", [128, KC, WCOLS], BF16); r_wsl = kb.res("wsl")
    xb = [kb.sb("xb%d" % i, [128, 2, D], BF16) for i in range(2)]
    r_xb = [kb.res("xb0"), kb.res("xb1")]
    junk = kb.sb("junk", [128, D], BF16); r_junk = kb.res("junk")
    hT = kb.sb("hT", [128, KC, TT], BF16); r_hT = [kb.res("hT0"), kb.res("hT1")]
    KT = kb.sb("KT", [128, S], BF16); r_KT = kb.res("KT")
    V = kb.sb("V", [128, NBLK, 129], BF16); r_V = kb.res("V")
    QT = kb.sb("QT", [128, TT], BF16); r_QT = kb.res("QT")
    NFT = kb.sb("NFT", [128, NBLK], F32); r_NFT = kb.res("NFT")
    NR = kb.sb("NR", [128, NBLK + 1], F32); r_NR = kb.res("NR")
    st = kb.sb("st", [128, 8], F32); r_st = kb.res("st")
    fl = kb.sb("fl", [128, 12], F32); r_fl = kb.res("fl")
    qf = kb.sb("qf", [128, TT], F32); r_qf = kb.res("qf")
    qs = kb.sb("qs", [128, TT], F32); r_qs = kb.res("qs")
    qr = qs; r_qr = r_qs
    Bj = [kb.sb("Bj%d" % i, [128, NBLK], F32) for i in range(2)]; r_Bj = [kb.res("Bj0"), kb.res("Bj1")]
    NPT = 4
    PT = [kb.sb("PT%d" % i, [128, 128], BF16) for i in range(NPT)]; r_PT = [kb.res("PT%d" % i) for i in range(NPT)]
    rec = kb.sb("rec", [128, 2], F32); r_rec = kb.res("rec")
    ao = kb.sb("ao", [128, 128], BF16); r_ao = kb.res("ao")
    aoT = [kb.sb("aoT%d" % i, [128, TT], BF16) for i in range(2)]; r_aoT = [kb.res("aoT0"), kb.res("aoT1")]
    lx = [kb.sb("lx%d" % i, [128, TT + 3], F32) for i in range(2)]; r_lx = [kb.res("lx0"), kb.res("lx1")]
    xc = kb.sb("xc", [128, TT], F32); r_xc = kb.res("xc")
    xcb = kb.sb("xcb", [128, TT], BF16); r_xcb = kb.res("xcb")
    rg = kb.sb("rg", [128, TT], F32); r_rg = kb.res("rg")
    ig = kb.sb("ig", [128, TT], F32); r_ig = kb.res("ig")
    aa = kb.sb("aa", [128, TT], F32); r_aa = kb.res("aa")
    a2 = kb.sb("a2", [128, TT], F32); r_a2 = kb.res("a2")
    hh = [kb.sb("hh%d" % i, [128, TT], F32) for i in range(2)]; r_hh = [kb.res("hh0"), kb.res("hh1")]
    gy = kb.sb("gy", [128, TT], F32); r_gy = kb.res("gy")
    lo = [kb.sb("lo%d" % i, [128, TT], BF16) for i in range(2)]; r_lo = [kb.res("lo0"), kb.res("lo1")]
    zero1 = kb.sb("zero1", [128, 1], F32); r_z = kb.res("zero1")
    kb.op("dve", lambda e: e.memset(zero1[:], 0.0), writes=[r_z])

    ps_tr = [kb.ps("ps_tr%d" % i, [128, 1024], BF16) for i in range(2)]; r_ptr = [kb.res("ptr0"), kb.res("ptr1")]
    ps_mm = [kb.ps("ps_mm%d" % i, [128, 512], F32) for i in range(2)]; r_pmm = [kb.res("pmm0"), kb.res("pmm1")]
    ps_s = [kb.ps("ps_s%d" % i, [128, 512], F32) for i in range(2)]
    r_pss = [kb.res("pss%d" % i) for i in range(2)]
    ps_o = [kb.ps("ps_o%d" % i, [128, 512], F32) for i in range(2)]; r_po = [kb.res("po0"), kb.res("po1")]

    xv = x.rearrange("(s b p) d -> s p b d", b=2, p=128)
    SCALE = float(128 ** -0.5)
    cnt = dict(mm=0, tr=0, ev=0, s=0, pt=0, o=0, bj=0, aot=0, lo=0)

    def load_x(gs, sidx):
        kb.dma("pool", xb[gs % 2][:], xv[sidx], r_xb[gs % 2], writes=[r_xb[gs % 2]])

    for u in range(2):
        ub = CF_U + u * CF_UW
        col = lambda k: cf[:, ub + k:ub + k + 1]
        for part in range(4):
            kb.dma("pool", wsl[:, part * 8:(part + 1) * 8, :],
                   w_d[u * 128:(u + 1) * 128, part * 8 * WCOLS:(part + 1) * 8 * WCOLS].rearrange("p (k c) -> p k c", c=WCOLS),
                   r_wsl, writes=[r_wsl], part=(part > 0))
        kb.op("dve", lambda e: e.memset(NR[:, 0:1], 0.0), writes=[r_NR])
        kb.op("dve", lambda e: e.memset(V[:, :, 128:129], 1.0), writes=[r_V])
        kb.op("dve", lambda e: e.memset(lx[0][:, 0:3], 0.0), writes=[r_lx[0]])
        load_x(u * ntiles, 0)
        hprev = None
        for T in range(ntiles):
            for hf in range(1):
                sidx = T
                gs = u * ntiles + sidx
                if sidx + 1 < ntiles:
                    load_x(gs + 1, sidx + 1)
                kb.op("dve", lambda e: e.memset(st[:, 0:2], 0.0), writes=[r_st])
                xt = xb[gs % 2]; rx = r_xb[gs % 2]
                for b in range(2):
                    kb.op("act", lambda e, b=b, xt=xt: e.activation(out=junk[:], in_=xt[:, b, :], func=AF.Square,
                                                                  scale=1.0 / 64.0, accum_out=st[:, b:b + 1]),
                          reads=[rx], writes=[r_junk, r_st])
                kb.op("act", lambda e: e.activation(out=st[:, 2:4], in_=st[:, 0:2], func=AF.Sqrt, bias=EPS, scale=1.0),
                      reads=[r_st], writes=[r_st])
                kb.op("dve", lambda e: e.reciprocal(out=st[:, 4:6], in_=st[:, 2:4]), reads=[r_st], writes=[r_st])
                for b in range(2):
                    kb.op("dve", lambda e, b=b, xt=xt: e.tensor_scalar(out=xt[:, b, :], in0=xt[:, b, :], scalar1=st[:, 4 + b:5 + b],
                                                                     scalar2=None, op0=ALU.mult),
                          reads=[r_st, rx], writes=[rx])
                for g in range(KC // 4):
                    pt = cnt["tr"] % 2; cnt["tr"] += 1
                    for k4 in range(4):
                        kc = g * 4 + k4
                        for b in range(2):
                            kb.op("pe", lambda e, kc=kc, k4=k4, b=b, xt=xt, pt=pt: e.transpose(
                                out=ps_tr[pt][:, k4 * 256 + b * 128:k4 * 256 + (b + 1) * 128],
                                in_=xt[:, b, kc * 128:(kc + 1) * 128], identity=ident[:]),
                                reads=[rx, r_id], writes=[r_ptr[pt]], sig=(k4 == 3 and b == 1))
                    for k4 in range(4):
                        kc = g * 4 + k4
                        dst = hT[:, kc, :]
                        src = ps_tr[pt][:, k4 * 256:(k4 + 1) * 256]
                        if cnt["ev"] % 2 == 0:
                            kb.op("act", lambda e, dst=dst, src=src, kc=kc: e.activation(
                                out=dst, in_=src, func=AF.Identity, scale=A1[:, kc:kc + 1], bias=cf[:, CF_SHIFT1 + kc:CF_SHIFT1 + kc + 1]),
                                reads=[r_ptr[pt], r_A1, r_cf], writes=[r_hT[hf]])
                        else:
                            kb.op("dve", lambda e, dst=dst, src=src, kc=kc: e.tensor_scalar(
                                out=dst, in0=src, scalar1=A1[:, kc:kc + 1], scalar2=cf[:, CF_SHIFT1 + kc:CF_SHIFT1 + kc + 1],
                                op0=ALU.mult, op1=ALU.add),
                                reads=[r_ptr[pt], r_A1, r_cf], writes=[r_hT[hf]])
                        cnt["ev"] += 1

            def proj(si):
                pm = cnt["mm"] % 2; cnt["mm"] += 1
                for kc in range(KC):
                    kb.op("pe", lambda e, kc=kc, pm=pm: e.matmul(ps_mm[pm][:, 0:TT], lhsT=wsl[:, kc, si * 128:(si + 1) * 128], rhs=hT[:, kc, :],
                                                                 start=(kc == 0), stop=(kc == KC - 1)),
                          reads=[r_wsl, r_hT[0]], writes=[r_pmm[pm]], sig=(kc == KC - 1))
                return pm

            def qknorm(pm, bcol, gcol, dst, r_dst):
                kb.op("act", lambda e: e.activation(out=qf[:], in_=ps_mm[pm][:, 0:TT], func=AF.Identity, bias=bcol, scale=1.0),
                      reads=[r_pmm[pm], r_cf], writes=[r_qf])
                kb.op("act", lambda e: e.activation(out=qs[:], in_=qf[:], func=AF.Square), reads=[r_qf], writes=[r_qs])
                px = cnt["mm"] % 2; cnt["mm"] += 1
                kb.op("pe", lambda e: e.matmul(ps_mm[px][:, 0:TT], lhsT=ones[:, 0:128], rhs=qs[:], start=True, stop=True),
                      reads=[r_on, r_qs], writes=[r_pmm[px]])
                kb.op("act", lambda e: e.activation(out=qr[:], in_=ps_mm[px][:, 0:TT], func=AF.Sqrt, bias=EPS, scale=1.0),
                      reads=[r_pmm[px]], writes=[r_qr])
                kb.op("dve", lambda e: e.reciprocal(out=qr[:], in_=qr[:]), reads=[r_qr], writes=[r_qr])
                kb.op("dve", lambda e: e.scalar_tensor_tensor(out=dst, in0=qf[:], scalar=gcol, in1=qr[:], op0=ALU.mult, op1=ALU.mult),
                      reads=[r_qf, r_qr, r_cf], writes=[r_dst])

            pm = proj(0)
            qknorm(pm, col(0), cf[:, CF_GQ:CF_GQ + 1], QT[:], r_QT)
            pm = proj(1)
            qknorm(pm, col(1), cf[:, CF_GK:CF_GK + 1], KT[:, T * TT:(T + 1) * TT], r_KT)
            for b in range(2):
                blk = T * 2 + b
                pm = cnt["mm"] % 2; cnt["mm"] += 1
                for kc in range(KC):
                    kb.op("pe", lambda e, kc=kc, pm=pm, b=b: e.matmul(ps_mm[pm][:, 0:129], lhsT=hT[:, kc, b * 128:(b + 1) * 128],
                                                                      rhs=wsl[:, kc, 512:641], start=(kc == 0), stop=(kc == KC - 1)),
                          reads=[r_wsl, r_hT[0]], writes=[r_pmm[pm]], sig=(kc == KC - 1))
                kb.op("dve", lambda e, pm=pm, blk=blk: e.tensor_tensor(out=V[:, blk, 0:128], in0=ps_mm[pm][:, 0:128],
                                                                       in1=bv[:, u * 129:u * 129 + 128], op=ALU.add),
                      reads=[r_pmm[pm], r_bv], writes=[r_V])
                kb.op("dve", lambda e, pm=pm, b=b: e.tensor_tensor(out=fl[:, b:b + 1], in0=ps_mm[pm][:, 128:129],
                                                                   in1=bv[:, u * 129 + 128:u * 129 + 129], op=ALU.add),
                      reads=[r_pmm[pm], r_bv], writes=[r_fl])
            kb.op("act", lambda e: e.activation(out=fl[:, 4:6], in_=fl[:, 0:2], func=AF.Exp, scale=-1.0), reads=[r_fl], writes=[r_fl])
            kb.op("act", lambda e: e.activation(out=fl[:, 8:10], in_=fl[:, 4:6], func=AF.Ln, bias=1.0, scale=1.0), reads=[r_fl], writes=[r_fl])
            pxc = cnt["mm"] % 2; cnt["mm"] += 1
            ps_x = ps_mm[pxc]; r_px = r_pmm[pxc]
            kb.op("pe", lambda e: e.matmul(ps_x[:, 0:2], lhsT=tri[:], rhs=fl[:, 8:10], start=True, stop=True),
                  reads=[r_tri, r_fl], writes=[r_px], sig=False)
            kb.op("pe", lambda e: e.matmul(ps_x[:, 4:6], lhsT=ones[:, 128:256], rhs=fl[:, 8:10], start=True, stop=True),
                  reads=[r_on, r_fl], writes=[r_px])
            for b in range(2):
                blk = T * 2 + b
                kb.op("dve", lambda e, b=b, blk=blk: e.tensor_tensor(out=NR[:, blk + 1:blk + 2], in0=NR[:, blk:blk + 1],
                                                                     in1=ps_x[:, 4 + b:5 + b], op=ALU.add),
                      reads=[r_px, r_NR], writes=[r_NR])
            kb.op("dve", lambda e: e.tensor_tensor(out=NFT[:, T * 2:T * 2 + 2], in0=ps_x[:, 0:2], in1=NR[:, T * 2:T * 2 + 2], op=ALU.add),
                  reads=[r_px, r_NR], writes=[r_NFT])

            lxi = T % 2
            pm = proj(2)
            kb.op("act", lambda e, pm=pm: e.activation(out=lx[lxi][:, 3:TT + 3], in_=ps_mm[pm][:, 0:TT], func=AF.Identity, bias=col(2), scale=1.0),
                  reads=[r_pmm[pm], r_cf], writes=[r_lx[lxi]])
            pm = proj(3)
            kb.op("act", lambda e, pm=pm: e.activation(out=gy[:], in_=ps_mm[pm][:, 0:TT], func=AF.Gelu, bias=col(3), scale=1.0),
                  reads=[r_pmm[pm], r_cf], writes=[r_gy])

            groups = []
            for jq in range(2):
                j = T * 2 + jq
                ks = list(range(j))
                for g0 in range(0, len(ks), 4):
                    groups.append((jq, ks[g0:g0 + 4]))
            bj_of = {}
            sb_of = {}

            def emit_s(n):
                jq, ks = groups[n]
                j = T * 2 + jq
                if ks[0] == 0:
                    bi = cnt["bj"] % 2; cnt["bj"] += 1
                    bj_of[jq] = bi
                    kb.op("dve", lambda e, bi=bi, j=j: e.tensor_scalar(out=Bj[bi][:, 0:j], in0=NFT[:, 0:j], scalar1=NR[:, j:j + 1],
                                                                      scalar2=None, op0=ALU.subtract),
                          reads=[r_NFT, r_NR], writes=[r_Bj[bi]])
                sbk = cnt["s"] % 2; cnt["s"] += 1
                sb_of[n] = sbk
                for kk, i in enumerate(ks):
                    kb.op("pe", lambda e, sbk=sbk, kk=kk, i=i, jq=jq: e.matmul(ps_s[sbk][:, kk * 128:(kk + 1) * 128],
                                                                              lhsT=KT[:, i * 128:(i + 1) * 128], rhs=QT[:, jq * 128:(jq + 1) * 128],
                                                                              start=True, stop=True),
                          reads=[r_KT, r_QT], writes=[r_pss[sbk]], sig=(kk == len(ks) - 1))

            def diag_block(jq):
                j = T * 2 + jq
                oi = jq % 2
                kb.op("dve", lambda e: e.tensor_scalar(out=dgn[:], in0=identf[:], scalar1=NFT[:, j:j + 1], scalar2=None, op0=ALU.mult),
                      reads=[r_idf, r_NFT], writes=[r_dgn])
                px = cnt["mm"] % 2; cnt["mm"] += 1
                kb.op("pe", lambda e: e.matmul(ps_mm[px][:, 0:128], lhsT=ones[:, 128:256], rhs=dgn[:], start=True, stop=True),
                      reads=[r_on, r_dgn], writes=[r_pmm[px]])
                kb.op("dve", lambda e: e.tensor_scalar(out=Dm[:], in0=ps_mm[px][:, 0:128], scalar1=NFT[:, j:j + 1], scalar2=-1.0,
                                                      op0=ALU.subtract, op1=ALU.mult), reads=[r_pmm[px], r_NFT], writes=[r_Dm])
                kb.op("dve", lambda e: e.tensor_tensor(out=Dm[:], in0=Dm[:], in1=maskneg[:], op=ALU.add), reads=[r_Dm, r_cm], writes=[r_Dm])
                sbk = cnt["s"] % 2; cnt["s"] += 1
                kb.op("pe", lambda e: e.matmul(ps_s[sbk][:, 0:128], lhsT=KT[:, j * 128:(j + 1) * 128], rhs=QT[:, jq * 128:(jq + 1) * 128],
                                               start=True, stop=True), reads=[r_KT, r_QT], writes=[r_pss[sbk]])
                kb.op("dve", lambda e: e.scalar_tensor_tensor(out=sdg[:], in0=ps_s[sbk][:, 0:128], scalar=SCALE, in1=Dm[:], op0=ALU.mult, op1=ALU.add),
                      reads=[r_pss[sbk], r_Dm], writes=[r_sdg])
                pi = cnt["pt"] % NPT; cnt["pt"] += 1
                kb.op("act", lambda e: e.activation(out=PT[pi][:], in_=sdg[:], func=AF.Exp), reads=[r_sdg], writes=[r_PT[pi]])
                kb.op("pe", lambda e: e.matmul(ps_o[oi][:, 256:385], lhsT=PT[pi][:], rhs=V[:, j, :], start=True, stop=True),
                      reads=[r_PT[pi], r_V], writes=[r_po[oi]])
                if j > 0:
                    kb.op("act", lambda e: e.activation(out=rec[:, 1:2], in_=NFT[:, j:j + 1], func=AF.Exp, scale=-1.0, bias=NR[:, j:j + 1]),
                          reads=[r_NFT, r_NR], writes=[r_rec])
                    kb.op("act", lambda e: e.activation(out=osb[:], in_=ps_o[oi][:, 256:385], func=AF.Identity), reads=[r_po[oi]], writes=[r_osb])
                    kb.op("dve", lambda e: e.scalar_tensor_tensor(out=osb[:], in0=ps_o[oi][:, 0:129], scalar=rec[:, 1:2], in1=osb[:],
                                                                 op0=ALU.mult, op1=ALU.add), reads=[r_po[oi], r_rec, r_osb], writes=[r_osb])
                else:
                    kb.op("act", lambda e: e.activation(out=osb[:], in_=ps_o[oi][:, 256:385], func=AF.Identity), reads=[r_po[oi]], writes=[r_osb])
                ai = T % 2
                kb.op("dve", lambda e: e.reciprocal(out=rec[:, 0:1], in_=osb[:, 128:129]), reads=[r_osb], writes=[r_rec])
                kb.op("dve", lambda e: e.tensor_scalar(out=ao[:], in0=osb[:, 0:128], scalar1=rec[:, 0:1], scalar2=None, op0=ALU.mult),
                      reads=[r_osb, r_rec], writes=[r_ao])
                pt = cnt["tr"] % 2; cnt["tr"] += 1
                kb.op("pe", lambda e, pt=pt: e.transpose(out=ps_tr[pt][:, 0:128], in_=ao[:], identity=ident[:]),
                      reads=[r_ao, r_id], writes=[r_ptr[pt]])
                kb.op("dve", lambda e, pt=pt: e.tensor_copy(out=aoT[ai][:, jq * 128:(jq + 1) * 128], in_=ps_tr[pt][:, 0:128]),
                      reads=[r_ptr[pt]], writes=[r_aoT[ai]])

            if groups:
                emit_s(0)
            done_diag = set()
            for n, (jq, ks) in enumerate(groups):
                j = T * 2 + jq
                if n + 1 < len(groups):
                    emit_s(n + 1)
                sbk = sb_of.pop(n)
                bi = bj_of[jq]
                oi = jq % 2
                for kk, i in enumerate(ks):
                    pi = cnt["pt"] % NPT; cnt["pt"] += 1
                    kb.op("act", lambda e, sbk=sbk, kk=kk, pi=pi, bi=bi, i=i: e.activation(
                        out=PT[pi][:], in_=ps_s[sbk][:, kk * 128:(kk + 1) * 128], func=AF.Exp,
                        scale=SCALE, bias=Bj[bi][:, i:i + 1]),
                        reads=[r_pss[sbk], r_Bj[bi]], writes=[r_PT[pi]])
                    kb.op("pe", lambda e, pi=pi, i=i, oi=oi, j=j: e.matmul(ps_o[oi][:, 0:129], lhsT=PT[pi][:], rhs=V[:, i, :],
                                                                            start=(i == 0), stop=(i == j - 1)),
                          reads=[r_PT[pi], r_V], writes=[r_po[oi]])
                    if i == j - 1:
                        diag_block(jq)
                        done_diag.add(jq)
            for jq in range(2):
                if jq not in done_diag:
                    diag_block(jq)
            ai = T % 2
            kb.dma("sp", alT[u * 128:(u + 1) * 128, T * TT:(T + 1) * TT], aoT[ai][:], r_aoT[ai], reads=[r_aoT[ai]], writes=[r_out], part=True)

            lxt = lx[lxi]
            kb.op("dve", lambda e: e.tensor_scalar(out=xc[:], in0=lxt[:, 0:TT], scalar1=col(4), scalar2=col(8), op0=ALU.mult, op1=ALU.add),
                  reads=[r_lx[lxi], r_cf], writes=[r_xc])
            for k in range(1, 4):
                kb.op("dve", lambda e, k=k: e.scalar_tensor_tensor(out=xc[:], in0=lxt[:, k:k + TT], scalar=col(4 + k), in1=xc[:],
                                                                  op0=ALU.mult, op1=ALU.add),
                      reads=[r_lx[lxi], r_cf, r_xc], writes=[r_xc])
            kb.op("pool", lambda e: e.tensor_copy(out=lx[1 - lxi][:, 0:3], in_=lxt[:, TT:TT + 3]), reads=[r_lx[lxi]], writes=[r_lx[1 - lxi]])
            kb.op("act", lambda e: e.activation(out=xcb[:], in_=xc[:], func=AF.Identity), reads=[r_xc], writes=[r_xcb])
            pmr = cnt["mm"] % 2; cnt["mm"] += 1
            kb.op("pe", lambda e: e.matmul(ps_mm[pmr][:, 0:TT], lhsT=wax[:, (u * 2) * 128:(u * 2 + 1) * 128], rhs=xcb[:], start=True, stop=True),
                  reads=[r_wax, r_xcb], writes=[r_pmm[pmr]])
            kb.op("act", lambda e: e.activation(out=rg[:], in_=ps_mm[pmr][:, 0:TT], func=AF.Sigmoid, bias=col(9), scale=1.0),
                  reads=[r_pmm[pmr], r_cf], writes=[r_rg])
            pmi = cnt["mm"] % 2; cnt["mm"] += 1
            kb.op("pe", lambda e: e.matmul(ps_mm[pmi][:, 0:TT], lhsT=wax[:, (u * 2 + 1) * 128:(u * 2 + 2) * 128], rhs=xcb[:], start=True, stop=True),
                  reads=[r_wax, r_xcb], writes=[r_pmm[pmi]])
            kb.op("act", lambda e: e.activation(out=ig[:], in_=ps_mm[pmi][:, 0:TT], func=AF.Sigmoid, bias=col(10), scale=1.0),
                  reads=[r_pmm[pmi], r_cf], writes=[r_ig])
            kb.op("act", lambda e: e.activation(out=aa[:], in_=rg[:], func=AF.Exp, scale=c8[:, u:u + 1]), reads=[r_rg, r_c8], writes=[r_aa])
            kb.op("act", lambda e: e.activation(out=a2[:], in_=rg[:], func=AF.Exp, scale=c8[:, 2 + u:3 + u]), reads=[r_rg, r_c8], writes=[r_a2])
            kb.op("dve", lambda e: e.tensor_scalar(out=a2[:], in0=a2[:], scalar1=-1.0, scalar2=1.0, op0=ALU.mult, op1=ALU.add),
                  reads=[r_a2], writes=[r_a2])
            kb.op("act", lambda e: e.activation(out=a2[:], in_=a2[:], func=AF.Sqrt), reads=[r_a2], writes=[r_a2])
            kb.op("dve", lambda e: e.tensor_tensor(out=ig[:], in0=ig[:], in1=xc[:], op=ALU.mult), reads=[r_ig, r_xc], writes=[r_ig])
            kb.op("dve", lambda e: e.tensor_tensor(out=ig[:], in0=ig[:], in1=a2[:], op=ALU.mult), reads=[r_ig, r_a2], writes=[r_ig])
            hi = T % 2
            init = zero1[:, 0:1] if T == 0 else hh[1 - hi][:, TT - 1:TT]
            kb.op("dve", lambda e, init=init: e.tensor_tensor_scan(out=hh[hi][:], data0=aa[:], data1=ig[:], initial=init, op0=ALU.mult, op1=ALU.add),
                  reads=[r_aa, r_ig, r_z, r_hh[1 - hi]], writes=[r_hh[hi]])
            li = cnt["lo"] % 2; cnt["lo"] += 1
            kb.op("dve", lambda e, li=li: e.tensor_tensor(out=lo[li][:], in0=hh[hi][:], in1=gy[:], op=ALU.mult),
                  reads=[r_hh[hi], r_gy], writes=[r_lo[li]])
            kb.dma("sp", alT[(2 + u) * 128:(3 + u) * 128, T * TT:(T + 1) * TT], lo[li][:], r_lo[li], reads=[r_lo[li]], writes=[r_out], part=True)
    kb.finish([r_out])
    return nc


Q_END, K_END, V_END = 2048, 4096, 6144
F_END = V_END + 16
LX_END = F_END + 2048
LY_END = LX_END + 2048
GA_END = LY_END + D


def _colT(v):
    return np.ascontiguousarray(np.asarray(v, np.float32).reshape(-1, 128).T)


def prep_l0(inp):
    cT = _colT(inp["c"][0])
    w = inp["w_ada"][0]
    b = inp["b_ada"][0]
    maps = []
    for r in range(NCORE):
        sl = slice(r * L0_COLS, (r + 1) * L0_COLS)
        maps.append({"cT": cT, "w": np.ascontiguousarray(w[:, sl]), "b": np.ascontiguousarray(b[None, sl])})
    return maps


def l1_consts():
    tri = np.triu(np.ones((128, 128), np.float32))
    return {
        "ident": np.eye(128, dtype=np.float32).astype(NPBF),
        "cmask": ((1.0 - tri) * np.float32(-1.0e9)).astype(np.float32),
        "identf": np.eye(128, dtype=np.float32),
        "tri": tri,
        "ones": np.concatenate([np.full((128, 128), 1.0 / 128, np.float32), np.ones((128, 128), np.float32)], axis=1),
    }


def prep_l1(inp, mod):
    x = np.ascontiguousarray(inp["x"][0])
    w_in = inp["w_in"][0]
    b_in = inp["b_in"][0]
    consts = l1_consts()
    maps = []
    for r in range(NCORE):
        cf = np.zeros((128, CF_N), np.float32)
        cf[:, CF_SCALE1:CF_SCALE1 + KC] = _colT(mod[D:2 * D])
        cf[:, CF_SHIFT1:CF_SHIFT1 + KC] = _colT(mod[0:D])
        cf[:, CF_G:CF_G + KC] = _colT(inp["norm_mix_g"][0])
        cf[:, CF_GQ] = inp["q_norm_g"][0]
        cf[:, CF_GK] = inp["k_norm_g"][0]
        bvrep = np.zeros((128, 258), np.float32)
        wslab = np.zeros((256, KC * WCOLS), np.float32)
        wax = np.zeros((128, 512), np.float32)
        for u in range(2):
            h = 2 * r + u
            hs = slice(h * 128, (h + 1) * 128)
            base = CF_U + u * CF_UW
            cols = [b_in[hs], b_in[Q_END + h * 128:Q_END + (h + 1) * 128],
                    b_in[F_END + h * 128:F_END + (h + 1) * 128], b_in[LX_END + h * 128:LX_END + (h + 1) * 128],
                    inp["conv_w"][0][0, hs], inp["conv_w"][0][1, hs], inp["conv_w"][0][2, hs], inp["conv_w"][0][3, hs],
                    inp["conv_b"][0][hs], inp["lru_ba"][0][hs], inp["lru_bx"][0][hs], inp["lru_lambda"][0][hs]]
            for k, cvec in enumerate(cols):
                cf[:, base + k] = cvec
            bvrep[:, u * 129:u * 129 + 128] = b_in[None, K_END + h * 128:K_END + (h + 1) * 128]
            bvrep[:, u * 129 + 128] = b_in[V_END + h]
            wcat = np.concatenate([w_in[:, hs], w_in[:, Q_END + h * 128:Q_END + (h + 1) * 128],
                                   w_in[:, F_END + h * 128:F_END + (h + 1) * 128], w_in[:, LX_END + h * 128:LX_END + (h + 1) * 128],
                                   w_in[:, K_END + h * 128:K_END + (h + 1) * 128], w_in[:, V_END + h:V_END + h + 1]], axis=1)
            wslab[u * 128:(u + 1) * 128] = wcat.reshape(KC, 128, WCOLS).transpose(1, 0, 2).reshape(128, KC * WCOLS)
            wax[:, u * 256:u * 256 + 128] = inp["lru_wa"][0][h]
            wax[:, u * 256 + 128:u * 256 + 256] = inp["lru_wx"][0][h]
        m = {"x": x, "cf": cf, "bvrep": bvrep, "wslab": wslab, "wax": wax}
        m.update(consts)
        maps.append(m)
    return maps


TOK2 = S // NCORE
C2_SCALE1, C2_SHIFT1, C2_G1, C2_SCALE2, C2_SHIFT2, C2_G2, C2_BGA, C2_BGL = 0, 32, 64, 96, 128, 160, 192, 224
C2_N = 256
NEG = -1.0e30


def build_l2(nhalf=2, nchunk=16, fz=None):
    from contextlib import ExitStack
    nc = _new_nc()
    kb = KB(nc)
    uid = [0]

    def sbt(stack, name, shape, dt):
        uid[0] += 1
        return stack.enter_context(nc.sbuf_tensor("s_%s_%d" % (name, uid[0]), shape, dt))

    def pst(stack, name, shape, dt):
        uid[0] += 1
        return stack.enter_context(nc.psum_tensor("p_%s_%d" % (name, uid[0]), shape, dt))

    def barrier():
        for k in kb.psem:
            assert not kb.pend[k]
        tickets = [(kb.psem[k], kb.pcnt[k], k) for k in kb.psem if kb.pcnt[k] > 0]
        tickets += [(r.dsem, r.dcnt, "d_" + r.name) for r in allres if r.dsem is not None]
        for e in ("pe", "act", "dve", "pool", "sp"):
            for t in tickets:
                kb._wait(e, t)

    allres = []

    def res(name):
        r = kb.res(name + "_%d" % len(allres))
        allres.append(r)
        return r

    r_al = res("al_d")
    r_mod = res("mod_s")
    cf_d = nc.dram_tensor("cf", [128, C2_N], F32, kind="ExternalInput").ap()
    if fz is None:
        x_d = nc.dram_tensor("x", [TOK2, D], F32, kind="ExternalInput").ap()
        g1_d = nc.dram_tensor("g1rep", [128, D], F32, kind="ExternalInput").ap()
        g2_d = nc.dram_tensor("g2rep", [128, D], F32, kind="ExternalInput").ap()
        al_d = nc.dram_tensor("alT", [KC * 128, TOK2], BF16, kind="ExternalInput").ap()
    else:
        NTL = fz["ntiles"]; NOWN = 4; NU = fz["nunits"]
        SP = NTL * 256
        xp_d = nc.dram_tensor("xpad", [SP, D], F32, kind="ExternalInput").ap()
        x_d = xp_d[SP - TOK2:SP, :]
        cT_d = nc.dram_tensor("cT", [128, KC], F32, kind="ExternalInput").ap()
        wada_d = nc.dram_tensor("w_ada", [D, 6 * D], F32, kind="ExternalInput").ap()
        bada_d = nc.dram_tensor("b_ada", [1, 6 * D], F32, kind="ExternalInput").ap()
        mod_s = nc.dram_tensor("mod_s", [1, 6 * D], F32, kind=("ExternalOutput" if fz.get("dbg") else "Internal")).ap()
        g1_d = mod_s[0:1, 2 * D:3 * D].broadcast_to([128, D])
        g2_d = mod_s[0:1, 5 * D:6 * D].broadcast_to([128, D])
        al_d = nc.dram_tensor("alT", [KC * 128, TOK2], BF16, kind=("ExternalOutput" if fz.get("dbg") else "Internal")).ap()
        hT_s = nc.dram_tensor("hT_s", [NTL * 128, KC * 256], BF16, kind="Internal").ap()
        r_hTs = [res("hTs%d" % t) for t in range(NTL)]
        cfm_d = nc.dram_tensor("cfm", [128, 2 + 16 * CF_UW], F32, kind="ExternalInput").ap()
        bvm_d = nc.dram_tensor("bvrep", [128, 16 * 129], F32, kind="ExternalInput").ap()
        wsl_d = nc.dram_tensor("wslab", [16 * 128, KC * WCOLS], F32, kind="ExternalInput").ap()
        wax_d = nc.dram_tensor("wax", [128, 16 * 256], F32, kind="ExternalInput").ap()
        tm_d = nc.dram_tensor("tmask", [128, NTL], F32, kind="ExternalInput").ap()
        cm_d = nc.dram_tensor("cmask", [128, 128], F32, kind="ExternalInput").ap()
        idf_d = nc.dram_tensor("identf", [128, 128], F32, kind="ExternalInput").ap()
        tri_d = nc.dram_tensor("tri", [128, 128], F32, kind="ExternalInput").ap()
        on_d = nc.dram_tensor("ones", [128, 256], F32, kind="ExternalInput").ap()
    wcat_d = nc.dram_tensor("wcat", [KC * 128, 96 * 128], F32, kind="ExternalInput").ap()
    wout_d = nc.dram_tensor("wout", [8 * 128, KC * 512], F32, kind="ExternalInput").ap()
    wq_d = nc.dram_tensor("wq", [16 * 128, KC * 128], F32, kind="ExternalInput").ap()
    sk_d = nc.dram_tensor("skT", [128, 16 * 128], F32, kind="ExternalInput").ap()
    uT_d = nc.dram_tensor("uT", [128 * 128, KC * 128], F32, kind="ExternalInput").ap()
    v_d = nc.dram_tensor("v", [128 * 128, D], F32, kind="ExternalInput").ap()
    id_d = nc.dram_tensor("ident", [128, 128], BF16, kind="ExternalInput").ap()
    y_d = nc.dram_tensor("y", [TOK2, D], F32, kind="ExternalOutput").ap()
    r_y = [[res("y%d_%d" % (b, n)) for n in range(8)] for b in range(8)]

    top = ExitStack()
    cf = sbt(top, "cf", [128, C2_N], F32); r_cf = res("cf")
    ident = sbt(top, "ident", [128, 128], BF16); r_id = res("ident")
    grep = sbt(top, "grep", [128, D], F32); r_grep = res("grep")
    AB = sbt(top, "AB", [128, 2 * KC], F32); r_AB = res("AB")
    skT = sbt(top, "skT", [128, 16 * 128], BF16); r_sk = res("skT")
    st = sbt(top, "st", [128, 8], F32); r_st = res("st")
    kb.dma("sp", cf[:], cf_d[:, :], r_cf, writes=[r_cf])
    kb.dma("sp", ident[:], id_d[:, :], r_id, writes=[r_id])
    kb.dma("pool", skT[:], sk_d[:, :], r_sk, writes=[r_sk])
    if fz is not None:
        with ExitStack() as sm_:
            ct = sbt(sm_, "ct", [128, KC], F32); r_ct = res("ct")
            scv = sbt(sm_, "scv", [128, KC], F32); r_scv = res("scv")
            bt = sbt(sm_, "bt", [1, D], F32); r_bt = res("bt")
            ot = sbt(sm_, "otm", [1, D], F32); r_otm = res("otm")
            wt = [sbt(sm_, "wt%d" % i, [128, 2, D], F32) for i in range(2)]; r_wt = [res("wt0"), res("wt1")]
            pss = [pst(sm_, "pm%d" % i, [128, 512], F32) for i in range(8)]; r_pss_ = [res("pm%d" % i) for i in range(8)]
            kb.dma("sp", ct[:], cT_d[:, :], r_ct, writes=[r_ct])
            kb.op("act", lambda e: e.activation(out=scv[:], in_=ct[:], func=AF.Silu), reads=[r_ct], writes=[r_scv])
            wv_ = wada_d.rearrange("(k p) n -> p k n", p=128)
            seqm = [(ps_, g) for ps_ in range(6) for g in range(KC // 2)]

            def load_m(i):
                ps_, g = seqm[i]
                kb.dma("sp", wt[i % 2][:], wv_[:, g * 2:(g + 1) * 2, ps_ * D:(ps_ + 1) * D], r_wt[i % 2], writes=[r_wt[i % 2]])

            load_m(0)
            for i, (ps_, g) in enumerate(seqm):
                if i + 1 < len(seqm):
                    load_m(i + 1)
                if g == 0:
                    kb.dma("sp", bt[:], bada_d[:, ps_ * D:(ps_ + 1) * D], r_bt, writes=[r_bt])
                for kk in range(2):
                    kc = g * 2 + kk
                    for n in range(8):
                        kb.op("pe", lambda e, kc=kc, kk=kk, n=n, i=i: e.matmul(pss[n][0:1, :], lhsT=scv[:, kc:kc + 1], rhs=wt[i % 2][:, kk, n * 512:(n + 1) * 512],
                                                                              start=(kc == 0), stop=(kc == KC - 1)),
                              reads=[r_scv, r_wt[i % 2]], writes=[r_pss_[n]], sig=(kc == KC - 1 or (kk == 1 and n == 7)))
                if g == KC // 2 - 1:
                    for n in range(8):
                        kb.op("dve", lambda e, n=n: e.tensor_tensor(out=ot[0:1, n * 512:(n + 1) * 512], in0=pss[n][0:1, :], in1=bt[0:1, n * 512:(n + 1) * 512], op=ALU.add),
                              reads=[r_pss_[n], r_bt], writes=[r_otm])
                    kb.dma("sp", mod_s[:, ps_ * D:(ps_ + 1) * D], ot[:], r_otm, reads=[r_otm], writes=[r_mod], part=True)
            barrier()
        modcol = mod_s.rearrange("o (k p) -> p (o k)", p=128)
        for sec, c0 in ((0, C2_SHIFT1), (1, C2_SCALE1), (3, C2_SHIFT2), (4, C2_SCALE2)):
            kb.dma("sp", cf[:, c0:c0 + KC], modcol[:, sec * KC:(sec + 1) * KC], r_cf, reads=[r_mod], writes=[r_cf], part=True,
                   allow_slow_non_contiguous=True)
    kb.op("dve", lambda e: e.scalar_tensor_tensor(out=AB[:, 0:KC], in0=cf[:, C2_SCALE1:C2_SCALE1 + KC], scalar=1.0,
                                                 in1=cf[:, C2_G1:C2_G1 + KC], op0=ALU.add, op1=ALU.mult), reads=[r_cf], writes=[r_AB])
    kb.op("dve", lambda e: e.scalar_tensor_tensor(out=AB[:, KC:2 * KC], in0=cf[:, C2_SCALE2:C2_SCALE2 + KC], scalar=1.0,
                                                 in1=cf[:, C2_G2:C2_G2 + KC], op0=ALU.add, op1=ALU.mult), reads=[r_cf], writes=[r_AB])
    P = {}
    cnt = dict(tr=0, ev=0, w=0, wo=0, xo=0, mm=0, u=0, vs=0, ga=0, gt=0, act=0, out=0, ot=0, ce=0)

    def norm_T(xb, rx, junk, r_junk, dst_fn, r_dst, acol, bcol):
        ps_tr, r_ptr = P["tr"], P["rtr"]
        kb.op("dve", lambda e: e.memset(st[:, 0:2], 0.0), writes=[r_st])
        for b in range(2):
            kb.op("act", lambda e, b=b: e.activation(out=junk[:], in_=xb[:, b, :], func=AF.Square, scale=1.0 / 64.0,
                                                    accum_out=st[:, b:b + 1]), reads=[rx], writes=[r_junk, r_st])
        kb.op("act", lambda e: e.activation(out=st[:, 2:4], in_=st[:, 0:2], func=AF.Sqrt, bias=EPS, scale=1.0), reads=[r_st], writes=[r_st])
        kb.op("dve", lambda e: e.reciprocal(out=st[:, 4:6], in_=st[:, 2:4]), reads=[r_st], writes=[r_st])
        for b in range(2):
            kb.op("dve", lambda e, b=b: e.tensor_scalar(out=xb[:, b, :], in0=xb[:, b, :], scalar1=st[:, 4 + b:5 + b], scalar2=None,
                                                       op0=ALU.mult), reads=[r_st, rx], writes=[rx])
        for g in range(KC // 4):
            pt = cnt["tr"] % 2; cnt["tr"] += 1
            for k4 in range(4):
                kc = g * 4 + k4
                for b in range(2):
                    kb.op("pe", lambda e, kc=kc, k4=k4, b=b, pt=pt: e.transpose(
                        out=ps_tr[pt][:, k4 * 256 + b * 128:k4 * 256 + (b + 1) * 128], in_=xb[:, b, kc * 128:(kc + 1) * 128], identity=ident[:]),
                        reads=[rx, r_id], writes=[r_ptr[pt]], sig=(k4 == 3 and b == 1))
            for k4 in range(4):
                kc = g * 4 + k4
                dst = dst_fn(kc)
                src = ps_tr[pt][:, k4 * 256:(k4 + 1) * 256]
                if cnt["ev"] % 2 == 0:
                    kb.op("act", lambda e, dst=dst, src=src, kc=kc: e.activation(out=dst, in_=src, func=AF.Identity,
                                                                                scale=AB[:, acol + kc:acol + kc + 1], bias=cf[:, bcol + kc:bcol + kc + 1]),
                          reads=[r_ptr[pt], r_AB, r_cf], writes=[r_dst])
                else:
                    kb.op("dve", lambda e, dst=dst, src=src, kc=kc: e.tensor_scalar(out=dst, in0=src, scalar1=AB[:, acol + kc:acol + kc + 1],
                                                                                   scalar2=cf[:, bcol + kc:bcol + kc + 1], op0=ALU.mult, op1=ALU.add),
                          reads=[r_ptr[pt], r_AB, r_cf], writes=[r_dst])
                cnt["ev"] += 1

    xv = x_d.rearrange("(s b p) d -> s p b d", b=2, p=128)
    yv = y_d.rearrange("(s b p) d -> s p b d", b=2, p=128)
    alv = al_d.rearrange("(k p) t -> p k t", p=128)
    vv = v_d.rearrange("(i j) d -> j i d", j=128)

    if fz is not None:
        with ExitStack() as s0:
            xbs = [sbt(s0, "xb%d" % i, [128, 2, D], BF16) for i in range(2)]; r_xbs = [res("xb0"), res("xb1")]
            junk = sbt(s0, "junk", [128, D], BF16); r_junk = res("junk")
            hTt = [sbt(s0, "hTt%d" % i, [128, KC, 256], BF16) for i in range(2)]; r_hTt = [res("hTt0"), res("hTt1")]
            P["tr"] = [pst(s0, "tr%d" % i, [128, 1024], BF16) for i in range(2)]; P["rtr"] = [res("ptr0"), res("ptr1")]
            xpv = xp_d.rearrange("(s b p) d -> s p b d", b=2, p=128)
            kb.dma("pool", xbs[0][:], xpv[0], r_xbs[0], writes=[r_xbs[0]])
            for T in range(NTL):
                if T + 1 < NTL:
                    kb.dma("pool", xbs[(T + 1) % 2][:], xpv[T + 1], r_xbs[(T + 1) % 2], writes=[r_xbs[(T + 1) % 2]])
                hb = hTt[T % 2]
                norm_T(xbs[T % 2], r_xbs[T % 2], junk, r_junk, lambda kc, hb=hb: hb[:, kc, :], r_hTt[T % 2], 0, C2_SHIFT1)
                kb.dma("sp", hT_s[T * 128:(T + 1) * 128, :].rearrange("p (k t) -> p k t", t=256), hb[:], r_hTt[T % 2],
                       reads=[r_hTt[T % 2]], writes=[r_hTs[T]], part=True)
            barrier()
        with ExitStack() as s1:
            TTm = 256
            cfm = sbt(s1, "cfm", [128, 2 + 16 * CF_UW], F32); r_cfm = res("cfm")
            bv = sbt(s1, "bv", [128, 16 * 129], F32); r_bv = res("bv")
            tmask = sbt(s1, "tmask", [128, NTL], F32); r_tm = res("tmask")
            maskneg = sbt(s1, "maskneg", [128, 128], F32); r_cm = res("maskneg")
            identf = sbt(s1, "identf", [128, 128], F32); r_idf = res("identf")
            tri = sbt(s1, "tri", [128, 128], F32); r_tri = res("tri")
            ones = sbt(s1, "ones", [128, 256], F32); r_on = res("ones")
            c8 = sbt(s1, "c8", [128, 4], F32); r_c8 = res("c8")
            nbias = sbt(s1, "nbias", [128, 2], F32); r_nbias = res("nbias")
            wax = sbt(s1, "wax", [128, 256], BF16); r_wax = res("wax")
            wsl = sbt(s1, "wsl", [128, KC, WCOLS], BF16); r_wsl5 = [res("wsl%d" % i) for i in range(5)]
            hTt = [sbt(s1, "hTm%d" % i, [128, KC, 256], BF16) for i in range(2)]; r_hTt = [res("hTm0"), res("hTm1")]
            KT = sbt(s1, "KT", [128, NTL * 256], BF16); r_KT = res("KT")
            V = sbt(s1, "V", [128, NTL * 2, 129], BF16); r_V = res("V")
            QT = sbt(s1, "QT", [128, 256], BF16); r_QT = res("QT")
            NFT = sbt(s1, "NFT", [128, NTL * 2], F32); r_NFT = res("NFT")
            NR = sbt(s1, "NR", [128, NTL * 2 + 1], F32); r_NR = res("NR")
            fl = sbt(s1, "fl", [128, 12], F32); r_fl = res("fl")
            qf = sbt(s1, "qf", [128, 256], F32); r_qf = res("qf")
            qs = sbt(s1, "qs", [128, 256], F32); r_qs = res("qs")
            Bj = [sbt(s1, "Bj%d" % i, [128, NTL * 2], F32) for i in range(2)]; r_Bj = [res("Bj0"), res("Bj1")]
            NPT = 4
            PT = [sbt(s1, "PT%d" % i, [128, 128], BF16) for i in range(NPT)]; r_PT = [res("PT%d" % i) for i in range(NPT)]
            rec = sbt(s1, "rec", [128, 2], F32); r_rec = res("rec")
            ao = sbt(s1, "ao", [128, 128], BF16); r_ao = res("ao")
            aoT = [sbt(s1, "aoT%d" % i, [128, 256], BF16) for i in range(2)]; r_aoT = [res("aoT0"), res("aoT1")]
            dgn = sbt(s1, "dgn", [128, 128], F32); r_dgn = res("dgn")
            Dm = sbt(s1, "Dm", [128, 128], F32); r_Dm = res("Dm")
            sdg = sbt(s1, "sdg", [128, 128], F32); r_sdg = res("sdg")
            osb = sbt(s1, "osb", [128, 129], F32); r_osb = res("osb")
            lx = [sbt(s1, "lx%d" % i, [128, 256 + 3], F32) for i in range(2)]; r_lx = [res("lx0"), res("lx1")]
            xc = sbt(s1, "xc", [128, 256], F32); r_xc = res("xc")
            xcb = sbt(s1, "xcb", [128, 256], BF16); r_xcb = res("xcb")
            rg = sbt(s1, "rg", [128, 256], F32); r_rg = res("rg")
            ig = sbt(s1, "ig", [128, 256], F32); r_ig = res("ig")
            aa = sbt(s1, "aa", [128, 256], F32); r_aa = res("aa")
            a2 = sbt(s1, "a2", [128, 256], F32); r_a2 = res("a2")
            hh = [sbt(s1, "hh%d" % i, [128, 256], F32) for i in range(2)]; r_hh = [res("hh0"), res("hh1")]
            gy = sbt(s1, "gy", [128, 256], F32); r_gy = res("gy")
            lo = [sbt(s1, "lo%d" % i, [128, 256], BF16) for i in range(2)]; r_lo = [res("lo0"), res("lo1")]
            zero1 = sbt(s1, "zero1", [128, 1], F32); r_z = res("zero1")
            ps_tr = [pst(s1, "tr%d" % i, [128, 1024], BF16) for i in range(1)]; r_ptr = [res("ptr0")]
            ps_mm = [pst(s1, "mm%d" % i, [128, 512], F32) for i in range(3)]; r_pmm = [res("pmm0"), res("pmm1"), res("pmm2")]
            ps_s = [pst(s1, "s%d" % i, [128, 512], F32) for i in range(2)]; r_pss = [res("pss0"), res("pss1")]
            ps_o = [pst(s1, "o%d" % i, [128, 512], F32) for i in range(2)]; r_po = [res("po0"), res("po1")]
            kb.dma("sp", cfm[:], cfm_d[:, :], r_cfm, writes=[r_cfm])
            kb.dma("sp", bv[:], bvm_d[:, :], r_bv, writes=[r_bv])
            kb.dma("sp", tmask[:], tm_d[:, :], r_tm, writes=[r_tm])
            kb.dma("sp", maskneg[:], cm_d[:, :], r_cm, writes=[r_cm])
            kb.dma("sp", identf[:], idf_d[:, :], r_idf, writes=[r_idf])
            kb.dma("sp", tri[:], tri_d[:, :], r_tri, writes=[r_tri])
            kb.dma("sp", ones[:], on_d[:, :], r_on, writes=[r_on])
            kb.op("dve", lambda e: e.memset(zero1[:], 0.0), writes=[r_z])
            SCALE = float(128 ** -0.5)
            mc = dict(mm=0, tr=0, s=0, pt=0, bj=0, lo=0, h=0)
            OWN0 = NTL - NOWN
            for u in range(NU):
                ub = 2 + u * CF_UW
                col = lambda k, ub=ub: cfm[:, ub + k:ub + k + 1]
                wsrc = wsl_d[u * 128:(u + 1) * 128, :].rearrange("p (k c) -> p k c", c=WCOLS)
                for si in (1, 2, 4, 0, 3):
                    c0, c1 = (512, 641) if si == 4 else (si * 128, (si + 1) * 128)
                    kb.dma("pool", wsl[:, :, c0:c1], wsrc[:, :, c0:c1], r_wsl5[si], writes=[r_wsl5[si]])
                kb.dma("pool", wax[:], wax_d[:, u * 256:(u + 1) * 256], r_wax, writes=[r_wax])
                kb.op("act", lambda e: e.activation(out=c8[:, 2:3], in_=col(11), func=AF.Exp, scale=-1.0), reads=[r_cfm], writes=[r_c8])
                kb.op("act", lambda e: e.activation(out=c8[:, 3:4], in_=c8[:, 2:3], func=AF.Ln, bias=1.0, scale=1.0), reads=[r_c8], writes=[r_c8])
                kb.op("dve", lambda e: e.tensor_scalar(out=c8[:, 0:1], in0=c8[:, 3:4], scalar1=-8.0, scalar2=None, op0=ALU.mult), reads=[r_c8], writes=[r_c8])
                kb.op("dve", lambda e: e.tensor_scalar(out=c8[:, 1:2], in0=c8[:, 3:4], scalar1=-16.0, scalar2=None, op0=ALU.mult), reads=[r_c8], writes=[r_c8])
                kb.op("dve", lambda e: e.memset(NR[:, 0:1], 0.0), writes=[r_NR])
                kb.op("dve", lambda e: e.memset(lx[0][:, 0:3], 0.0), writes=[r_lx[0]])
                kb.op("dve", lambda e: e.tensor_scalar(out=nbias[:, 0:2], in0=cfm[:, ub + 9:ub + 11], scalar1=-1.0, scalar2=None, op0=ALU.mult),
                      reads=[r_cfm], writes=[r_nbias])

                def load_h(T):
                    i_ = mc["h"] % 2; mc["h"] += 1
                    kb.dma("sp", hTt[i_][:], hT_s[T * 128:(T + 1) * 128, :].rearrange("p (k t) -> p k t", t=256), r_hTt[i_],
                           reads=[r_hTs[T]], writes=[r_hTt[i_]])
                    return i_

                h_next = load_h(0)
                for T in range(NTL):
                    own = T >= OWN0
                    hi_ = h_next
                    if T + 1 < NTL:
                        h_next = load_h(T + 1)
                    hT = hTt[hi_]; r_hT = r_hTt[hi_]
                    tmc = tmask[:, T:T + 1]

                    def proj(si):
                        pm = mc["mm"] % 3; mc["mm"] += 1
                        for kc in range(KC):
                            kb.op("pe", lambda e, kc=kc, pm=pm: e.matmul(ps_mm[pm][:, 0:256], lhsT=wsl[:, kc, si * 128:(si + 1) * 128], rhs=hT[:, kc, :],
                                                                         start=(kc == 0), stop=(kc == KC - 1)),
                                  reads=[r_wsl5[si], r_hT], writes=[r_pmm[pm]], sig=(kc == KC - 1))
                        return pm

                    def qknorm(pm, bcol, gcol, dst, r_dst):
                        kb.op("act", lambda e: e.activation(out=qf[:], in_=ps_mm[pm][:, 0:256], func=AF.Identity, bias=bcol, scale=1.0),
                              reads=[r_pmm[pm], r_cfm], writes=[r_qf])
                        kb.op("act", lambda e: e.activation(out=qs[:], in_=qf[:], func=AF.Square), reads=[r_qf], writes=[r_qs])
                        px = mc["mm"] % 3; mc["mm"] += 1
                        kb.op("pe", lambda e: e.matmul(ps_mm[px][:, 0:256], lhsT=ones[:, 0:128], rhs=qs[:], start=True, stop=True),
                              reads=[r_on, r_qs], writes=[r_pmm[px]])
                        kb.op("act", lambda e: e.activation(out=qs[:], in_=ps_mm[px][:, 0:256], func=AF.Ln, bias=EPS, scale=1.0),
                              reads=[r_pmm[px]], writes=[r_qs])
                        kb.op("act", lambda e: e.activation(out=qs[:], in_=qs[:], func=AF.Exp, scale=-0.5), reads=[r_qs], writes=[r_qs])
                        kb.op("dve", lambda e: e.scalar_tensor_tensor(out=dst, in0=qf[:], scalar=gcol, in1=qs[:], op0=ALU.mult, op1=ALU.mult),
                              reads=[r_qf, r_qs, r_cfm], writes=[r_dst])

                    pm = proj(1)
                    kb.op("act", lambda e, pm=pm: e.activation(out=qf[:], in_=ps_mm[pm][:, 0:256], func=AF.Identity, bias=col(1), scale=1.0),
                          reads=[r_pmm[pm], r_cfm], writes=[r_qf])
                    kb.op("act", lambda e: e.activation(out=qs[:], in_=qf[:], func=AF.Square), reads=[r_qf], writes=[r_qs])
                    lxi = T % 2
                    lxt = lx[lxi]
                    pm = proj(2)
                    kb.op("act", lambda e, pm=pm: e.activation(out=lx[lxi][:, 3:259], in_=ps_mm[pm][:, 0:256], func=AF.Identity, bias=col(2), scale=1.0),
                          reads=[r_pmm[pm], r_cfm], writes=[r_lx[lxi]])
                    kb.op("dve", lambda e: e.tensor_scalar(out=lx[lxi][:, 3:259], in0=lx[lxi][:, 3:259], scalar1=tmc, scalar2=None, op0=ALU.mult),
                          reads=[r_lx[lxi], r_tm], writes=[r_lx[lxi]])
                    kb.op("dve", lambda e: e.tensor_scalar(out=xc[:], in0=lxt[:, 0:256], scalar1=col(4), scalar2=col(8), op0=ALU.mult, op1=ALU.add),
                          reads=[r_lx[lxi], r_cfm], writes=[r_xc])
                    for k in range(1, 4):
                        kb.op("dve", lambda e, k=k: e.scalar_tensor_tensor(out=xc[:], in0=lxt[:, k:k + 256], scalar=col(4 + k), in1=xc[:],
                                                                          op0=ALU.mult, op1=ALU.add),
                              reads=[r_lx[lxi], r_cfm, r_xc], writes=[r_xc])
                    kb.op("pool", lambda e: e.tensor_copy(out=lx[1 - lxi][:, 0:3], in_=lxt[:, 256:259]), reads=[r_lx[lxi]], writes=[r_lx[1 - lxi]])
                    kb.op("act", lambda e: e.activation(out=xcb[:], in_=xc[:], func=AF.Identity), reads=[r_xc], writes=[r_xcb])
                    for b in range(2):
                        blk = T * 2 + b
                        pm = mc["mm"] % 3; mc["mm"] += 1
                        for kc in range(KC):
                            kb.op("pe", lambda e, kc=kc, pm=pm, b=b: e.matmul(ps_mm[pm][:, 0:129], lhsT=hT[:, kc, b * 128:(b + 1) * 128],
                                                                              rhs=wsl[:, kc, 512:641], start=(kc == 0), stop=(kc == KC - 1)),
                                  reads=[r_wsl5[4], r_hT], writes=[r_pmm[pm]], sig=(kc == KC - 1))
                        kb.op("dve", lambda e, pm=pm, blk=blk: e.tensor_tensor(out=V[:, blk, 0:128], in0=ps_mm[pm][:, 0:128],
                                                                               in1=bv[:, u * 129:u * 129 + 128], op=ALU.add),
                              reads=[r_pmm[pm], r_bv], writes=[r_V])
                        kb.op("dve", lambda e, blk=blk: e.tensor_scalar(out=V[:, blk, 0:128], in0=V[:, blk, 0:128], scalar1=tmc, scalar2=None, op0=ALU.mult),
                              reads=[r_V, r_tm], writes=[r_V])
                        kb.op("dve", lambda e, blk=blk: e.tensor_copy(out=V[:, blk, 128:129], in_=tmc), reads=[r_tm], writes=[r_V])
                        kb.op("dve", lambda e, pm=pm, b=b: e.tensor_tensor(out=fl[:, b:b + 1], in0=ps_mm[pm][:, 128:129],
                                                                           in1=bv[:, u * 129 + 128:u * 129 + 129], op=ALU.add),
                              reads=[r_pmm[pm], r_bv], writes=[r_fl])
                    px = mc["mm"] % 3; mc["mm"] += 1
                    kb.op("pe", lambda e: e.matmul(ps_mm[px][:, 0:256], lhsT=ones[:, 0:128], rhs=qs[:], start=True, stop=True),
                          reads=[r_on, r_qs], writes=[r_pmm[px]])
                    kb.op("act", lambda e: e.activation(out=qs[:], in_=ps_mm[px][:, 0:256], func=AF.Ln, bias=EPS, scale=1.0),
                          reads=[r_pmm[px]], writes=[r_qs])
                    kb.op("act", lambda e: e.activation(out=qs[:], in_=qs[:], func=AF.Exp, scale=-0.5), reads=[r_qs], writes=[r_qs])
                    kb.op("dve", lambda e: e.scalar_tensor_tensor(out=KT[:, T * 256:(T + 1) * 256], in0=qf[:], scalar=cfm[:, 1:2], in1=qs[:], op0=ALU.mult, op1=ALU.mult),
                          reads=[r_qf, r_qs, r_cfm], writes=[r_KT])
                    pmr = mc["mm"] % 3; mc["mm"] += 1
                    kb.op("pe", lambda e: e.matmul(ps_mm[pmr][:, 0:256], lhsT=wax[:, 0:128], rhs=xcb[:], start=True, stop=True),
                          reads=[r_wax, r_xcb], writes=[r_pmm[pmr]])
                    pmi = mc["mm"] % 3; mc["mm"] += 1
                    kb.op("pe", lambda e: e.matmul(ps_mm[pmi][:, 0:256], lhsT=wax[:, 128:256], rhs=xcb[:], start=True, stop=True),
                          reads=[r_wax, r_xcb], writes=[r_pmm[pmi]])
                    kb.op("act", lambda e: e.activation(out=fl[:, 4:6], in_=fl[:, 0:2], func=AF.Exp, scale=-1.0), reads=[r_fl], writes=[r_fl])
                    kb.op("act", lambda e: e.activation(out=fl[:, 8:10], in_=fl[:, 4:6], func=AF.Ln, bias=1.0, scale=1.0), reads=[r_fl], writes=[r_fl])
                    kb.op("act", lambda e: e.activation(out=rg[:], in_=ps_mm[pmr][:, 0:256], func=AF.Exp, bias=nbias[:, 0:1], scale=-1.0),
                          reads=[r_pmm[pmr], r_nbias], writes=[r_rg])
                    kb.op("act", lambda e: e.activation(out=ig[:], in_=ps_mm[pmi][:, 0:256], func=AF.Exp, bias=nbias[:, 1:2], scale=-1.0),
                          reads=[r_pmm[pmi], r_nbias], writes=[r_ig])
                    pxc = mc["mm"] % 3; mc["mm"] += 1
                    ps_x = ps_mm[pxc]; r_px = r_pmm[pxc]
                    kb.op("pe", lambda e: e.matmul(ps_x[:, 0:2], lhsT=tri[:], rhs=fl[:, 8:10], start=True, stop=True),
                          reads=[r_tri, r_fl], writes=[r_px], sig=False)
                    kb.op("pe", lambda e: e.matmul(ps_x[:, 4:6], lhsT=ones[:, 128:256], rhs=fl[:, 8:10], start=True, stop=True),
                          reads=[r_on, r_fl], writes=[r_px])
                    for b in range(2):
                        blk = T * 2 + b
                        kb.op("dve", lambda e, b=b, blk=blk: e.tensor_tensor(out=NR[:, blk + 1:blk + 2], in0=NR[:, blk:blk + 1],
                                                                             in1=ps_x[:, 4 + b:5 + b], op=ALU.add),
                              reads=[r_px, r_NR], writes=[r_NR])
                    kb.op("dve", lambda e: e.tensor_tensor(out=NFT[:, T * 2:T * 2 + 2], in0=ps_x[:, 0:2], in1=NR[:, T * 2:T * 2 + 2], op=ALU.add),
                          reads=[r_px, r_NR], writes=[r_NFT])
                    kb.op("act", lambda e: e.activation(out=rg[:], in_=rg[:], func=AF.Ln, bias=1.0, scale=1.0), reads=[r_rg], writes=[r_rg])
                    kb.op("act", lambda e: e.activation(out=rg[:], in_=rg[:], func=AF.Exp, scale=-1.0), reads=[r_rg], writes=[r_rg])
                    kb.op("act", lambda e: e.activation(out=ig[:], in_=ig[:], func=AF.Ln, bias=1.0, scale=1.0), reads=[r_ig], writes=[r_ig])
                    kb.op("act", lambda e: e.activation(out=ig[:], in_=ig[:], func=AF.Exp, scale=-1.0), reads=[r_ig], writes=[r_ig])
                    kb.op("act", lambda e: e.activation(out=aa[:], in_=rg[:], func=AF.Exp, scale=c8[:, 0:1]), reads=[r_rg, r_c8], writes=[r_aa])
                    kb.op("act", lambda e: e.activation(out=a2[:], in_=rg[:], func=AF.Exp, scale=c8[:, 1:2]), reads=[r_rg, r_c8], writes=[r_a2])
                    kb.op("act", lambda e: e.activation(out=a2[:], in_=a2[:], func=AF.Ln, bias=1.0, scale=-1.0), reads=[r_a2], writes=[r_a2])
                    kb.op("act", lambda e: e.activation(out=a2[:], in_=a2[:], func=AF.Exp, scale=0.5), reads=[r_a2], writes=[r_a2])
                    kb.op("dve", lambda e: e.scalar_tensor_tensor(out=ig[:], in0=ig[:], scalar=tmc, in1=xc[:], op0=ALU.mult, op1=ALU.mult),
                          reads=[r_ig, r_xc, r_tm], writes=[r_ig])
                    kb.op("dve", lambda e: e.tensor_tensor(out=ig[:], in0=ig[:], in1=a2[:], op=ALU.mult), reads=[r_ig, r_a2], writes=[r_ig])
                    hi = T % 2
                    init = zero1[:, 0:1] if T == 0 else hh[1 - hi][:, 255:256]
                    kb.op("dve", lambda e, init=init: e.tensor_tensor_scan(out=hh[hi][:], data0=aa[:], data1=ig[:], initial=init, op0=ALU.mult, op1=ALU.add),
                          reads=[r_aa, r_ig, r_z, r_hh[1 - hi]], writes=[r_hh[hi]])
                    if own:
                        pm = proj(0)
                        qknorm(pm, col(0), cfm[:, 0:1], QT[:], r_QT)
                        pm = proj(3)
                        kb.op("act", lambda e, pm=pm: e.activation(out=gy[:], in_=ps_mm[pm][:, 0:256], func=AF.Gelu, bias=col(3), scale=1.0),
                              reads=[r_pmm[pm], r_cfm], writes=[r_gy])
                        li = mc["lo"] % 2; mc["lo"] += 1
                        kb.op("dve", lambda e, li=li: e.tensor_tensor(out=lo[li][:], in0=hh[hi][:], in1=gy[:], op=ALU.mult),
                              reads=[r_hh[hi], r_gy], writes=[r_lo[li]])
                        tcol = (T - OWN0) * 256
                        kb.dma("sp", al_d[(16 + u) * 128:(17 + u) * 128, tcol:tcol + 256], lo[li][:], r_lo[li], reads=[r_lo[li]], writes=[r_al], part=True)

                    if own:
                        groups = []
                        for jq in range(2):
                            j = T * 2 + jq
                            ks = list(range(j))
                            for g0 in range(0, len(ks), 4):
                                groups.append((jq, ks[g0:g0 + 4]))
                        bj_of = {}
                        sb_of = {}

                        def emit_s(n):
                            jq, ks = groups[n]
                            j = T * 2 + jq
                            if ks[0] == 0:
                                bi = mc["bj"] % 2; mc["bj"] += 1
                                bj_of[jq] = bi
                                kb.op("dve", lambda e, bi=bi, j=j: e.tensor_scalar(out=Bj[bi][:, 0:j], in0=NFT[:, 0:j], scalar1=NR[:, j:j + 1],
                                                                                  scalar2=None, op0=ALU.subtract),
                                      reads=[r_NFT, r_NR], writes=[r_Bj[bi]])
                            sbk = mc["s"] % 2; mc["s"] += 1
                            sb_of[n] = sbk
                            for kk, i in enumerate(ks):
                                kb.op("pe", lambda e, sbk=sbk, kk=kk, i=i, jq=jq: e.matmul(ps_s[sbk][:, kk * 128:(kk + 1) * 128],
                                                                                          lhsT=KT[:, i * 128:(i + 1) * 128], rhs=QT[:, jq * 128:(jq + 1) * 128],
                                                                                          start=True, stop=True),
                                      reads=[r_KT, r_QT], writes=[r_pss[sbk]], sig=(kk == len(ks) - 1))

                        def diag_block(jq):
                            j = T * 2 + jq
                            oi = jq % 2
                            kb.op("dve", lambda e: e.tensor_scalar(out=dgn[:], in0=identf[:], scalar1=NFT[:, j:j + 1], scalar2=None, op0=ALU.mult),
                                  reads=[r_idf, r_NFT], writes=[r_dgn])
                            px = mc["mm"] % 3; mc["mm"] += 1
                            kb.op("pe", lambda e: e.matmul(ps_mm[px][:, 0:128], lhsT=ones[:, 128:256], rhs=dgn[:], start=True, stop=True),
                                  reads=[r_on, r_dgn], writes=[r_pmm[px]])
                            kb.op("dve", lambda e: e.tensor_scalar(out=Dm[:], in0=ps_mm[px][:, 0:128], scalar1=NFT[:, j:j + 1], scalar2=-1.0,
                                                                  op0=ALU.subtract, op1=ALU.mult), reads=[r_pmm[px], r_NFT], writes=[r_Dm])
                            kb.op("dve", lambda e: e.tensor_tensor(out=Dm[:], in0=Dm[:], in1=maskneg[:], op=ALU.add), reads=[r_Dm, r_cm], writes=[r_Dm])
                            sbk = mc["s"] % 2; mc["s"] += 1
                            kb.op("pe", lambda e: e.matmul(ps_s[sbk][:, 0:128], lhsT=KT[:, j * 128:(j + 1) * 128], rhs=QT[:, jq * 128:(jq + 1) * 128],
                                                           start=True, stop=True), reads=[r_KT, r_QT], writes=[r_pss[sbk]])
                            kb.op("dve", lambda e: e.scalar_tensor_tensor(out=sdg[:], in0=ps_s[sbk][:, 0:128], scalar=SCALE, in1=Dm[:], op0=ALU.mult, op1=ALU.add),
                                  reads=[r_pss[sbk], r_Dm], writes=[r_sdg])
                            pi = mc["pt"] % NPT; mc["pt"] += 1
                            kb.op("act", lambda e: e.activation(out=PT[pi][:], in_=sdg[:], func=AF.Exp), reads=[r_sdg], writes=[r_PT[pi]])
                            kb.op("pe", lambda e: e.matmul(ps_o[oi][:, 256:385], lhsT=PT[pi][:], rhs=V[:, j, :], start=True, stop=True),
                                  reads=[r_PT[pi], r_V], writes=[r_po[oi]])
                            if j > 0:
                                kb.op("act", lambda e: e.activation(out=rec[:, 1:2], in_=NFT[:, j:j + 1], func=AF.Exp, scale=-1.0, bias=NR[:, j:j + 1]),
                                      reads=[r_NFT, r_NR], writes=[r_rec])
                                kb.op("act", lambda e: e.activation(out=osb[:], in_=ps_o[oi][:, 256:385], func=AF.Identity), reads=[r_po[oi]], writes=[r_osb])
                                kb.op("dve", lambda e: e.scalar_tensor_tensor(out=osb[:], in0=ps_o[oi][:, 0:129], scalar=rec[:, 1:2], in1=osb[:],
                                                                             op0=ALU.mult, op1=ALU.add), reads=[r_po[oi], r_rec, r_osb], writes=[r_osb])
                            else:
                                kb.op("act", lambda e: e.activation(out=osb[:], in_=ps_o[oi][:, 256:385], func=AF.Identity), reads=[r_po[oi]], writes=[r_osb])
                            ai = T % 2
                            kb.op("dve", lambda e: e.reciprocal(out=rec[:, 0:1], in_=osb[:, 128:129]), reads=[r_osb], writes=[r_rec])
                            kb.op("dve", lambda e: e.tensor_scalar(out=ao[:], in0=osb[:, 0:128], scalar1=rec[:, 0:1], scalar2=None, op0=ALU.mult),
                                  reads=[r_osb, r_rec], writes=[r_ao])
                            pt = 0
                            kb.op("pe", lambda e, pt=pt: e.transpose(out=ps_tr[pt][:, 0:128], in_=ao[:], identity=ident[:]),
                                  reads=[r_ao, r_id], writes=[r_ptr[pt]])
                            kb.op("dve", lambda e, pt=pt: e.tensor_copy(out=aoT[ai][:, jq * 128:(jq + 1) * 128], in_=ps_tr[pt][:, 0:128]),
                                  reads=[r_ptr[pt]], writes=[r_aoT[ai]])

                        if groups:
                            emit_s(0)
                        done_diag = set()
                        for n, (jq, ks) in enumerate(groups):
                            j = T * 2 + jq
                            if n + 1 < len(groups):
                                emit_s(n + 1)
                            sbk = sb_of.pop(n)
                            bi = bj_of[jq]
                            oi = jq % 2
                            for kk, i in enumerate(ks):
                                pi = mc["pt"] % NPT; mc["pt"] += 1
                                kb.op("act", lambda e, sbk=sbk, kk=kk, pi=pi, bi=bi, i=i: e.activation(
                                    out=PT[pi][:], in_=ps_s[sbk][:, kk * 128:(kk + 1) * 128], func=AF.Exp,
                                    scale=SCALE, bias=Bj[bi][:, i:i + 1]),
                                    reads=[r_pss[sbk], r_Bj[bi]], writes=[r_PT[pi]])
                                kb.op("pe", lambda e, pi=pi, i=i, oi=oi, j=j: e.matmul(ps_o[oi][:, 0:129], lhsT=PT[pi][:], rhs=V[:, i, :],
                                                                                        start=(i == 0), stop=(i == j - 1)),
                                      reads=[r_PT[pi], r_V], writes=[r_po[oi]])
                                if i == j - 1:
                                    diag_block(jq)
                                    done_diag.add(jq)
                        for jq in range(2):
                            if jq not in done_diag:
                                diag_block(jq)
                        ai = T % 2
                        tcol = (T - OWN0) * 256
                        kb.dma("sp", al_d[u * 128:(u + 1) * 128, tcol:tcol + 256], aoT[ai][:], r_aoT[ai], reads=[r_aoT[ai]], writes=[r_al], part=True)

            barrier()
    for hf in range(nhalf):
        with ExitStack() as sa:
            xb = sbt(sa, "xb", [128, 2, D], BF16); r_xb = res("xb")
            junk = sbt(sa, "junk", [128, D], BF16); r_junk = res("junk")
            hT = sbt(sa, "hT", [128, KC, 256], BF16); r_hT = res("hT")
            alq = sbt(sa, "alq", [128, KC, 256], BF16); r_alq = res("alq")
            mT = sbt(sa, "mT", [128, KC, 256], BF16); r_mT = res("mT")
            NW = 4
            wp = [sbt(sa, "wp%d" % i, [128, KC, 128], BF16) for i in range(NW)]; r_wp = [res("wp%d" % i) for i in range(NW)]
            wo = [sbt(sa, "wo%d" % i, [128, 8, 512], BF16) for i in range(2)]; r_wo = [res("wo%d" % i) for i in range(2)]
            sg = sbt(sa, "sg", [128, 256], F32); r_sg = res("sg")
            t1 = sbt(sa, "t1", [128, 256], F32); r_t1 = res("t1")
            xt = [sbt(sa, "xt%d" % i, [128, 512], F32) for i in range(2)]; r_xt = [res("xt0"), res("xt1")]
            ot = [sbt(sa, "ot%d" % i, [128, 512], F32) for i in range(2)]; r_ot = [res("ot0"), res("ot1")]
            P["tr"] = [pst(sa, "tr%d" % i, [128, 1024], BF16) for i in range(2)]; P["rtr"] = [res("ptr0"), res("ptr1")]
            pacc = [pst(sa, "acc%d" % i, [128, 512], F32) for i in range(4)]; r_pacc = [res("pacc%d" % i) for i in range(4)]
            pout = [pst(sa, "out%d" % i, [128, 512], F32) for i in range(2)]; r_pout = [res("pout0"), res("pout1")]
            kb.dma("sp", grep[:], g1_d, r_grep, reads=([r_mod] if fz is not None else []), writes=[r_grep])
            for q in range(2):
                qi = hf * 2 + q
                t0 = qi * 256
                kb.dma("pool", xb[:], xv[qi], r_xb, writes=[r_xb])
                kb.dma("sp", alq[:], alv[:, :, t0:t0 + 256], r_alq, reads=[r_al], writes=[r_alq])
                norm_T(xb, r_xb, junk, r_junk, lambda kc: hT[:, kc, :], r_hT, 0, C2_SHIFT1)
                slabs = [(c, m) for c in range(KC) for m in range(3)]

                def load_slab(n):
                    c, m = slabs[n]
                    slot = cnt["w"] % NW; cnt["w"] += 1
                    src = wcat_d[c * 128:(c + 1) * 128, m * 32 * 128:(m + 1) * 32 * 128].rearrange("p (k j) -> p k j", j=128)
                    kb.dma("pool", wp[slot][:], src, r_wp[slot], writes=[r_wp[slot]])
                    return slot

                slot_of = {}
                for n in range(min(3, len(slabs))):
                    slot_of[n] = load_slab(n)
                for n, (c, m) in enumerate(slabs):
                    if n + 3 < len(slabs):
                        slot_of[n + 3] = load_slab(n + 3)
                    sl = slot_of.pop(n)
                    if m < 2:
                        for kc in range(KC):
                            kb.op("pe", lambda e, kc=kc, sl=sl, m=m: e.matmul(pacc[m][:, 0:256], lhsT=wp[sl][:, kc, :], rhs=hT[:, kc, :],
                                                                              start=(kc == 0), stop=(kc == KC - 1)),
                                  reads=[r_wp[sl], r_hT], writes=[r_pacc[m]], sig=(kc == KC - 1))
                    else:
                        for br in range(2):
                            for kk in range(16):
                                kc = br * 16 + kk
                                kb.op("pe", lambda e, kc=kc, kk=kk, sl=sl, br=br: e.matmul(pacc[2 + br][:, 0:256], lhsT=wp[sl][:, kc, :], rhs=alq[:, kc, :],
                                                                                        start=(kk == 0), stop=(kk == 15)),
                                      reads=[r_wp[sl], r_alq], writes=[r_pacc[2 + br]], sig=(kk == 15))
                        kb.op("act", lambda e, c=c: e.activation(out=sg[:], in_=pacc[0][:, 0:256], func=AF.Sigmoid,
                                                                bias=cf[:, C2_BGA + c:C2_BGA + c + 1], scale=1.0), reads=[r_pacc[0], r_cf], writes=[r_sg])
                        kb.op("dve", lambda e: e.tensor_tensor(out=t1[:], in0=sg[:], in1=pacc[2][:, 0:256], op=ALU.mult),
                              reads=[r_sg, r_pacc[2]], writes=[r_t1])
                        kb.op("act", lambda e, c=c: e.activation(out=sg[:], in_=pacc[1][:, 0:256], func=AF.Sigmoid,
                                                                bias=cf[:, C2_BGL + c:C2_BGL + c + 1], scale=1.0), reads=[r_pacc[1], r_cf], writes=[r_sg])
                        kb.op("dve", lambda e: e.tensor_tensor(out=sg[:], in0=sg[:], in1=pacc[3][:, 0:256], op=ALU.mult),
                              reads=[r_sg, r_pacc[3]], writes=[r_sg])
                        kb.op("dve", lambda e, c=c: e.tensor_tensor(out=mT[:, c, :], in0=sg[:], in1=t1[:], op=ALU.add),
                              reads=[r_sg, r_t1], writes=[r_mT])
                def load_wo(n, kg):
                    slot = cnt["wo"] % 2; cnt["wo"] += 1
                    src = wout_d[n * 128:(n + 1) * 128, kg * 8 * 512:(kg + 1) * 8 * 512].rearrange("p (k j) -> p k j", j=512)
                    kb.dma("pool", wo[slot][:], src, r_wo[slot], writes=[r_wo[slot]])
                    return slot

                seq = [(n, kg) for n in range(8) for kg in range(4)]
                nxt = load_wo(*seq[0])
                for idx, (n, kg) in enumerate(seq):
                    sl = nxt
                    if idx + 1 < len(seq):
                        nxt = load_wo(*seq[idx + 1])
                    for b in range(2):
                        for kk in range(8):
                            kc = kg * 8 + kk
                            kb.op("pe", lambda e, b=b, kk=kk, kc=kc, sl=sl: e.matmul(pout[b][:], lhsT=mT[:, kc, b * 128:(b + 1) * 128], rhs=wo[sl][:, kk, :],
                                                                                    start=(kc == 0), stop=(kc == KC - 1)),
                                  reads=[r_mT, r_wo[sl]], writes=[r_pout[b]], sig=(kk == 7))
                    if kg == 3:
                        for b in range(2):
                            blk = qi * 2 + b
                            xs = cnt["xo"] % 2; cnt["xo"] += 1
                            kb.dma("sp", xt[xs][:], x_d[blk * 128:(blk + 1) * 128, n * 512:(n + 1) * 512], r_xt[xs], writes=[r_xt[xs]])
                            kb.op("dve", lambda e, b=b, n=n, xs=xs: e.tensor_tensor(out=ot[xs][:], in0=pout[b][:], in1=grep[:, n * 512:(n + 1) * 512], op=ALU.mult),
                                  reads=[r_pout[b], r_grep], writes=[r_ot[xs]])
                            kb.op("dve", lambda e, xs=xs: e.tensor_tensor(out=ot[xs][:], in0=ot[xs][:], in1=xt[xs][:], op=ALU.add),
                                  reads=[r_ot[xs], r_xt[xs]], writes=[r_ot[xs]])
                            kb.dma("sp", y_d[blk * 128:(blk + 1) * 128, n * 512:(n + 1) * 512], ot[xs][:], r_ot[xs], reads=[r_ot[xs]],
                                   writes=[r_y[blk][n]], part=True)
            barrier()
        with ExitStack() as sb_:
            h2T = sbt(sb_, "h2T", [128, KC, 512], BF16); r_h2T = res("h2T")
            Ssc = sbt(sb_, "Ssc", [128, 4, 16, 128], F32); r_S = res("S")
            tau = sbt(sb_, "tau", [128, 32], F32); r_tau = res("tau")
            nbv = sbt(sb_, "nbv", [128, 32], F32); r_nb = res("nb")
            kb.dma("sp", grep[:], g2_d, r_grep, reads=([r_mod] if fz is not None else []), writes=[r_grep])
            with ExitStack() as sq:
                xb = sbt(sq, "xb", [128, 2, D], BF16); r_xb = res("xb")
                junk = sbt(sq, "junk", [128, D], BF16); r_junk = res("junk")
                qT = sbt(sq, "qT", [128, 16, 512], BF16); r_qT = res("qT")
                wp = [sbt(sq, "wq%d" % i, [128, KC, 128], BF16) for i in range(2)]; r_wp = [res("wq0"), res("wq1")]
                t16 = sbt(sq, "t16", [128, 2, 16], F32); r_t16 = res("t16")
                tmp = sbt(sq, "tmp", [128, 128], F32); r_tmp = res("tmp")
                cand = sbt(sq, "cand", [128, 256], F32); r_cand = res("cand")
                tmp2 = sbt(sq, "tmp2", [128, 256], F32); r_tmp2 = res("tmp2")
                b16 = sbt(sq, "b16", [128, 16], F32); r_b16 = res("b16")
                sm = sbt(sq, "sm", [128, 24], F32); r_sm = res("sm")
                P["tr"] = [pst(sq, "tr%d" % i, [128, 1024], BF16) for i in range(2)]; P["rtr"] = [res("ptr0"), res("ptr1")]
                pq = [pst(sq, "pq%d" % i, [128, 512], F32) for i in range(2)]; r_pq = [res("pq0"), res("pq1")]
                for sub in range(2):
                    qi = hf * 2 + sub
                    rys = [r_y[qi * 2 + b][n] for b in range(2) for n in range(8)]
                    kb.dma("pool", xb[:], yv[qi], r_xb, reads=rys, writes=[r_xb])
                    norm_T(xb, r_xb, junk, r_junk, lambda kc, sub=sub: h2T[:, kc, sub * 256:(sub + 1) * 256], r_h2T, KC, C2_SHIFT2)
                def load_wq(hp):
                    kb.dma("pool", wp[hp % 2][:], wq_d[hp * 128:(hp + 1) * 128, :].rearrange("p (k j) -> p k j", j=128), r_wp[hp % 2], writes=[r_wp[hp % 2]])
                load_wq(0)
                for hp in range(16):
                    if hp + 1 < 16:
                        load_wq(hp + 1)
                    for kc in range(KC):
                        kb.op("pe", lambda e, kc=kc, hp=hp: e.matmul(pq[hp % 2][:], lhsT=wp[hp % 2][:, kc, :], rhs=h2T[:, kc, :], start=(kc == 0), stop=(kc == KC - 1)),
                              reads=[r_wp[hp % 2], r_h2T], writes=[r_pq[hp % 2]], sig=(kc == KC - 1))
                    kb.op("act", lambda e, hp=hp: e.activation(out=qT[:, hp, :], in_=pq[hp % 2][:], func=AF.Identity), reads=[r_pq[hp % 2]], writes=[r_qT])
                for blk in range(4):
                    for g4 in range(4):
                        pb = cnt["mm"] % 2; cnt["mm"] += 1
                        for k in range(4):
                            hp = g4 * 4 + k
                            kb.op("pe", lambda e, hp=hp, k=k, blk=blk, pb=pb: e.matmul(pq[pb][:, k * 128:(k + 1) * 128], lhsT=qT[:, hp, blk * 128:(blk + 1) * 128],
                                                                                       rhs=skT[:, hp * 128:(hp + 1) * 128], start=True, stop=True),
                                  reads=[r_qT, r_sk], writes=[r_pq[pb]], sig=(k == 3))
                        kb.op("act", lambda e, blk=blk, g4=g4, pb=pb: e.activation(out=Ssc[:, blk, g4 * 4:(g4 + 1) * 4, :], in_=pq[pb][:].rearrange("p (a b) -> p a b", b=128),
                                                                                   func=AF.Identity), reads=[r_pq[pb]], writes=[r_S])
                for blk in range(4):
                    for h in range(8):
                        colx = blk * 8 + h
                        for p in range(2):
                            sp_ = Ssc[:, blk, 2 * h + p, :]
                            kb.op("dve", lambda e, p=p, sp_=sp_: e.max(out=t16[:, p, 0:8], in_=sp_), reads=[r_S], writes=[r_t16])
                            kb.op("dve", lambda e, p=p, sp_=sp_: e.match_replace(out=tmp[:], in_to_replace=t16[:, p, 0:8], in_values=sp_, imm_value=NEG),
                                  reads=[r_S, r_t16], writes=[r_tmp])
                            kb.op("dve", lambda e, p=p: e.max(out=t16[:, p, 8:16], in_=tmp[:]), reads=[r_tmp], writes=[r_t16])
                        kb.op("dve", lambda e: e.tensor_tensor(out=cand[:].rearrange("p (a b) -> p a b", b=16),
                                                               in0=t16[:, 0, :].unsqueeze(2).broadcast_to([128, 16, 16]),
                                                               in1=t16[:, 1, :].unsqueeze(1).broadcast_to([128, 16, 16]), op=ALU.add),
                              reads=[r_t16], writes=[r_cand])
                        kb.op("dve", lambda e: e.max(out=b16[:, 0:8], in_=cand[:]), reads=[r_cand], writes=[r_b16])
                        kb.op("dve", lambda e: e.match_replace(out=tmp2[:], in_to_replace=b16[:, 0:8], in_values=cand[:], imm_value=NEG),
                              reads=[r_cand, r_b16], writes=[r_tmp2])
                        kb.op("dve", lambda e: e.max(out=b16[:, 8:16], in_=tmp2[:]), reads=[r_tmp2], writes=[r_b16])
                        kb.op("dve", lambda e, colx=colx: e.tensor_copy(out=tau[:, colx:colx + 1], in_=b16[:, 15:16]), reads=[r_b16], writes=[r_tau])
                        kb.op("dve", lambda e: e.tensor_scalar(out=sm[:, 0:1], in0=b16[:, 0:1], scalar1=-1.0, scalar2=None, op0=ALU.mult), reads=[r_b16], writes=[r_sm])
                        kb.op("dve", lambda e: e.memset(sm[:, 1:2], 0.0), writes=[r_sm])
                        kb.op("act", lambda e: e.activation(out=sm[:, 4:20], in_=b16[:], func=AF.Exp, bias=sm[:, 0:1], scale=1.0, accum_out=sm[:, 1:2]),
                              reads=[r_b16, r_sm], writes=[r_sm])
                        kb.op("act", lambda e: e.activation(out=sm[:, 2:3], in_=sm[:, 1:2], func=AF.Ln), reads=[r_sm], writes=[r_sm])
                        kb.op("dve", lambda e, colx=colx: e.tensor_tensor(out=nbv[:, colx:colx + 1], in0=sm[:, 0:1], in1=sm[:, 2:3], op=ALU.subtract),
                              reads=[r_sm], writes=[r_nb])
                        kb.op("act", lambda e, blk=blk, h=h, colx=colx: e.activation(out=Ssc[:, blk, 2 * h, :], in_=Ssc[:, blk, 2 * h, :], func=AF.Exp,
                                                                                    bias=nbv[:, colx:colx + 1], scale=1.0), reads=[r_S, r_nb], writes=[r_S])
                        kb.op("act", lambda e, blk=blk, h=h: e.activation(out=Ssc[:, blk, 2 * h + 1, :], in_=Ssc[:, blk, 2 * h + 1, :], func=AF.Exp),
                              reads=[r_S], writes=[r_S])
                        kb.op("act", lambda e, colx=colx: e.activation(out=tau[:, colx:colx + 1], in_=tau[:, colx:colx + 1], func=AF.Exp,
                                                                      bias=nbv[:, colx:colx + 1], scale=1.0), reads=[r_tau, r_nb], writes=[r_tau])
                        kb.op("dve", lambda e, colx=colx: e.tensor_scalar(out=tau[:, colx:colx + 1], in0=tau[:, colx:colx + 1], scalar1=0.9999, scalar2=None, op0=ALU.mult),
                              reads=[r_tau], writes=[r_tau])
                barrier()
            with ExitStack() as sc_:
                uTs = [sbt(sc_, "uT%d" % i, [128, KC, 128], BF16) for i in range(2)]; r_uT = [res("uT0"), res("uT1")]
                GTs2 = [sbt(sc_, "GTs%d" % i, [128, 8, 512], BF16) for i in range(2)]; r_GTs2 = [res("GTs0"), res("GTs1")]
                Gact = [sbt(sc_, "Gact%d" % i, [128, 8, 512], BF16) for i in range(2)]; r_Gact = [res("Gact0"), res("Gact1")]
                vs = [sbt(sc_, "vs%d" % i, [128, 8, 512], BF16) for i in range(2)]; r_vs = [res("vs0"), res("vs1")]
                cc = [sbt(sc_, "cc%d" % i, [128, 8, 128], F32) for i in range(2)]; r_cc = [res("cc0"), res("cc1")]
                ee = [sbt(sc_, "ee%d" % i, [128, 1024], BF16) for i in range(2)]; r_ee = [res("ee0"), res("ee1")]
                ww = [sbt(sc_, "ww%d" % i, [128, 1024], BF16) for i in range(2)]; r_ww = [res("ww0"), res("ww1")]
                ga = [sbt(sc_, "ga%d" % i, [128, 512], BF16) for i in range(2)]; r_ga = [res("ga0"), res("ga1")]
                oc = [sbt(sc_, "oc%d" % i, [128, 512], F32) for i in range(2)]; r_oc = [res("oc0"), res("oc1")]
                pgt = [[pst(sc_, "gt%d_%d" % (i, k), [128, 512], F32) for k in range(2)] for i in range(2)]
                r_pgt = [res("pgt0"), res("pgt1")]
                pact = [pst(sc_, "act%d" % i, [128, 512], F32) for i in range(2)]; r_pact = [res("pact0"), res("pact1")]
                pfo = [pst(sc_, "fo%d" % i, [128, 512], F32) for i in range(2)]; r_pfo = [res("pfo0"), res("pfo1")]
                uview = uT_d.rearrange("(i p) (k j) -> i p k j", p=128, j=128)

                def load_u(i):
                    s_ = cnt["u"] % 2; cnt["u"] += 1
                    kb.dma("pool", uTs[s_][:], uview[i], r_uT[s_], writes=[r_uT[s_]])
                    return s_

                def stage1(ch):
                    i0 = ch * 8
                    gt_ = ch % 2
                    for blk in range(4):
                        gb = cnt["gt"] % 2; cnt["gt"] += 1
                        for h in range(8):
                            colx = blk * 8 + h
                            k_ = cnt["ce"] % 2; cnt["ce"] += 1
                            kb.op("dve", lambda e, k_=k_, blk=blk, h=h: e.tensor_tensor(
                                out=cc[k_][:], in0=Ssc[:, blk, 2 * h, i0:i0 + 8].unsqueeze(2).broadcast_to([128, 8, 128]),
                                in1=Ssc[:, blk, 2 * h + 1, :].unsqueeze(1).broadcast_to([128, 8, 128]), op=ALU.mult),
                                reads=[r_S], writes=[r_cc[k_]])
                            kb.op("dve", lambda e, k_=k_, colx=colx: e.scalar_tensor_tensor(out=ww[k_][:], in0=cc[k_][:].rearrange("p a b -> p (a b)"),
                                                                                           scalar=tau[:, colx:colx + 1], in1=cc[k_][:].rearrange("p a b -> p (a b)"),
                                                                                           op0=ALU.is_ge, op1=ALU.mult),
                                  reads=[r_cc[k_], r_tau], writes=[r_ww[k_]])
                            for ii in range(8):
                                kb.op("pe", lambda e, ii=ii, k_=k_, gb=gb, h=h: e.matmul(pgt[gb][ii // 4][:, (ii % 4) * 128:(ii % 4 + 1) * 128],
                                                                                        lhsT=ww[k_][:, ii * 128:(ii + 1) * 128], rhs=ident[:],
                                                                                        start=(h == 0 and ii % 4 == 0), stop=(h == 7 and ii % 4 == 3)),
                                      reads=[r_ww[k_], r_id], writes=[r_pgt[gb]], sig=(ii == 7))
                            yield
                        for k in range(2):
                            dst = GTs2[gt_][:, k * 4:(k + 1) * 4, blk * 128:(blk + 1) * 128]
                            src = pgt[gb][k][:].rearrange("p (a b) -> p a b", b=128)
                            if k == 0:
                                kb.op("act", lambda e, dst=dst, src=src: e.activation(out=dst, in_=src, func=AF.Identity), reads=[r_pgt[gb]], writes=[r_GTs2[gt_]])
                            else:
                                kb.op("dve", lambda e, dst=dst, src=src: e.tensor_copy(out=dst, in_=src), reads=[r_pgt[gb]], writes=[r_GTs2[gt_]])
                        yield

                useq = [(ch, ii) for ch in range(nchunk) for ii in range(8)]
                vseq = [(ch, n) for ch in range(nchunk) for n in range(8)]
                uslot = {}
                vslot = {}

                def load_u_idx(k):
                    if k < len(useq) and k not in uslot:
                        ch_, ii_ = useq[k]
                        uslot[k] = load_u(ch_ * 8 + ii_)

                def load_v_idx(k):
                    if k < len(vseq) and k not in vslot:
                        ch_, n_ = vseq[k]
                        s_ = cnt["vs"] % 2; cnt["vs"] += 1
                        kb.dma("pool", vs[s_][:], vv[:, ch_ * 8:ch_ * 8 + 8, n_ * 512:(n_ + 1) * 512], r_vs[s_], writes=[r_vs[s_]])
                        vslot[k] = s_

                def stage23(ch):
                    gsl = ch % 2
                    gt_ = ch % 2
                    for ii in range(8):
                        ku = ch * 8 + ii
                        load_u_idx(ku)
                        us = uslot.pop(ku)
                        load_u_idx(ku + 1)
                        if ii == 7:
                            load_v_idx(ch * 8)
                        pa = cnt["act"] % 2; cnt["act"] += 1
                        for kc in range(KC):
                            kb.op("pe", lambda e, kc=kc, us=us, pa=pa: e.matmul(pact[pa][:], lhsT=uTs[us][:, kc, :], rhs=h2T[:, kc, :], start=(kc == 0), stop=(kc == KC - 1)),
                                  reads=[r_uT[us], r_h2T], writes=[r_pact[pa]], sig=(kc == KC - 1))
                        g_ = cnt["ga"] % 2; cnt["ga"] += 1
                        kb.op("act", lambda e, pa=pa, g_=g_: e.activation(out=ga[g_][:], in_=pact[pa][:], func=AF.Gelu), reads=[r_pact[pa]], writes=[r_ga[g_]])
                        kb.op("pool", lambda e, ii=ii, g_=g_: e.tensor_tensor(out=Gact[gsl][:, ii, :], in0=ga[g_][:], in1=GTs2[gt_][:, ii, :], op=ALU.mult),
                              reads=[r_ga[g_], r_GTs2[gt_]], writes=[r_Gact[gsl]])
                        yield
                    for n in range(8):
                        kv = ch * 8 + n
                        load_v_idx(kv)
                        vsl = vslot.pop(kv)
                        if n + 1 < 8:
                            load_v_idx(kv + 1)
                        for blk in range(4):
                            po = cnt["out"] % 2; cnt["out"] += 1
                            for ii in range(8):
                                kb.op("pe", lambda e, ii=ii, blk=blk, vsl=vsl, po=po: e.matmul(pfo[po][:], lhsT=Gact[gsl][:, ii, blk * 128:(blk + 1) * 128], rhs=vs[vsl][:, ii, :],
                                                                                              start=(ii == 0), stop=(ii == 7)),
                                      reads=[r_Gact[gsl], r_vs[vsl]], writes=[r_pfo[po]], sig=(ii == 7))
                            o_ = cnt["ot"] % 2; cnt["ot"] += 1
                            kb.op("dve", lambda e, po=po, o_=o_, n=n: e.tensor_tensor(out=oc[o_][:], in0=pfo[po][:], in1=grep[:, n * 512:(n + 1) * 512], op=ALU.mult),
                                  reads=[r_pfo[po], r_grep], writes=[r_oc[o_]])
                            gblk = hf * 4 + blk
                            kb.dma("pool", y_d[gblk * 128:(gblk + 1) * 128, n * 512:(n + 1) * 512], oc[o_][:], r_oc[o_], reads=[r_oc[o_], r_y[gblk][n]],
                                   writes=[r_y[gblk][n]], accum_op=ALU.add)
                        yield

                load_u_idx(0)
                for _ in stage1(0):
                    pass
                for ch in range(nchunk):
                    g23 = stage23(ch)
                    g1n = stage1(ch + 1) if ch + 1 < nchunk else iter(())
                    alive1 = True
                    for _ in g23:
                        for _k in range(3):
                            if alive1:
                                try:
                                    next(g1n)
                                except StopIteration:
                                    alive1 = False
                    if alive1:
                        for _ in g1n:
                            pass
                barrier()
    kb.finish([r for row in r_y for r in row] + [r_al, r_mod])
    return nc


def prep_l2_shared(inp):
    w_in = inp["w_in"][0]
    wga = w_in[:, LY_END:LY_END + D].reshape(KC, 128, KC, 128).transpose(2, 1, 0, 3)
    wgl = w_in[:, GA_END:GA_END + D].reshape(KC, 128, KC, 128).transpose(2, 1, 0, 3)
    wao = inp["w_attn_o"][0].reshape(16, 128, KC, 128).transpose(2, 1, 0, 3)
    wlo = inp["w_lru_o"][0].reshape(16, 128, KC, 128).transpose(2, 1, 0, 3)
    wcat = np.ascontiguousarray(np.concatenate([wga, wgl, wao, wlo], axis=2)).reshape(KC * 128, 96 * 128)
    wout = np.ascontiguousarray(inp["w_out"][0].reshape(KC, 128, 8, 512).transpose(2, 1, 0, 3)).reshape(8 * 128, KC * 512)
    wq = np.ascontiguousarray(inp["peer_wq"][0].reshape(KC, 128, 16, 128).transpose(2, 1, 0, 3)).reshape(16 * 128, KC * 128)
    skT = np.ascontiguousarray(inp["peer_subkeys"][0].reshape(16, 128, 128).transpose(2, 0, 1)).reshape(128, 16 * 128)
    uT = np.ascontiguousarray(inp["peer_u"][0].reshape(128, 128, KC, 128).transpose(0, 3, 2, 1)).reshape(128 * 128, KC * 128)
    v = np.ascontiguousarray(inp["peer_v"][0])
    return {"wcat": wcat, "wout": wout, "wq": wq, "skT": skT, "uT": uT, "v": v,
            "ident": np.eye(128, dtype=np.float32).astype(NPBF)}


def prep_l2(inp, mod, al_all, shared=None):
    if shared is None:
        shared = prep_l2_shared(inp)
    x = inp["x"][0]
    b_in = inp["b_in"][0]
    cf = np.zeros((128, C2_N), np.float32)
    cf[:, C2_SHIFT1:C2_SHIFT1 + KC] = _colT(mod[0:D])
    cf[:, C2_SCALE1:C2_SCALE1 + KC] = _colT(mod[D:2 * D])
    cf[:, C2_G1:C2_G1 + KC] = _colT(inp["norm_mix_g"][0])
    cf[:, C2_SHIFT2:C2_SHIFT2 + KC] = _colT(mod[3 * D:4 * D])
    cf[:, C2_SCALE2:C2_SCALE2 + KC] = _colT(mod[4 * D:5 * D])
    cf[:, C2_G2:C2_G2 + KC] = _colT(inp["norm_ffn_g"][0])
    cf[:, C2_BGA:C2_BGA + KC] = _colT(b_in[LY_END:LY_END + D])
    cf[:, C2_BGL:C2_BGL + KC] = _colT(b_in[GA_END:GA_END + D])
    g1rep = np.ascontiguousarray(np.broadcast_to(mod[None, 2 * D:3 * D], (128, D)))
    g2rep = np.ascontiguousarray(np.broadcast_to(mod[None, 5 * D:6 * D], (128, D)))
    maps = []
    for r in range(NCORE):
        ts = slice(r * TOK2, (r + 1) * TOK2)
        al = np.empty((KC * 128, TOK2), NPBF)
        for kc in range(16):
            al[kc * 128:(kc + 1) * 128] = al_all[kc // 2, (kc % 2) * 128:(kc % 2 + 1) * 128, ts]
            al[(16 + kc) * 128:(17 + kc) * 128] = al_all[kc // 2, (2 + kc % 2) * 128:(3 + kc % 2) * 128, ts]
        m = {"x": np.ascontiguousarray(x[ts]), "cf": cf, "g1rep": g1rep, "g2rep": g2rep, "alT": al}
        m.update(shared)
        maps.append(m)
    return maps


def prep_fused_shared(inp):
    sh = prep_l2_shared(inp)
    w_in = inp["w_in"][0]
    b_in = inp["b_in"][0]
    cfm = np.zeros((128, 2 + 16 * CF_UW), np.float32)
    cfm[:, 0] = inp["q_norm_g"][0]
    cfm[:, 1] = inp["k_norm_g"][0]
    bvrep = np.zeros((128, 16 * 129), np.float32)
    wslab = np.zeros((16 * 128, KC * WCOLS), np.float32)
    wax = np.zeros((128, 16 * 256), np.float32)
    for h in range(16):
        hs = slice(h * 128, (h + 1) * 128)
        base = 2 + h * CF_UW
        cols = [b_in[hs], b_in[Q_END + h * 128:Q_END + (h + 1) * 128],
                b_in[F_END + h * 128:F_END + (h + 1) * 128], b_in[LX_END + h * 128:LX_END + (h + 1) * 128],
                inp["conv_w"][0][0, hs], inp["conv_w"][0][1, hs], inp["conv_w"][0][2, hs], inp["conv_w"][0][3, hs],
                inp["conv_b"][0][hs], inp["lru_ba"][0][hs], inp["lru_bx"][0][hs], inp["lru_lambda"][0][hs]]
        for k, cvec in enumerate(cols):
            cfm[:, base + k] = cvec
        bvrep[:, h * 129:h * 129 + 128] = b_in[None, K_END + h * 128:K_END + (h + 1) * 128]
        bvrep[:, h * 129 + 128] = b_in[V_END + h]
        wcat = np.concatenate([w_in[:, hs], w_in[:, Q_END + h * 128:Q_END + (h + 1) * 128],
                               w_in[:, F_END + h * 128:F_END + (h + 1) * 128], w_in[:, LX_END + h * 128:LX_END + (h + 1) * 128],
                               w_in[:, K_END + h * 128:K_END + (h + 1) * 128], w_in[:, V_END + h:V_END + h + 1]], axis=1)
        wslab[h * 128:(h + 1) * 128] = wcat.reshape(KC, 128, WCOLS).transpose(1, 0, 2).reshape(128, KC * WCOLS)
        wax[:, h * 256:h * 256 + 128] = inp["lru_wa"][0][h]
        wax[:, h * 256 + 128:h * 256 + 256] = inp["lru_wx"][0][h]
    cf = np.zeros((128, C2_N), np.float32)
    cf[:, C2_G1:C2_G1 + KC] = _colT(inp["norm_mix_g"][0])
    cf[:, C2_G2:C2_G2 + KC] = _colT(inp["norm_ffn_g"][0])
    cf[:, C2_BGA:C2_BGA + KC] = _colT(b_in[LY_END:LY_END + D])
    cf[:, C2_BGL:C2_BGL + KC] = _colT(b_in[GA_END:GA_END + D])
    c1 = l1_consts()
    sh.update({"cfm": cfm, "bvrep": bvrep, "wslab": wslab, "wax": wax, "cf": cf,
               "cT": _colT(inp["c"][0]), "w_ada": np.ascontiguousarray(inp["w_ada"][0]), "b_ada": np.ascontiguousarray(inp["b_ada"][0][None]),
               "cmask": c1["cmask"], "identf": c1["identf"], "tri": c1["tri"], "ones": c1["ones"]})
    return sh


def prep_fused_core(inp, shared, nreal_tiles, ntiles=NT):
    x = inp["x"][0]
    pad = ntiles - nreal_tiles
    xpad = np.zeros((ntiles * 256, D), np.float32)
    xpad[pad * 256:] = x[0:nreal_tiles * 256]
    tmask = np.zeros((128, ntiles), np.float32)
    tmask[:, pad:] = 1.0
    m = {"xpad": xpad, "tmask": tmask}
    m.update(shared)
    return m


_CACHE = {}


def kernel(**inputs):
    inp = {k: np.asarray(v) for k, v in inputs.items()}
    cores = list(range(NCORE))
    if "fused" not in _CACHE:
        _CACHE["fused"] = build_l2(2, 16, fz=dict(nunits=16, ntiles=NT))
    shared = prep_fused_shared(inp)
    maps = [prep_fused_core(inp, shared, 4 * (r + 1)) for r in cores]
    res = run_bass_kernel_spmd(_CACHE["fused"], maps, core_ids=cores)
    y = np.concatenate([np.asarray(r["y"]) for r in res.results], axis=0)
    return y[None].astype(np.float32)
```

```python
import numpy as np
import ml_dtypes
import concourse.bass as bass
import concourse.mybir as mybir
from concourse.bass_utils import run_bass_kernel_spmd

F32, BF16 = mybir.dt.float32, mybir.dt.bfloat16
ALU = mybir.AluOpType
AF = mybir.ActivationFunctionType
NPBF = ml_dtypes.bfloat16

D = 4096
S = 8192
NCORE = 8
KC = D // 128
EPS = 1e-6


class Res:
    __slots__ = ("name", "w", "r", "dsem", "dcnt", "pend")

    def __init__(self, name):
        self.name = name
        self.w = {}
        self.r = {}
        self.dsem = None
        self.dcnt = 0
        self.pend = None


class KB:
    def __init__(self, nc):
        self.nc = nc
        self.eng = dict(pe=nc.tensor, act=nc.scalar, dve=nc.vector, pool=nc.gpsimd, sp=nc.sync)
        self.psem = {}
        self.pcnt = {}
        for k in ("pe", "act", "dve", "pool"):
            self.psem[k] = self.newsem("prog_" + k)
            self.pcnt[k] = 0
        self.waited = {}
        self.pend = {k: [] for k in self.psem}
        self.nres = 0

    def newsem(self, name):
        return self.nc.semaphore(name).__enter__()

    def res(self, name=None):
        self.nres += 1
        return Res(name or "r%d" % self.nres)

    def sb(self, name, shape, dt):
        return self.nc.sbuf_tensor("s_" + name, shape, dt).__enter__()

    def ps(self, name, shape, dt):
        return self.nc.psum_tensor("p_" + name, shape, dt).__enter__()

    def _wait(self, e, t):
        sem, val, key = t
        if key == "pe" and e == "pe":
            return
        if self.waited.get((e, key), 0) >= val:
            return
        self.waited[(e, key)] = val
        self.eng[e].wait_ge(sem, val)

    def _deps(self, e, reads, writes, skipkey=None):
        for r in reads:
            assert r.pend is None or r.pend == e, (r.name, r.pend, e)
            for k, t in r.w.items():
                if k != skipkey:
                    self._wait(e, t)
        for w in writes:
            assert w.pend is None or w.pend == e, (w.name, w.pend, e)
            for k, t in w.w.items():
                if k != skipkey:
                    self._wait(e, t)
            for k, t in w.r.items():
                self._wait(e, t)

    @staticmethod
    def _assign(t, reads, writes, part=False):
        key = t[2]
        for w in writes:
            if part:
                w.w[key] = t
            else:
                w.w = {key: t}
                w.r = {}
            w.pend = None
        for r in reads:
            r.r[key] = t
            r.pend = None

    def op(self, e, fn, reads=(), writes=(), sig=True):
        self._deps(e, reads, writes)
        ins = fn(self.eng[e])
        if sig:
            self.pcnt[e] += 1
            ins.then_inc(self.psem[e], 1)
            t = (self.psem[e], self.pcnt[e], e)
            for (rs, ws) in self.pend[e]:
                self._assign(t, rs, ws)
            self.pend[e] = []
            self._assign(t, reads, writes)
        else:
            self.pend[e].append((reads, writes))
            for x in list(reads) + list(writes):
                x.pend = e
        return ins

    def dma(self, q, out, in_, sres, reads=(), writes=(), part=False, **kw):
        key = "d_" + sres.name
        self._deps(q, reads, writes, skipkey=key if part else None)
        ins = self.eng[q].dma_start(out=out, in_=in_, **kw)
        if sres.dsem is None:
            sres.dsem = self.newsem("dma_" + sres.name)
        sres.dcnt += 16
        ins.then_inc(sres.dsem, 16)
        t = (sres.dsem, sres.dcnt, key)
        self._assign(t, reads, writes, part=part)
        return ins

    def finish(self, resources, e="sp"):
        for r in resources:
            for t in list(r.w.values()) + list(r.r.values()):
                self._wait(e, t)


def _new_nc():
    return bass.Bass("TRN2", target_bir_lowering=False)


L0_COLS = 6 * D // NCORE


def build_l0():
    nc = _new_nc()
    kb = KB(nc)
    cT = nc.dram_tensor("cT", [128, KC], F32, kind="ExternalInput").ap()
    w = nc.dram_tensor("w", [D, L0_COLS], F32, kind="ExternalInput").ap()
    b = nc.dram_tensor("b", [1, L0_COLS], F32, kind="ExternalInput").ap()
    mod = nc.dram_tensor("mod", [1, L0_COLS], F32, kind="ExternalOutput").ap()
    NB = L0_COLS // 512
    ct = kb.sb("ct", [128, KC], F32)
    sc = kb.sb("sc", [128, KC], F32)
    bt = kb.sb("bt", [1, L0_COLS], F32)
    ot = kb.sb("ot", [1, L0_COLS], F32)
    G = 4
    wt = [kb.sb("wt%d" % i, [128, G, L0_COLS], F32) for i in range(2)]
    r_ct, r_sc, r_bt, r_ot, r_mod = kb.res("ct"), kb.res("sc"), kb.res("bt"), kb.res("ot"), kb.res("mod")
    r_wt = [kb.res("wt0"), kb.res("wt1")]
    pss = [kb.ps("ps%d" % i, [128, 512], F32) for i in range(NB)]
    r_ps = [kb.res("ps%d" % i) for i in range(NB)]
    kb.dma("sp", ct[:], cT[:, :], r_ct, writes=[r_ct])
    kb.dma("sp", bt[:], b[:, :], r_bt, writes=[r_bt])
    kb.op("act", lambda e: e.activation(out=sc[:], in_=ct[:], func=AF.Silu), reads=[r_ct], writes=[r_sc])
    wv = w.rearrange("(k p) n -> p k n", p=128)
    ngrp = KC // G

    def load(g):
        kb.dma("sp", wt[g % 2][:], wv[:, g * G:(g + 1) * G, :], r_wt[g % 2], writes=[r_wt[g % 2]])

    load(0)
    for g in range(ngrp):
        if g + 1 < ngrp:
            load(g + 1)
        for kk in range(G):
            kc = g * G + kk
            for n in range(NB):
                last = (kc == KC - 1)
                kb.op("pe", lambda e, kc=kc, kk=kk, n=n, g=g: e.matmul(
                    pss[n][0:1, :], lhsT=sc[:, kc:kc + 1], rhs=wt[g % 2][:, kk, n * 512:(n + 1) * 512],
                    start=(kc == 0), stop=(kc == KC - 1)),
                    reads=[r_sc, r_wt[g % 2]], writes=[r_ps[n]],
                    sig=(last or (kk == G - 1 and n == NB - 1)))
    for n in range(NB):
        kb.op("dve", lambda e, n=n: e.tensor_tensor(out=ot[0:1, n * 512:(n + 1) * 512], in0=pss[n][0:1, :],
                                                    in1=bt[0:1, n * 512:(n + 1) * 512], op=ALU.add),
              reads=[r_ps[n], r_bt], writes=[r_ot])
    kb.dma("sp", mod[:, :], ot[:], r_ot, reads=[r_ot], writes=[r_mod], part=True)
    kb.finish([r_mod])
    return nc


TT = 256
NT = S // TT
NBLK = S // 128
WCOLS = 641
CF_SCALE1, CF_SHIFT1, CF_G = 0, 32, 64
CF_GQ, CF_GK = 96, 97
CF_U = 98
CF_UW = 12
CF_N = CF_U + 2 * CF_UW


def build_l1(ntiles=NT):
    nc = _new_nc()
    kb = KB(nc)
    x = nc.dram_tensor("x", [S, D], F32, kind="ExternalInput").ap()
    cf_d = nc.dram_tensor("cf", [128, CF_N], F32, kind="ExternalInput").ap()
    bv_d = nc.dram_tensor("bvrep", [128, 2 * 129], F32, kind="ExternalInput").ap()
    w_d = nc.dram_tensor("wslab", [2 * 128, KC * WCOLS], F32, kind="ExternalInput").ap()
    wax_d = nc.dram_tensor("wax", [128, 2 * 2 * 128], F32, kind="ExternalInput").ap()
    id_d = nc.dram_tensor("ident", [128, 128], BF16, kind="ExternalInput").ap()
    cm_d = nc.dram_tensor("cmask", [128, 128], F32, kind="ExternalInput").ap()
    idf_d = nc.dram_tensor("identf", [128, 128], F32, kind="ExternalInput").ap()
    tri_d = nc.dram_tensor("tri", [128, 128], F32, kind="ExternalInput").ap()
    on_d = nc.dram_tensor("ones", [128, 2 * 128], F32, kind="ExternalInput").ap()
    alT = nc.dram_tensor("alT", [4 * 128, S], BF16, kind="ExternalOutput").ap()
    r_out = kb.res("alT")

    cf = kb.sb("cf", [128, CF_N], F32); r_cf = kb.res("cf")
    bv = kb.sb("bv", [128, 2 * 129], F32); r_bv = kb.res("bv")
    ident = kb.sb("ident", [128, 128], BF16); r_id = kb.res("ident")
    maskneg = kb.sb("cmask", [128, 128], F32); r_cm = kb.res("cmask")
    cmask = maskneg
    identf = kb.sb("identf", [128, 128], F32); r_idf = kb.res("identf")
    dgn = kb.sb("dgn", [128, 128], F32); r_dgn = kb.res("dgn")
    Dm = kb.sb("Dm", [128, 128], F32); r_Dm = kb.res("Dm")
    sdg = kb.sb("sdg", [128, 128], F32); r_sdg = kb.res("sdg")
    osb = kb.sb("osb", [128, 129], F32); r_osb = kb.res("osb")
    tri = kb.sb("tri", [128, 128], F32); r_tri = kb.res("tri")
    ones = kb.sb("ones", [128, 256], F32); r_on = kb.res("ones")
    wax = kb.sb("wax", [128, 512], BF16); r_wax = kb.res("wax")
    kb.dma("sp", cf[:], cf_d[:, :], r_cf, writes=[r_cf])
    kb.dma("sp", bv[:], bv_d[:, :], r_bv, writes=[r_bv])
    kb.dma("sp", ident[:], id_d[:, :], r_id, writes=[r_id])
    kb.dma("sp", cmask[:], cm_d[:, :], r_cm, writes=[r_cm])
    kb.dma("sp", identf[:], idf_d[:, :], r_idf, writes=[r_idf])
    kb.dma("sp", tri[:], tri_d[:, :], r_tri, writes=[r_tri])
    kb.dma("sp", ones[:], on_d[:, :], r_on, writes=[r_on])
    kb.dma("pool", wax[:], wax_d[:, :], r_wax, writes=[r_wax])
    A1 = kb.sb("A1", [128, KC], F32); r_A1 = kb.res("A1")
    kb.op("dve", lambda e: e.scalar_tensor_tensor(out=A1[:], in0=cf[:, CF_SCALE1:CF_SCALE1 + KC], scalar=1.0,
                                                 in1=cf[:, CF_G:CF_G + KC], op0=ALU.add, op1=ALU.mult),
          reads=[r_cf], writes=[r_A1])
    c8 = kb.sb("c8", [128, 8], F32); r_c8 = kb.res("c8")
    for u in range(2):
        lam = cf[:, CF_U + u * CF_UW + 11:CF_U + u * CF_UW + 12]
        kb.op("act", lambda e, u=u, lam=lam: e.activation(out=c8[:, 4 + u:5 + u], in_=lam, func=AF.Exp, scale=-1.0),
              reads=[r_cf], writes=[r_c8])
        kb.op("act", lambda e, u=u: e.activation(out=c8[:, 6 + u:7 + u], in_=c8[:, 4 + u:5 + u], func=AF.Ln, bias=1.0, scale=1.0),
              reads=[r_c8], writes=[r_c8])
        kb.op("dve", lambda e, u=u: e.tensor_scalar(out=c8[:, u:u + 1], in0=c8[:, 6 + u:7 + u], scalar1=-8.0, scalar2=None, op0=ALU.mult),
              reads=[r_c8], writes=[r_c8])
        kb.op("dve", lambda e, u=u: e.tensor_scalar(out=c8[:, 2 + u:3 + u], in0=c8[:, 6 + u:7 + u], scalar1=-16.0, scalar2=None, op0=ALU.mult),
              reads=[r_c8], writes=[r_c8])

    wsl = kb.sb("wsl", [128, KC, WCOLS], BF16); r_wsl = kb.res("wsl")
    xb = [kb.sb("xb%d" % i, [128, 2, D], BF16) for i in range(2)]
    r_xb = [kb.res("xb0"), kb.res("xb1")]
    junk = kb.sb("junk", [128, D], BF16); r_junk = kb.res("junk")
    hT = kb.sb("hT", [128, KC, TT], BF16); r_hT = [kb.res("hT0"), kb.res("hT1")]
    KT = kb.sb("KT", [128, S], BF16); r_KT = kb.res("KT")
    V = kb.sb("V", [128, NBLK, 129], BF16); r_V = kb.res("V")
    QT = kb.sb("QT", [128, TT], BF16); r_QT = kb.res("QT")
    NFT = kb.sb("NFT", [128, NBLK], F32); r_NFT = kb.res("NFT")
    NR = kb.sb("NR", [128, NBLK + 1], F32); r_NR = kb.res("NR")
    st = kb.sb("st", [128, 8], F32); r_st = kb.res("st")
    fl = kb.sb("fl", [128, 12], F32); r_fl = kb.res("fl")
    qf = kb.sb("qf", [128, TT], F32); r_qf = kb.res("qf")
    qs = kb.sb("qs", [128, TT], F32); r_qs = kb.res("qs")
    qr = qs; r_qr = r_qs
    Bj = [kb.sb("Bj%d" % i, [128, NBLK], F32) for i in range(2)]; r_Bj = [kb.res("Bj0"), kb.res("Bj1")]
    NPT = 4
    PT = [kb.sb("PT%d" % i, [128, 128], BF16) for i in range(NPT)]; r_PT = [kb.res("PT%d" % i) for i in range(NPT)]
    rec = kb.sb("rec", [128, 2], F32); r_rec = kb.res("rec")
    ao = kb.sb("ao", [128, 128], BF16); r_ao = kb.res("ao")
    aoT = [kb.sb("aoT%d" % i, [128, TT], BF16) for i in range(2)]; r_aoT = [kb.res("aoT0"), kb.res("aoT1")]
    lx = [kb.sb("lx%d" % i, [128, TT + 3], F32) for i in range(2)]; r_lx = [kb.res("lx0"), kb.res("lx1")]
    xc = kb.sb("xc", [128, TT], F32); r_xc = kb.res("xc")
    xcb = kb.sb("xcb", [128, TT], BF16); r_xcb = kb.res("xcb")
    rg = kb.sb("rg", [128, TT], F32); r_rg = kb.res("rg")
    ig = kb.sb("ig", [128, TT], F32); r_ig = kb.res("ig")
    aa = kb.sb("aa", [128, TT], F32); r_aa = kb.res("aa")
    a2 = kb.sb("a2", [128, TT], F32); r_a2 = kb.res("a2")
    hh = [kb.sb("hh%d" % i, [128, TT], F32) for i in range(2)]; r_hh = [kb.res("hh0"), kb.res("hh1")]
    gy = kb.sb("gy", [128, TT], F32); r_gy = kb.res("gy")
    lo = [kb.sb("lo%d" % i, [128, TT], BF16) for i in range(2)]; r_lo = [kb.res("lo0"), kb.res("lo1")]
    zero1 = kb.sb("zero1", [128, 1], F32); r_z = kb.res("zero1")
    kb.op("dve", lambda e: e.memset(zero1[:], 0.0), writes=[r_z])

    ps_tr = [kb.ps("ps_tr%d" % i, [128, 1024], BF16) for i in range(2)]; r_ptr = [kb.res("ptr0"), kb.res("ptr1")]
    ps_mm = [kb.ps("ps_mm%d" % i, [128, 512], F32) for i in range(2)]; r_pmm = [kb.res("pmm0"), kb.res("pmm1")]
    ps_s = [kb.ps("ps_s%d" % i, [128, 512], F32) for i in range(2)]
    r_pss = [kb.res("pss%d" % i) for i in range(2)]
    ps_o = [kb.ps("ps_o%d" % i, [128, 512], F32) for i in range(2)]; r_po = [kb.res("po0"), kb.res("po1")]

    xv = x.rearrange("(s b p) d -> s p b d", b=2, p=128)
    SCALE = float(128 ** -0.5)
    cnt = dict(mm=0, tr=0, ev=0, s=0, pt=0, o=0, bj=0, aot=0, lo=0)

    def load_x(gs, sidx):
        kb.dma("pool", xb[gs % 2][:], xv[sidx], r_xb[gs % 2], writes=[r_xb[gs % 2]])

    for u in range(2):
        ub = CF_U + u * CF_UW
        col = lambda k: cf[:, ub + k:ub + k + 1]
        for part in range(4):
            kb.dma("pool", wsl[:, part * 8:(part + 1) * 8, :],
                   w_d[u * 128:(u + 1) * 128, part * 8 * WCOLS:(part + 1) * 8 * WCOLS].rearrange("p (k c) -> p k c", c=WCOLS),
                   r_wsl, writes=[r_wsl], part=(part > 0))
        kb.op("dve", lambda e: e.memset(NR[:, 0:1], 0.0), writes=[r_NR])
        kb.op("dve", lambda e: e.memset(V[:, :, 128:129], 1.0), writes=[r_V])
        kb.op("dve", lambda e: e.memset(lx[0][:, 0:3], 0.0), writes=[r_lx[0]])
        load_x(u * ntiles, 0)
        hprev = None
        for T in range(ntiles):
            for hf in range(1):
                sidx = T
                gs = u * ntiles + sidx
                if sidx + 1 < ntiles:
                    load_x(gs + 1, sidx + 1)
                kb.op("dve", lambda e: e.memset(st[:, 0:2], 0.0), writes=[r_st])
                xt = xb[gs % 2]; rx = r_xb[gs % 2]
                for b in range(2):
                    kb.op("act", lambda e, b=b, xt=xt: e.activation(out=junk[:], in_=xt[:, b, :], func=AF.Square,
                                                                  scale=1.0 / 64.0, accum_out=st[:, b:b + 1]),
                          reads=[rx], writes=[r_junk, r_st])
                kb.op("act", lambda e: e.activation(out=st[:, 2:4], in_=st[:, 0:2], func=AF.Sqrt, bias=EPS, scale=1.0),
                      reads=[r_st], writes=[r_st])
                kb.op("dve", lambda e: e.reciprocal(out=st[:, 4:6], in_=st[:, 2:4]), reads=[r_st], writes=[r_st])
                for b in range(2):
                    kb.op("dve", lambda e, b=b, xt=xt: e.tensor_scalar(out=xt[:, b, :], in0=xt[:, b, :], scalar1=st[:, 4 + b:5 + b],
                                                                     scalar2=None, op0=ALU.mult),
                          reads=[r_st, rx], writes=[rx])
                for g in range(KC // 4):
                    pt = cnt["tr"] % 2; cnt["tr"] += 1
                    for k4 in range(4):
                        kc = g * 4 + k4
                        for b in range(2):
                            kb.op("pe", lambda e, kc=kc, k4=k4, b=b, xt=xt, pt=pt: e.transpose(
                                out=ps_tr[pt][:, k4 * 256 + b * 128:k4 * 256 + (b + 1) * 128],
                                in_=xt[:, b, kc * 128:(kc + 1) * 128], identity=ident[:]),
                                reads=[rx, r_id], writes=[r_ptr[pt]], sig=(k4 == 3 and b == 1))
                    for k4 in range(4):
                        kc = g * 4 + k4
                        dst = hT[:, kc, :]
                        src = ps_tr[pt][:, k4 * 256:(k4 + 1) * 256]
                        if cnt["ev"] % 2 == 0:
                            kb.op("act", lambda e, dst=dst, src=src, kc=kc: e.activation(
                                out=dst, in_=src, func=AF.Identity, scale=A1[:, kc:kc + 1], bias=cf[:, CF_SHIFT1 + kc:CF_SHIFT1 + kc + 1]),
                                reads=[r_ptr[pt], r_A1, r_cf], writes=[r_hT[hf]])
                        else:
                            kb.op("dve", lambda e, dst=dst, src=src, kc=kc: e.tensor_scalar(
                                out=dst, in0=src, scalar1=A1[:, kc:kc + 1], scalar2=cf[:, CF_SHIFT1 + kc:CF_SHIFT1 + kc + 1],
                                op0=ALU.mult, op1=ALU.add),
                                reads=[r_ptr[pt], r_A1, r_cf], writes=[r_hT[hf]])
                        cnt["ev"] += 1

            def proj(si):
                pm = cnt["mm"] % 2; cnt["mm"] += 1
                for kc in range(KC):
                    kb.op("pe", lambda e, kc=kc, pm=pm: e.matmul(ps_mm[pm][:, 0:TT], lhsT=wsl[:, kc, si * 128:(si + 1) * 128], rhs=hT[:, kc, :],
                                                                 start=(kc == 0), stop=(kc == KC - 1)),
                          reads=[r_wsl, r_hT[0]], writes=[r_pmm[pm]], sig=(kc == KC - 1))
                return pm

            def qknorm(pm, bcol, gcol, dst, r_dst):
                kb.op("act", lambda e: e.activation(out=qf[:], in_=ps_mm[pm][:, 0:TT], func=AF.Identity, bias=bcol, scale=1.0),
                      reads=[r_pmm[pm], r_cf], writes=[r_qf])
                kb.op("act", lambda e: e.activation(out=qs[:], in_=qf[:], func=AF.Square), reads=[r_qf], writes=[r_qs])
                px = cnt["mm"] % 2; cnt["mm"] += 1
                kb.op("pe", lambda e: e.matmul(ps_mm[px][:, 0:TT], lhsT=ones[:, 0:128], rhs=qs[:], start=True, stop=True),
                      reads=[r_on, r_qs], writes=[r_pmm[px]])
                kb.op("act", lambda e: e.activation(out=qr[:], in_=ps_mm[px][:, 0:TT], func=AF.Sqrt, bias=EPS, scale=1.0),
                      reads=[r_pmm[px]], writes=[r_qr])
                kb.op("dve", lambda e: e.reciprocal(out=qr[:], in_=qr[:]), reads=[r_qr], writes=[r_qr])
                kb.op("dve", lambda e: e.scalar_tensor_tensor(out=dst, in0=qf[:], scalar=gcol, in1=qr[:], op0=ALU.mult, op1=ALU.mult),
                      reads=[r_qf, r_qr, r_cf], writes=[r_dst])

            pm = proj(0)
            qknorm(pm, col(0), cf[:, CF_GQ:CF_GQ + 1], QT[:], r_QT)
            pm = proj(1)
            qknorm(pm, col(1), cf[:, CF_GK:CF_GK + 1], KT[:, T * TT:(T + 1) * TT], r_KT)
            for b in range(2):
                blk = T * 2 + b
                pm = cnt["mm"] % 2; cnt["mm"] += 1
                for kc in range(KC):
                    kb.op("pe", lambda e, kc=kc, pm=pm, b=b: e.matmul(ps_mm[pm][:, 0:129], lhsT=hT[:, kc, b * 128:(b + 1) * 128],
                                                                      rhs=wsl[:, kc, 512:641], start=(kc == 0), stop=(kc == KC - 1)),
                          reads=[r_wsl, r_hT[0]], writes=[r_pmm[pm]], sig=(kc == KC - 1))
                kb.op("dve", lambda e, pm=pm, blk=blk: e.tensor_tensor(out=V[:, blk, 0:128], in0=ps_mm[pm][:, 0:128],
                                                                       in1=bv[:, u * 129:u * 129 + 128], op=ALU.add),
                      reads=[r_pmm[pm], r_bv], writes=[r_V])
                kb.op("dve", lambda e, pm=pm, b=b: e.tensor_tensor(out=fl[:, b:b + 1], in0=ps_mm[pm][:, 128:129],
                                                                   in1=bv[:, u * 129 + 128:u * 129 + 129], op=ALU.add),
                      reads=[r_pmm[pm], r_bv], writes=[r_fl])
            kb.op("act", lambda e: e.activation(out=fl[:, 4:6], in_=fl[:, 0:2], func=AF.Exp, scale=-1.0), reads=[r_fl], writes=[r_fl])
            kb.op("act", lambda e: e.activation(out=fl[:, 8:10], in_=fl[:, 4:6], func=AF.Ln, bias=1.0, scale=1.0), reads=[r_fl], writes=[r_fl])
            pxc = cnt["mm"] % 2; cnt["mm"] += 1
            ps_x = ps_mm[pxc]; r_px = r_pmm[pxc]
            kb.op("pe", lambda e: e.matmul(ps_x[:, 0:2], lhsT=tri[:], rhs=fl[:, 8:10], start=True, stop=True),
                  reads=[r_tri, r_fl], writes=[r_px], sig=False)
            kb.op("pe", lambda e: e.matmul(ps_x[:, 4:6], lhsT=ones[:, 128:256], rhs=fl[:, 8:10], start=True, stop=True),
                  reads=[r_on, r_fl], writes=[r_px])
            for b in range(2):
                blk = T * 2 + b
                kb.op("dve", lambda e, b=b, blk=blk: e.tensor_tensor(out=NR[:, blk + 1:blk + 2], in0=NR[:, blk:blk + 1],
                                                                     in1=ps_x[:, 4 + b:5 + b], op=ALU.add),
                      reads=[r_px, r_NR], writes=[r_NR])
            kb.op("dve", lambda e: e.tensor_tensor(out=NFT[:, T * 2:T * 2 + 2], in0=ps_x[:, 0:2], in1=NR[:, T * 2:T * 2 + 2], op=ALU.add),
                  reads=[r_px, r_NR], writes=[r_NFT])

            lxi = T % 2
            pm = proj(2)
            kb.op("act", lambda e, pm=pm: e.activation(out=lx[lxi][:, 3:TT + 3], in_=ps_mm[pm][:, 0:TT], func=AF.Identity, bias=col(2), scale=1.0),
                  reads=[r_pmm[pm], r_cf], writes=[r_lx[lxi]])
            pm = proj(3)
            kb.op("act", lambda e, pm=pm: e.activation(out=gy[:], in_=ps_mm[pm][:, 0:TT], func=AF.Gelu, bias=col(3), scale=1.0),
                  reads=[r_pmm[pm], r_cf], writes=[r_gy])

            groups = []
            for jq in range(2):
                j = T * 2 + jq
                ks = list(range(j))
                for g0 in range(0, len(ks), 4):
                    groups.append((jq, ks[g0:g0 + 4]))
            bj_of = {}
            sb_of = {}

            def emit_s(n):
                jq, ks = groups[n]
                j = T * 2 + jq
                if ks[0] == 0:
                    bi = cnt["bj"] % 2; cnt["bj"] += 1
                    bj_of[jq] = bi
                    kb.op("dve", lambda e, bi=bi, j=j: e.tensor_scalar(out=Bj[bi][:, 0:j], in0=NFT[:, 0:j], scalar1=NR[:, j:j + 1],
                                                                      scalar2=None, op0=ALU.subtract),
                          reads=[r_NFT, r_NR], writes=[r_Bj[bi]])
                sbk = cnt["s"] % 2; cnt["s"] += 1
                sb_of[n] = sbk
                for kk, i in enumerate(ks):
                    kb.op("pe", lambda e, sbk=sbk, kk=kk, i=i, jq=jq: e.matmul(ps_s[sbk][:, kk * 128:(kk + 1) * 128],
                                                                              lhsT=KT[:, i * 128:(i + 1) * 128], rhs=QT[:, jq * 128:(jq + 1) * 128],
                                                                              start=True, stop=True),
                          reads=[r_KT, r_QT], writes=[r_pss[sbk]], sig=(kk == len(ks) - 1))

            def diag_block(jq):
                j = T * 2 + jq
                oi = jq % 2
                kb.op("dve", lambda e: e.tensor_scalar(out=dgn[:], in0=identf[:], scalar1=NFT[:, j:j + 1], scalar2=None, op0=ALU.mult),
                      reads=[r_idf, r_NFT], writes=[r_dgn])
                px = cnt["mm"] % 2; cnt["mm"] += 1
                kb.op("pe", lambda e: e.matmul(ps_mm[px][:, 0:128], lhsT=ones[:, 128:256], rhs=dgn[:], start=True, stop=True),
                      reads=[r_on, r_dgn], writes=[r_pmm[px]])
                kb.op("dve", lambda e: e.tensor_scalar(out=Dm[:], in0=ps_mm[px][:, 0:128], scalar1=NFT[:, j:j + 1], scalar2=-1.0,
                                                      op0=ALU.subtract, op1=ALU.mult), reads=[r_pmm[px], r_NFT], writes=[r_Dm])
                kb.op("dve", lambda e: e.tensor_tensor(out=Dm[:], in0=Dm[:], in1=maskneg[:], op=ALU.add), reads=[r_Dm, r_cm], writes=[r_Dm])
                sbk = cnt["s"] % 2; cnt["s"] += 1
                kb.op("pe", lambda e: e.matmul(ps_s[sbk][:, 0:128], lhsT=KT[:, j * 128:(j + 1) * 128], rhs=QT[:, jq * 128:(jq + 1) * 128],
                                               start=True, stop=True), reads=[r_KT, r_QT], writes=[r_pss[sbk]])
                kb.op("dve", lambda e: e.scalar_tensor_tensor(out=sdg[:], in0=ps_s[sbk][:, 0:128], scalar=SCALE, in1=Dm[:], op0=ALU.mult, op1=ALU.add),
                      reads=[r_pss[sbk], r_Dm], writes=[r_sdg])
                pi = cnt["pt"] % NPT; cnt["pt"] += 1
                kb.op("act", lambda e: e.activation(out=PT[pi][:], in_=sdg[:], func=AF.Exp), reads=[r_sdg], writes=[r_PT[pi]])
                kb.op("pe", lambda e: e.matmul(ps_o[oi][:, 256:385], lhsT=PT[pi][:], rhs=V[:, j, :], start=True, stop=True),
                      reads=[r_PT[pi], r_V], writes=[r_po[oi]])
                if j > 0:
                    kb.op("act", lambda e: e.activation(out=rec[:, 1:2], in_=NFT[:, j:j + 1], func=AF.Exp, scale=-1.0, bias=NR[:, j:j + 1]),
                          reads=[r_NFT, r_NR], writes=[r_rec])
                    kb.op("act", lambda e: e.activation(out=osb[:], in_=ps_o[oi][:, 256:385], func=AF.Identity), reads=[r_po[oi]], writes=[r_osb])
                    kb.op("dve", lambda e: e.scalar_tensor_tensor(out=osb[:], in0=ps_o[oi][:, 0:129], scalar=rec[:, 1:2], in1=osb[:],
                                                                 op0=ALU.mult, op1=ALU.add), reads=[r_po[oi], r_rec, r_osb], writes=[r_osb])
                else:
                    kb.op("act", lambda e: e.activation(out=osb[:], in_=ps_o[oi][:, 256:385], func=AF.Identity), reads=[r_po[oi]], writes=[r_osb])
                ai = T % 2
                kb.op("dve", lambda e: e.reciprocal(out=rec[:, 0:1], in_=osb[:, 128:129]), reads=[r_osb], writes=[r_rec])
                kb.op("dve", lambda e: e.tensor_scalar(out=ao[:], in0=osb[:, 0:128], scalar1=rec[:, 0:1], scalar2=None, op0=ALU.mult),
                      reads=[r_osb, r_rec], writes=[r_ao])
                pt = cnt["tr"] % 2; cnt["tr"] += 1
                kb.op("pe", lambda e, pt=pt: e.transpose(out=ps_tr[pt][:, 0:128], in_=ao[:], identity=ident[:]),
                      reads=[r_ao, r_id], writes=[r_ptr[pt]])
                kb.op("dve", lambda e, pt=pt: e.tensor_copy(out=aoT[ai][:, jq * 128:(jq + 1) * 128], in_=ps_tr[pt][:, 0:128]),
                      reads=[r_ptr[pt]], writes=[r_aoT[ai]])

            if groups:
                emit_s(0)
            done_diag = set()
            for n, (jq, ks) in enumerate(groups):
                j = T * 2 + jq
                if n + 1 < len(groups):
                    emit_s(n + 1)
                sbk = sb_of.pop(n)
                bi = bj_of[jq]
                oi = jq % 2
                for kk, i in enumerate(ks):
                    pi = cnt["pt"] % NPT; cnt["pt"] += 1
                    kb.op("act", lambda e, sbk=sbk, kk=kk, pi=pi, bi=bi, i=i: e.activation(
                        out=PT[pi][:], in_=ps_s[sbk][:, kk * 128:(kk + 1) * 128], func=AF.Exp,
                        scale=SCALE, bias=Bj[bi][:, i:i + 1]),
                        reads=[r_pss[sbk], r_Bj[bi]], writes=[r_PT[pi]])
                    kb.op("pe", lambda e, pi=pi, i=i, oi=oi, j=j: e.matmul(ps_o[oi][:, 0:129], lhsT=PT[pi][:], rhs=V[:, i, :],
                                                                            start=(i == 0), stop=(i == j - 1)),
                          reads=[r_PT[pi], r_V], writes=[r_po[oi]])
                    if i == j - 1:
                        diag_block(jq)
                        done_diag.add(jq)
            for jq in range(2):
                if jq not in done_diag:
                    diag_block(jq)
            ai = T % 2
            kb.dma("sp", alT[u * 128:(u + 1) * 128, T * TT:(T + 1) * TT], aoT[ai][:], r_aoT[ai], reads=[r_aoT[ai]], writes=[r_out], part=True)

            lxt = lx[lxi]
            kb.op("dve", lambda e: e.tensor_scalar(out=xc[:], in0=lxt[:, 0:TT], scalar1=col(4), scalar2=col(8), op0=ALU.mult, op1=ALU.add),
                  reads=[r_lx[lxi], r_cf], writes=[r_xc])
            for k in range(1, 4):
                kb.op("dve", lambda e, k=k: e.scalar_tensor_tensor(out=xc[:], in0=lxt[:, k:k + TT], scalar=col(4 + k), in1=xc[:],
                                                                  op0=ALU.mult, op1=ALU.add),
                      reads=[r_lx[lxi], r_cf, r_xc], writes=[r_xc])
            kb.op("pool", lambda e: e.tensor_copy(out=lx[1 - lxi][:, 0:3], in_=lxt[:, TT:TT + 3]), reads=[r_lx[lxi]], writes=[r_lx[1 - lxi]])
            kb.op("act", lambda e: e.activation(out=xcb[:], in_=xc[:], func=AF.Identity), reads=[r_xc], writes=[r_xcb])
            pmr = cnt["mm"] % 2; cnt["mm"] += 1
            kb.op("pe", lambda e: e.matmul(ps_mm[pmr][:, 0:TT], lhsT=wax[:, (u * 2) * 128:(u * 2 + 1) * 128], rhs=xcb[:], start=True, stop=True),
                  reads=[r_wax, r_xcb], writes=[r_pmm[pmr]])
            kb.op("act", lambda e: e.activation(out=rg[:], in_=ps_mm[pmr][:, 0:TT], func=AF.Sigmoid, bias=col(9), scale=1.0),
                  reads=[r_pmm[pmr], r_cf], writes=[r_rg])
            pmi = cnt["mm"] % 2; cnt["mm"] += 1
            kb.op("pe", lambda e: e.matmul(ps_mm[pmi][:, 0:TT], lhsT=wax[:, (u * 2 + 1) * 128:(u * 2 + 2) * 128], rhs=xcb[:], start=True, stop=True),
                  reads=[r_wax, r_xcb], writes=[r_pmm[pmi]])
            kb.op("act", lambda e: e.activation(out=ig[:], in_=ps_mm[pmi][:, 0:TT], func=AF.Sigmoid, bias=col(10), scale=1.0),
                  reads=[r_pmm[pmi], r_cf], writes=[r_ig])
            kb.op("act", lambda e: e.activation(out=aa[:], in_=rg[:], func=AF.Exp, scale=c8[:, u:u + 1]), reads=[r_rg, r_c8], writes=[r_aa])
            kb.op("act", lambda e: e.activation(out=a2[:], in_=rg[:], func=AF.Exp, scale=c8[:, 2 + u:3 + u]), reads=[r_rg, r_c8], writes=[r_a2])
            kb.op("dve", lambda e: e.tensor_scalar(out=a2[:], in0=a2[:], scalar1=-1.0, scalar2=1.0, op0=ALU.mult, op1=ALU.add),
                  reads=[r_a2], writes=[r_a2])
            kb.op("act", lambda e: e.activation(out=a2[:], in_=a2[:], func=AF.Sqrt), reads=[r_a2], writes=[r_a2])
            kb.op("dve", lambda e: e.tensor_tensor(out=ig[:], in0=ig[:], in1=xc[:], op=ALU.mult), reads=[r_ig, r_xc], writes=[r_ig])
            kb.op("dve", lambda e: e.tensor_tensor(out=ig[:], in0=ig[:], in1=a2[:], op=ALU.mult), reads=[r_ig, r_a2], writes=[r_ig])
            hi = T % 2
            init = zero1[:, 0:1] if T == 0 else hh[1 - hi][:, TT - 1:TT]
            kb.op("dve", lambda e, init=init: e.tensor_tensor_scan(out=hh[hi][:], data0=aa[:], data1=ig[:], initial=init, op0=ALU.mult, op1=ALU.add),
                  reads=[r_aa, r_ig, r_z, r_hh[1 - hi]], writes=[r_hh[hi]])
            li = cnt["lo"] % 2; cnt["lo"] += 1
            kb.op("dve", lambda e, li=li: e.tensor_tensor(out=lo[li][:], in0=hh[hi][:], in1=gy[:], op=ALU.mult),
                  reads=[r_hh[hi], r_gy], writes=[r_lo[li]])
            kb.dma("sp", alT[(2 + u) * 128:(3 + u) * 128, T * TT:(T + 1) * TT], lo[li][:], r_lo[li], reads=[r_lo[li]], writes=[r_out], part=True)
    kb.finish([r_out])
    return nc


Q_END, K_END, V_END = 2048, 4096, 6144
F_END = V_END + 16
LX_END = F_END + 2048
LY_END = LX_END + 2048
GA_END = LY_END + D


def _colT(v):
    return np.ascontiguousarray(np.asarray(v, np.float32).reshape(-1, 128).T)


def prep_l0(inp):
    cT = _colT(inp["c"][0])
    w = inp["w_ada"][0]
    b = inp["b_ada"][0]
    maps = []
    for r in range(NCORE):
        sl = slice(r * L0_COLS, (r + 1) * L0_COLS)
        maps.append({"cT": cT, "w": np.ascontiguousarray(w[:, sl]), "b": np.ascontiguousarray(b[None, sl])})
    return maps


def l1_consts():
    tri = np.triu(np.ones((128, 128), np.float32))
    return {
        "ident": np.eye(128, dtype=np.float32).astype(NPBF),
        "cmask": ((1.0 - tri) * np.float32(-1.0e9)).astype(np.float32),
        "identf": np.eye(128, dtype=np.float32),
        "tri": tri,
        "ones": np.concatenate([np.full((128, 128), 1.0 / 128, np.float32), np.ones((128, 128), np.float32)], axis=1),
    }


def prep_l1(inp, mod):
    x = np.ascontiguousarray(inp["x"][0])
    w_in = inp["w_in"][0]
    b_in = inp["b_in"][0]
    consts = l1_consts()
    maps = []
    for r in range(NCORE):
        cf = np.zeros((128, CF_N), np.float32)
        cf[:, CF_SCALE1:CF_SCALE1 + KC] = _colT(mod[D:2 * D])
        cf[:, CF_SHIFT1:CF_SHIFT1 + KC] = _colT(mod[0:D])
        cf[:, CF_G:CF_G + KC] = _colT(inp["norm_mix_g"][0])
        cf[:, CF_GQ] = inp["q_norm_g"][0]
        cf[:, CF_GK] = inp["k_norm_g"][0]
        bvrep = np.zeros((128, 258), np.float32)
        wslab = np.zeros((256, KC * WCOLS), np.float32)
        wax = np.zeros((128, 512), np.float32)
        for u in range(2):
            h = 2 * r + u
            hs = slice(h * 128, (h + 1) * 128)
            base = CF_U + u * CF_UW
            cols = [b_in[hs], b_in[Q_END + h * 128:Q_END + (h + 1) * 128],
                    b_in[F_END + h * 128:F_END + (h + 1) * 128], b_in[LX_END + h * 128:LX_END + (h + 1) * 128],
                    inp["conv_w"][0][0, hs], inp["conv_w"][0][1, hs], inp["conv_w"][0][2, hs], inp["conv_w"][0][3, hs],
                    inp["conv_b"][0][hs], inp["lru_ba"][0][hs], inp["lru_bx"][0][hs], inp["lru_lambda"][0][hs]]
            for k, cvec in enumerate(cols):
                cf[:, base + k] = cvec
            bvrep[:, u * 129:u * 129 + 128] = b_in[None, K_END + h * 128:K_END + (h + 1) * 128]
            bvrep[:, u * 129 + 128] = b_in[V_END + h]
            wcat = np.concatenate([w_in[:, hs], w_in[:, Q_END + h * 128:Q_END + (h + 1) * 128],
                                   w_in[:, F_END + h * 128:F_END + (h + 1) * 128], w_in[:, LX_END + h * 128:LX_END + (h + 1) * 128],
                                   w_in[:, K_END + h * 128:K_END + (h + 1) * 128], w_in[:, V_END + h:V_END + h + 1]], axis=1)
            wslab[u * 128:(u + 1) * 128] = wcat.reshape(KC, 128, WCOLS).transpose(1, 0, 2).reshape(128, KC * WCOLS)
            wax[:, u * 256:u * 256 + 128] = inp["lru_wa"][0][h]
            wax[:, u * 256 + 128:u * 256 + 256] = inp["lru_wx"][0][h]
        m = {"x": x, "cf": cf, "bvrep": bvrep, "wslab": wslab, "wax": wax}
        m.update(consts)
        maps.append(m)
    return maps


TOK2 = S // NCORE
C2_SCALE1, C2_SHIFT1, C2_G1, C2_SCALE2, C2_SHIFT2, C2_G2, C2_BGA, C2_BGL = 0, 32, 64, 96, 128, 160, 192, 224
C2_N = 256
NEG = -1.0e30


def build_l2(nhalf=2, nchunk=16, fz=None):
    from contextlib import ExitStack
    nc = _new_nc()
    kb = KB(nc)
    uid = [0]

    def sbt(stack, name, shape, dt):
        uid[0] += 1
        return stack.enter_context(nc.sbuf_tensor("s_%s_%d" % (name, uid[0]), shape, dt))

    def pst(stack, name, shape, dt):
        uid[0] += 1
        return stack.enter_context(nc.psum_tensor("p_%s_%d" % (name, uid[0]), shape, dt))

    def barrier():
        for k in kb.psem:
            assert not kb.pend[k]
        tickets = [(kb.psem[k], kb.pcnt[k], k) for k in kb.psem if kb.pcnt[k] > 0]
        tickets += [(r.dsem, r.dcnt, "d_" + r.name) for r in allres if r.dsem is not None]
        for e in ("pe", "act", "dve", "pool", "sp"):
            for t in tickets:
                kb._wait(e, t)

    allres = []

    def res(name):
        r = kb.res(name + "_%d" % len(allres))
        allres.append(r)
        return r

    r_al = res("al_d")
    r_mod = res("mod_s")
    cf_d = nc.dram_tensor("cf", [128, C2_N], F32, kind="ExternalInput").ap()
    if fz is None:
        x_d = nc.dram_tensor("x", [TOK2, D], F32, kind="ExternalInput").ap()
        g1_d = nc.dram_tensor("g1rep", [128, D], F32, kind="ExternalInput").ap()
        g2_d = nc.dram_tensor("g2rep", [128, D], F32, kind="ExternalInput").ap()
        al_d = nc.dram_tensor("alT", [KC * 128, TOK2], BF16, kind="ExternalInput").ap()
    else:
        NTL = fz["ntiles"]; NOWN = 4; NU = fz["nunits"]
        SP = NTL * 256
        xp_d = nc.dram_tensor("xpad", [SP, D], F32, kind="ExternalInput").ap()
        x_d = xp_d[SP - TOK2:SP, :]
        cT_d = nc.dram_tensor("cT", [128, KC], F32, kind="ExternalInput").ap()
        wada_d = nc.dram_tensor("w_ada", [D, 6 * D], F32, kind="ExternalInput").ap()
        bada_d = nc.dram_tensor("b_ada", [1, 6 * D], F32, kind="ExternalInput").ap()
        mod_s = nc.dram_tensor("mod_s", [1, 6 * D], F32, kind=("ExternalOutput" if fz.get("dbg") else "Internal")).ap()
        g1_d = mod_s[0:1, 2 * D:3 * D].broadcast_to([128, D])
        g2_d = mod_s[0:1, 5 * D:6 * D].broadcast_to([128, D])
        al_d = nc.dram_tensor("alT", [KC * 128, TOK2], BF16, kind=("ExternalOutput" if fz.get("dbg") else "Internal")).ap()
        hT_s = nc.dram_tensor("hT_s", [NTL * 128, KC * 256], BF16, kind="Internal").ap()
        r_hTs = [res("hTs%d" % t) for t in range(NTL)]
        cfm_d = nc.dram_tensor("cfm", [128, 2 + 16 * CF_UW], F32, kind="ExternalInput").ap()
        bvm_d = nc.dram_tensor("bvrep", [128, 16 * 129], F32, kind="ExternalInput").ap()
        wsl_d = nc.dram_tensor("wslab", [16 * 128, KC * WCOLS], F32, kind="ExternalInput").ap()
        wax_d = nc.dram_tensor("wax", [128, 16 * 256], F32, kind="ExternalInput").ap()
        tm_d = nc.dram_tensor("tmask", [128, NTL], F32, kind="ExternalInput").ap()
        cm_d = nc.dram_tensor("cmask", [128, 128], F32, kind="ExternalInput").ap()
        idf_d = nc.dram_tensor("identf", [128, 128], F32, kind="ExternalInput").ap()
        tri_d = nc.dram_tensor("tri", [128, 128], F32, kind="ExternalInput").ap()
        on_d = nc.dram_tensor("ones", [128, 256], F32, kind="ExternalInput").ap()
    wcat_d = nc.dram_tensor("wcat", [KC * 128, 96 * 128], F32, kind="ExternalInput").ap()
    wout_d = nc.dram_tensor("wout", [8 * 128, KC * 512], F32, kind="ExternalInput").ap()
    wq_d = nc.dram_tensor("wq", [16 * 128, KC * 128], F32, kind="ExternalInput").ap()
    sk_d = nc.dram_tensor("skT", [128, 16 * 128], F32, kind="ExternalInput").ap()
    uT_d = nc.dram_tensor("uT", [128 * 128, KC * 128], F32, kind="ExternalInput").ap()
    v_d = nc.dram_tensor("v", [128 * 128, D], F32, kind="ExternalInput").ap()
    id_d = nc.dram_tensor("ident", [128, 128], BF16, kind="ExternalInput").ap()
    y_d = nc.dram_tensor("y", [TOK2, D], F32, kind="ExternalOutput").ap()
    r_y = [[res("y%d_%d" % (b, n)) for n in range(8)] for b in range(8)]

    top = ExitStack()
    cf = sbt(top, "cf", [128, C2_N], F32); r_cf = res("cf")
    ident = sbt(top, "ident", [128, 128], BF16); r_id = res("ident")
    grep = sbt(top, "grep", [128, D], F32); r_grep = res("grep")
    AB = sbt(top, "AB", [128, 2 * KC], F32); r_AB = res("AB")
    skT = sbt(top, "skT", [128, 16 * 128], BF16); r_sk = res("skT")
    st = sbt(top, "st", [128, 8], F32); r_st = res("st")
    kb.dma("sp", cf[:], cf_d[:, :], r_cf, writes=[r_cf])
    kb.dma("sp", ident[:], id_d[:, :], r_id, writes=[r_id])
    kb.dma("pool", skT[:], sk_d[:, :], r_sk, writes=[r_sk])
    if fz is not None:
        with ExitStack() as sm_:
            ct = sbt(sm_, "ct", [128, KC], F32); r_ct = res("ct")
            scv = sbt(sm_, "scv", [128, KC], F32); r_scv = res("scv")
            bt = sbt(sm_, "bt", [1, D], F32); r_bt = res("bt")
            ot = sbt(sm_, "otm", [1, D], F32); r_otm = res("otm")
            wt = [sbt(sm_, "wt%d" % i, [128, 2, D], F32) for i in range(2)]; r_wt = [res("wt0"), res("wt1")]
            pss = [pst(sm_, "pm%d" % i, [128, 512], F32) for i in range(8)]; r_pss_ = [res("pm%d" % i) for i in range(8)]
            kb.dma("sp", ct[:], cT_d[:, :], r_ct, writes=[r_ct])
            kb.op("act", lambda e: e.activation(out=scv[:], in_=ct[:], func=AF.Silu), reads=[r_ct], writes=[r_scv])
            wv_ = wada_d.rearrange("(k p) n -> p k n", p=128)
            seqm = [(ps_, g) for ps_ in range(6) for g in range(KC // 2)]

            def load_m(i):
                ps_, g = seqm[i]
                kb.dma("sp", wt[i % 2][:], wv_[:, g * 2:(g + 1) * 2, ps_ * D:(ps_ + 1) * D], r_wt[i % 2], writes=[r_wt[i % 2]])

            load_m(0)
            for i, (ps_, g) in enumerate(seqm):
                if i + 1 < len(seqm):
                    load_m(i + 1)
                if g == 0:
                    kb.dma("sp", bt[:], bada_d[:, ps_ * D:(ps_ + 1) * D], r_bt, writes=[r_bt])
                for kk in range(2):
                    kc = g * 2 + kk
                    for n in range(8):
                        kb.op("pe", lambda e, kc=kc, kk=kk, n=n, i=i: e.matmul(pss[n][0:1, :], lhsT=scv[:, kc:kc + 1], rhs=wt[i % 2][:, kk, n * 512:(n + 1) * 512],
                                                                              start=(kc == 0), stop=(kc == KC - 1)),
                              reads=[r_scv, r_wt[i % 2]], writes=[r_pss_[n]], sig=(kc == KC - 1 or (kk == 1 and n == 7)))
                if g == KC // 2 - 1:
                    for n in range(8):
                        kb.op("dve", lambda e, n=n: e.tensor_tensor(out=ot[0:1, n * 512:(n + 1) * 512], in0=pss[n][0:1, :], in1=bt[0:1, n * 512:(n + 1) * 512], op=ALU.add),
                              reads=[r_pss_[n], r_bt], writes=[r_otm])
                    kb.dma("sp", mod_s[:, ps_ * D:(ps_ + 1) * D], ot[:], r_otm, reads=[r_otm], writes=[r_mod], part=True)
            barrier()
        modcol = mod_s.rearrange("o (k p) -> p (o k)", p=128)
        for sec, c0 in ((0, C2_SHIFT1), (1, C2_SCALE1), (3, C2_SHIFT2), (4, C2_SCALE2)):
            kb.dma("sp", cf[:, c0:c0 + KC], modcol[:, sec * KC:(sec + 1) * KC], r_cf, reads=[r_mod], writes=[r_cf], part=True,
                   allow_slow_non_contiguous=True)
    kb.op("dve", lambda e: e.scalar_tensor_tensor(out=AB[:, 0:KC], in0=cf[:, C2_SCALE1:C2_SCALE1 + KC], scalar=1.0,
                                                 in1=cf[:, C2_G1:C2_G1 + KC], op0=ALU.add, op1=ALU.mult), reads=[r_cf], writes=[r_AB])
    kb.op("dve", lambda e: e.scalar_tensor_tensor(out=AB[:, KC:2 * KC], in0=cf[:, C2_SCALE2:C2_SCALE2 + KC], scalar=1.0,
                                                 in1=cf[:, C2_G2:C2_G2 + KC], op0=ALU.add, op1=ALU.mult), reads=[r_cf], writes=[r_AB])
    P = {}
    cnt = dict(tr=0, ev=0, w=0, wo=0, xo=0, mm=0, u=0, vs=0, ga=0, gt=0, act=0, out=0, ot=0, ce=0)

    def norm_T(xb, rx, junk, r_junk, dst_fn, r_dst, acol, bcol):
        ps_tr, r_ptr = P["tr"], P["rtr"]
        kb.op("dve", lambda e: e.memset(st[:, 0:2], 0.0), writes=[r_st])
        for b in range(2):
            kb.op("act", lambda e, b=b: e.activation(out=junk[:], in_=xb[:, b, :], func=AF.Square, scale=1.0 / 64.0,
                                                    accum_out=st[:, b:b + 1]), reads=[rx], writes=[r_junk, r_st])
        kb.op("act", lambda e: e.activation(out=st[:, 2:4], in_=st[:, 0:2], func=AF.Sqrt, bias=EPS, scale=1.0), reads=[r_st], writes=[r_st])
        kb.op("dve", lambda e: e.reciprocal(out=st[:, 4:6], in_=st[:, 2:4]), reads=[r_st], writes=[r_st])
        for b in range(2):
            kb.op("dve", lambda e, b=b: e.tensor_scalar(out=xb[:, b, :], in0=xb[:, b, :], scalar1=st[:, 4 + b:5 + b], scalar2=None,
                                                       op0=ALU.mult), reads=[r_st, rx], writes=[rx])
        for g in range(KC // 4):
            pt = cnt["tr"] % 2; cnt["tr"] += 1
            for k4 in range(4):
                kc = g * 4 + k4
                for b in range(2):
                    kb.op("pe", lambda e, kc=kc, k4=k4, b=b, pt=pt: e.transpose(
                        out=ps_tr[pt][:, k4 * 256 + b * 128:k4 * 256 + (b + 1) * 128], in_=xb[:, b, kc * 128:(kc + 1) * 128], identity=ident[:]),
                        reads=[rx, r_id], writes=[r_ptr[pt]], sig=(k4 == 3 and b == 1))
            for k4 in range(4):
                kc = g * 4 + k4
                dst = dst_fn(kc)
                src = ps_tr[pt][:, k4 * 256:(k4 + 1) * 256]
                if cnt["ev"] % 2 == 0:
                    kb.op("act", lambda e, dst=dst, src=src, kc=kc: e.activation(out=dst, in_=src, func=AF.Identity,
                                                                                scale=AB[:, acol + kc:acol + kc + 1], bias=cf[:, bcol + kc:bcol + kc + 1]),
                          reads=[r_ptr[pt], r_AB, r_cf], writes=[r_dst])
                else:
                    kb.op("dve", lambda e, dst=dst, src=src, kc=kc: e.tensor_scalar(out=dst, in0=src, scalar1=AB[:, acol + kc:acol + kc + 1],
                                                                                   scalar2=cf[:, bcol + kc:bcol + kc + 1], op0=ALU.mult, op1=ALU.add),
                          reads=[r_ptr[pt], r_AB, r_cf], writes=[r_dst])
                cnt["ev"] += 1

    xv = x_d.rearrange("(s b p) d -> s p b d", b=2, p=128)
    yv = y_d.rearrange("(s b p) d -> s p b d", b=2, p=128)
    alv = al_d.rearrange("(k p) t -> p k t", p=128)
    vv = v_d.rearrange("(i j) d -> j i d", j=128)

    if fz is not None:
        with ExitStack() as s0:
            xbs = [sbt(s0, "xb%d" % i, [128, 2, D], BF16) for i in range(2)]; r_xbs = [res("xb0"), res("xb1")]
            junk = sbt(s0, "junk", [128, D], BF16); r_junk = res("junk")
            hTt = [sbt(s0, "hTt%d" % i, [128, KC, 256], BF16) for i in range(2)]; r_hTt = [res("hTt0"), res("hTt1")]
            P["tr"] = [pst(s0, "tr%d" % i, [128, 1024], BF16) for i in range(2)]; P["rtr"] = [res("ptr0"), res("ptr1")]
            xpv = xp_d.rearrange("(s b p) d -> s p b d", b=2, p=128)
            kb.dma("pool", xbs[0][:], xpv[0], r_xbs[0], writes=[r_xbs[0]])
            for T in range(NTL):
                if T + 1 < NTL:
                    kb.dma("pool", xbs[(T + 1) % 2][:], xpv[T + 1], r_xbs[(T + 1) % 2], writes=[r_xbs[(T + 1) % 2]])
                hb = hTt[T % 2]
                norm_T(xbs[T % 2], r_xbs[T % 2], junk, r_junk, lambda kc, hb=hb: hb[:, kc, :], r_hTt[T % 2], 0, C2_SHIFT1)
                kb.dma("sp", hT_s[T * 128:(T + 1) * 128, :].rearrange("p (k t) -> p k t", t=256), hb[:], r_hTt[T % 2],
                       reads=[r_hTt[T % 2]], writes=[r_hTs[T]], part=True)
            barrier()
        with ExitStack() as s1:
            TTm = 256
            cfm = sbt(s1, "cfm", [128, 2 + 16 * CF_UW], F32); r_cfm = res("cfm")
            bv = sbt(s1, "bv", [128, 16 * 129], F32); r_bv = res("bv")
            tmask = sbt(s1, "tmask", [128, NTL], F32); r_tm = res("tmask")
            maskneg = sbt(s1, "maskneg", [128, 128], F32); r_cm = res("maskneg")
            identf = sbt(s1, "identf", [128, 128], F32); r_idf = res("identf")
            tri = sbt(s1, "tri", [128, 128], F32); r_tri = res("tri")
            ones = sbt(s1, "ones", [128, 256], F32); r_on = res("ones")
            c8 = sbt(s1, "c8", [128, 4], F32); r_c8 = res("c8")
            nbias = sbt(s1, "nbias", [128, 2], F32); r_nbias = res("nbias")
            wax = sbt(s1, "wax", [128, 256], BF16); r_wax = res("wax")
            wsl = sbt(s1, "wsl", [128, KC, WCOLS], BF16); r_wsl5 = [res("wsl%d" % i) for i in range(5)]
            hTt = [sbt(s1, "hTm%d" % i, [128, KC, 256], BF16) for i in range(2)]; r_hTt = [res("hTm0"), res("hTm1")]
            KT = sbt(s1, "KT", [128, NTL * 256], BF16); r_KT = res("KT")
            V = sbt(s1, "V", [128, NTL * 2, 129], BF16); r_V = res("V")
            QT = sbt(s1, "QT", [128, 256], BF16); r_QT = res("QT")
            NFT = sbt(s1, "NFT", [128, NTL * 2], F32); r_NFT = res("NFT")
            NR = sbt(s1, "NR", [128, NTL * 2 + 1], F32); r_NR = res("NR")
            fl = sbt(s1, "fl", [128, 12], F32); r_fl = res("fl")
            qf = sbt(s1, "qf", [128, 256], F32); r_qf = res("qf")
            qs = sbt(s1, "qs", [128, 256], F32); r_qs = res("qs")
            Bj = [sbt(s1, "Bj%d" % i, [128, NTL * 2], F32) for i in range(2)]; r_Bj = [res("Bj0"), res("Bj1")]
            NPT = 4
            PT = [sbt(s1, "PT%d" % i, [128, 128], BF16) for i in range(NPT)]; r_PT = [res("PT%d" % i) for i in range(NPT)]
            rec = sbt(s1, "rec", [128, 2], F32); r_rec = res("rec")
            ao = sbt(s1, "ao", [128, 128], BF16); r_ao = res("ao")
            aoT = [sbt(s1, "aoT%d" % i, [128, 256], BF16) for i in range(2)]; r_aoT = [res("aoT0"), res("aoT1")]
            dgn = sbt(s1, "dgn", [128, 128], F32); r_dgn = res("dgn")
            Dm = sbt(s1, "Dm", [128, 128], F32); r_Dm = res("Dm")
            sdg = sbt(s1, "sdg", [128, 128], F32); r_sdg = res("sdg")
            osb = sbt(s1, "osb", [128, 129], F32); r_osb = res("osb")
            lx = [sbt(s1, "lx%d" % i, [128, 256 + 3], F32) for i in range(2)]; r_lx = [res("lx0"), res("lx1")]
            xc = sbt(s1, "xc", [128, 256], F32); r_xc = res("xc")
            xcb = sbt(s1, "xcb", [128, 256], BF16); r_xcb = res("xcb")
            rg = sbt(s1, "rg", [128, 256], F32); r_rg = res("rg")
            ig = sbt(s1, "ig", [128, 256], F32); r_ig = res("ig")
            aa = sbt(s1, "aa", [128, 256], F32); r_aa = res("aa")
            a2 = sbt(s1, "a2", [128, 256], F32); r_a2 = res("a2")
            hh = [sbt(s1, "hh%d" % i, [128, 256], F32) for i in range(2)]; r_hh = [res("hh0"), res("hh1")]
            gy = sbt(s1, "gy", [128, 256], F32); r_gy = res("gy")
            lo = [sbt(s1, "lo%d" % i, [128, 256], BF16) for i in range(2)]; r_lo = [res("lo0"), res("lo1")]
            zero1 = sbt(s1, "zero1", [128, 1], F32); r_z = res("zero1")
            ps_tr = [pst(s1, "tr%d" % i, [128, 1024], BF16) for i in range(1)]; r_ptr = [res("ptr0")]
            ps_mm = [pst(s1, "mm%d" % i, [128, 512], F32) for i in range(3)]; r_pmm = [res("pmm0"), res("pmm1"), res("pmm2")]
            ps_s = [pst(s1, "s%d" % i, [128, 512], F32) for i in range(2)]; r_pss = [res("pss0"), res("pss1")]
            ps_o = [pst(s1, "o%d" % i, [128, 512], F32) for i in range(2)]; r_po = [res("po0"), res("po1")]
            kb.dma("sp", cfm[:], cfm_d[:, :], r_cfm, writes=[r_cfm])
            kb.dma("sp", bv[:], bvm_d[:, :], r_bv, writes=[r_bv])
            kb.dma("sp", tmask[:], tm_d[:, :], r_tm, writes=[r_tm])
            kb.dma("sp", maskneg[:], cm_d[:, :], r_cm, writes=[r_cm])
            kb.dma("sp", identf[:], idf_d[:, :], r_idf, writes=[r_idf])
            kb.dma("sp", tri[:], tri_d[:, :], r_tri, writes=[r_tri])
            kb.dma("sp", ones[:], on_d[:, :], r_on, writes=[r_on])
            kb.op("dve", lambda e: e.memset(zero1[:], 0.0), writes=[r_z])
            SCALE = float(128 ** -0.5)
            mc = dict(mm=0, tr=0, s=0, pt=0, bj=0, lo=0, h=0)
            OWN0 = NTL - NOWN
            for u in range(NU):
                ub = 2 + u * CF_UW
                col = lambda k, ub=ub: cfm[:, ub + k:ub + k + 1]
                wsrc = wsl_d[u * 128:(u + 1) * 128, :].rearrange("p (k c) -> p k c", c=WCOLS)
                for si in (1, 2, 4, 0, 3):
                    c0, c1 = (512, 641) if si == 4 else (si * 128, (si + 1) * 128)
                    kb.dma("pool", wsl[:, :, c0:c1], wsrc[:, :, c0:c1], r_wsl5[si], writes=[r_wsl5[si]])
                kb.dma("pool", wax[:], wax_d[:, u * 256:(u + 1) * 256], r_wax, writes=[r_wax])
                kb.op("act", lambda e: e.activation(out=c8[:, 2:3], in_=col(11), func=AF.Exp, scale=-1.0), reads=[r_cfm], writes=[r_c8])
                kb.op("act", lambda e: e.activation(out=c8[:, 3:4], in_=c8[:, 2:3], func=AF.Ln, bias=1.0, scale=1.0), reads=[r_c8], writes=[r_c8])
                kb.op("dve", lambda e: e.tensor_scalar(out=c8[:, 0:1], in0=c8[:, 3:4], scalar1=-8.0, scalar2=None, op0=ALU.mult), reads=[r_c8], writes=[r_c8])
                kb.op("dve", lambda e: e.tensor_scalar(out=c8[:, 1:2], in0=c8[:, 3:4], scalar1=-16.0, scalar2=None, op0=ALU.mult), reads=[r_c8], writes=[r_c8])
                kb.op("dve", lambda e: e.memset(NR[:, 0:1], 0.0), writes=[r_NR])
                kb.op("dve", lambda e: e.memset(lx[0][:, 0:3], 0.0), writes=[r_lx[0]])
                kb.op("dve", lambda e: e.tensor_scalar(out=nbias[:, 0:2], in0=cfm[:, ub + 9:ub + 11], scalar1=-1.0, scalar2=None, op0=ALU.mult),
                      reads=[r_cfm], writes=[r_nbias])

                def load_h(T):
                    i_ = mc["h"] % 2; mc["h"] += 1
                    kb.dma("sp", hTt[i_][:], hT_s[T * 128:(T + 1) * 128, :].rearrange("p (k t) -> p k t", t=256), r_hTt[i_],
                           reads=[r_hTs[T]], writes=[r_hTt[i_]])
                    return i_

                h_next = load_h(0)
                for T in range(NTL):
                    own = T >= OWN0
                    hi_ = h_next
                    if T + 1 < NTL:
                        h_next = load_h(T + 1)
                    hT = hTt[hi_]; r_hT = r_hTt[hi_]
                    tmc = tmask[:, T:T + 1]

                    def proj(si):
                        pm = mc["mm"] % 3; mc["mm"] += 1
                        for kc in range(KC):
                            kb.op("pe", lambda e, kc=kc, pm=pm: e.matmul(ps_mm[pm][:, 0:256], lhsT=wsl[:, kc, si * 128:(si + 1) * 128], rhs=hT[:, kc, :],
                                                                         start=(kc == 0), stop=(kc == KC - 1)),
                                  reads=[r_wsl5[si], r_hT], writes=[r_pmm[pm]], sig=(kc == KC - 1))
                        return pm

                    def qknorm(pm, bcol, gcol, dst, r_dst):
                        kb.op("act", lambda e: e.activation(out=qf[:], in_=ps_mm[pm][:, 0:256], func=AF.Identity, bias=bcol, scale=1.0),
                              reads=[r_pmm[pm], r_cfm], writes=[r_qf])
                        kb.op("act", lambda e: e.activation(out=qs[:], in_=qf[:], func=AF.Square), reads=[r_qf], writes=[r_qs])
                        px = mc["mm"] % 3; mc["mm"] += 1
                        kb.op("pe", lambda e: e.matmul(ps_mm[px][:, 0:256], lhsT=ones[:, 0:128], rhs=qs[:], start=True, stop=True),
                              reads=[r_on, r_qs], writes=[r_pmm[px]])
                        kb.op("act", lambda e: e.activation(out=qs[:], in_=ps_mm[px][:, 0:256], func=AF.Ln, bias=EPS, scale=1.0),
                              reads=[r_pmm[px]], writes=[r_qs])
                        kb.op("act", lambda e: e.activation(out=qs[:], in_=qs[:], func=AF.Exp, scale=-0.5), reads=[r_qs], writes=[r_qs])
                        kb.op("dve", lambda e: e.scalar_tensor_tensor(out=dst, in0=qf[:], scalar=gcol, in1=qs[:], op0=ALU.mult, op1=ALU.mult),
                              reads=[r_qf, r_qs, r_cfm], writes=[r_dst])

                    pm = proj(1)
                    kb.op("act", lambda e, pm=pm: e.activation(out=qf[:], in_=ps_mm[pm][:, 0:256], func=AF.Identity, bias=col(1), scale=1.0),
                          reads=[r_pmm[pm], r_cfm], writes=[r_qf])
                    kb.op("act", lambda e: e.activation(out=qs[:], in_=qf[:], func=AF.Square), reads=[r_qf], writes=[r_qs])
                    lxi = T % 2
                    lxt = lx[lxi]
                    pm = proj(2)
                    kb.op("act", lambda e, pm=pm: e.activation(out=lx[lxi][:, 3:259], in_=ps_mm[pm][:, 0:256], func=AF.Identity, bias=col(2), scale=1.0),
                          reads=[r_pmm[pm], r_cfm], writes=[r_lx[lxi]])
                    kb.op("dve", lambda e: e.tensor_scalar(out=lx[lxi][:, 3:259], in0=lx[lxi][:, 3:259], scalar1=tmc, scalar2=None, op0=ALU.mult),
                          reads=[r_lx[lxi], r_tm], writes=[r_lx[lxi]])
                    kb.op("dve", lambda e: e.tensor_scalar(out=xc[:], in0=lxt[:, 0:256], scalar1=col(4), scalar2=col(8), op0=ALU.mult, op1=ALU.add),
                          reads=[r_lx[lxi], r_cfm], writes=[r_xc])
                    for k in range(1, 4):
                        kb.op("dve", lambda e, k=k: e.scalar_tensor_tensor(out=xc[:], in0=lxt[:, k:k + 256], scalar=col(4 + k), in1=xc[:],
                                                                          op0=ALU.mult, op1=ALU.add),
                              reads=[r_lx[lxi], r_cfm, r_xc], writes=[r_xc])
                    kb.op("pool", lambda e: e.tensor_copy(out=lx[1 - lxi][:, 0:3], in_=lxt[:, 256:259]), reads=[r_lx[lxi]], writes=[r_lx[1 - lxi]])
                    kb.op("act", lambda e: e.activation(out=xcb[:], in_=xc[:], func=AF.Identity), reads=[r_xc], writes=[r_xcb])
                    for b in range(2):
                        blk = T * 2 + b
                        pm = mc["mm"] % 3; mc["mm"] += 1
                        for kc in range(KC):
                            kb.op("pe", lambda e, kc=kc, pm=pm, b=b: e.matmul(ps_mm[pm][:, 0:129], lhsT=hT[:, kc, b * 128:(b + 1) * 128],
                                                                              rhs=wsl[:, kc, 512:641], start=(kc == 0), stop=(kc == KC - 1)),
                                  reads=[r_wsl5[4], r_hT], writes=[r_pmm[pm]], sig=(kc == KC - 1))
                        kb.op("dve", lambda e, pm=pm, blk=blk: e.tensor_tensor(out=V[:, blk, 0:128], in0=ps_mm[pm][:, 0:128],
                                                                               in1=bv[:, u * 129:u * 129 + 128], op=ALU.add),
                              reads=[r_pmm[pm], r_bv], writes=[r_V])
                        kb.op("dve", lambda e, blk=blk: e.tensor_scalar(out=V[:, blk, 0:128], in0=V[:, blk, 0:128], scalar1=tmc, scalar2=None, op0=ALU.mult),
                              reads=[r_V, r_tm], writes=[r_V])
                        kb.op("dve", lambda e, blk=blk: e.tensor_copy(out=V[:, blk, 128:129], in_=tmc), reads=[r_tm], writes=[r_V])
                        kb.op("dve", lambda e, pm=pm, b=b: e.tensor_tensor(out=fl[:, b:b + 1], in0=ps_mm[pm][:, 128:129],
                                                                           in1=bv[:, u * 129 + 128:u * 129 + 129], op=ALU.add),
                              reads=[r_pmm[pm], r_bv], writes=[r_fl])
                    px = mc["mm"] % 3; mc["mm"] += 1
                    kb.op("pe", lambda e: e.matmul(ps_mm[px][:, 0:256], lhsT=ones[:, 0:128], rhs=qs[:], start=True, stop=True),
                          reads=[r_on, r_qs], writes=[r_pmm[px]])
                    kb.op("act", lambda e: e.activation(out=qs[:], in_=ps_mm[px][:, 0:256], func=AF.Ln, bias=EPS, scale=1.0),
                          reads=[r_pmm[px]], writes=[r_qs])
                    kb.op("act", lambda e: e.activation(out=qs[:], in_=qs[:], func=AF.Exp, scale=-0.5), reads=[r_qs], writes=[r_qs])
                    kb.op("dve", lambda e: e.scalar_tensor_tensor(out=KT[:, T * 256:(T + 1) * 256], in0=qf[:], scalar=cfm[:, 1:2], in1=qs[:], op0=ALU.mult, op1=ALU.mult),
                          reads=[r_qf, r_qs, r_cfm], writes=[r_KT])
                    pmr = mc["mm"] % 3; mc["mm"] += 1
                    kb.op("pe", lambda e: e.matmul(ps_mm[pmr][:, 0:256], lhsT=wax[:, 0:128], rhs=xcb[:], start=True, stop=True),
                          reads=[r_wax, r_xcb], writes=[r_pmm[pmr]])
                    pmi = mc["mm"] % 3; mc["mm"] += 1
                    kb.op("pe", lambda e: e.matmul(ps_mm[pmi][:, 0:256], lhsT=wax[:, 128:256], rhs=xcb[:], start=True, stop=True),
                          reads=[r_wax, r_xcb], writes=[r_pmm[pmi]])
                    kb.op("act", lambda e: e.activation(out=fl[:, 4:6], in_=fl[:, 0:2], func=AF.Exp, scale=-1.0), reads=[r_fl], writes=[r_fl])
                    kb.op("act", lambda e: e.activation(out=fl[:, 8:10], in_=fl[:, 4:6], func=AF.Ln, bias=1.0, scale=1.0), reads=[r_fl], writes=[r_fl])
                    kb.op("act", lambda e: e.activation(out=rg[:], in_=ps_mm[pmr][:, 0:256], func=AF.Exp, bias=nbias[:, 0:1], scale=-1.0),
                          reads=[r_pmm[pmr], r_nbias], writes=[r_rg])
                    kb.op("act", lambda e: e.activation(out=ig[:], in_=ps_mm[pmi][:, 0:256], func=AF.Exp, bias=nbias[:, 1:2], scale=-1.0),
                          reads=[r_pmm[pmi], r_nbias], writes=[r_ig])
                    pxc = mc["mm"] % 3; mc["mm"] += 1
                    ps_x = ps_mm[pxc]; r_px = r_pmm[pxc]
                    kb.op("pe", lambda e: e.matmul(ps_x[:, 0:2], lhsT=tri[:], rhs=fl[:, 8:10], start=True, stop=True),
                          reads=[r_tri, r_fl], writes=[r_px], sig=False)
                    kb.op("pe", lambda e: e.matmul(ps_x[:, 4:6], lhsT=ones[:, 128:256], rhs=fl[:, 8:10], start=True, stop=True),
                          reads=[r_on, r_fl], writes=[r_px])
                    for b in range(2):
                        blk = T * 2 + b
                        kb.op("dve", lambda e, b=b, blk=blk: e.tensor_tensor(out=NR[:, blk + 1:blk + 2], in0=NR[:, blk:blk + 1],
                                                                             in1=ps_x[:, 4 + b:5 + b], op=ALU.add),
                              reads=[r_px, r_NR], writes=[r_NR])
                    kb.op("dve", lambda e: e.tensor_tensor(out=NFT[:, T * 2:T * 2 + 2], in0=ps_x[:, 0:2], in1=NR[:, T * 2:T * 2 + 2], op=ALU.add),
                          reads=[r_px, r_NR], writes=[r_NFT])
                    kb.op("act", lambda e: e.activation(out=rg[:], in_=rg[:], func=AF.Ln, bias=1.0, scale=1.0), reads=[r_rg], writes=[r_rg])
                    kb.op("act", lambda e: e.activation(out=rg[:], in_=rg[:], func=AF.Exp, scale=-1.0), reads=[r_rg], writes=[r_rg])
                    kb.op("act", lambda e: e.activation(out=ig[:], in_=ig[:], func=AF.Ln, bias=1.0, scale=1.0), reads=[r_ig], writes=[r_ig])
                    kb.op("act", lambda e: e.activation(out=ig[:], in_=ig[:], func=AF.Exp, scale=-1.0), reads=[r_ig], writes=[r_ig])
                    kb.op("act", lambda e: e.activation(out=aa[:], in_=rg[:], func=AF.Exp, scale=c8[:, 0:1]), reads=[r_rg, r_c8], writes=[r_aa])
                    kb.op("act", lambda e: e.activation(out=a2[:], in_=rg[:], func=AF.Exp, scale=c8[:, 1:2]), reads=[r_rg, r_c8], writes=[r_a2])
                    kb.op("act", lambda e: e.activation(out=a2[:], in_=a2[:], func=AF.Ln, bias=1.0, scale=-1.0), reads=[r_a2], writes=[r_a2])
                    kb.op("act", lambda e: e.activation(out=a2[:], in_=a2[:], func=AF.Exp, scale=0.5), reads=[r_a2], writes=[r_a2])
                    kb.op("dve", lambda e: e.scalar_tensor_tensor(out=ig[:], in0=ig[:], scalar=tmc, in1=xc[:], op0=ALU.mult, op1=ALU.mult),
                          reads=[r_ig, r_xc, r_tm], writes=[r_ig])
                    kb.op("dve", lambda e: e.tensor_tensor(out=ig[:], in0=ig[:], in1=a2[:], op=ALU.mult), reads=[r_ig, r_a2], writes=[r_ig])
                    hi = T % 2
                    init = zero1[:, 0:1] if T == 0 else hh[1 - hi][:, 255:256]
                    kb.op("dve", lambda e, init=init: e.tensor_tensor_scan(out=hh[hi][:], data0=aa[:], data1=ig[:], initial=init, op0=ALU.mult, op1=ALU.add),
                          reads=[r_aa, r_ig, r_z, r_hh[1 - hi]], writes=[r_hh[hi]])
                    if own:
                        pm = proj(0)
                        qknorm(pm, col(0), cfm[:, 0:1], QT[:], r_QT)
                        pm = proj(3)
                        kb.op("act", lambda e, pm=pm: e.activation(out=gy[:], in_=ps_mm[pm][:, 0:256], func=AF.Gelu, bias=col(3), scale=1.0),
                              reads=[r_pmm[pm], r_cfm], writes=[r_gy])
                        li = mc["lo"] % 2; mc["lo"] += 1
                        kb.op("dve", lambda e, li=li: e.tensor_tensor(out=lo[li][:], in0=hh[hi][:], in1=gy[:], op=ALU.mult),
                              reads=[r_hh[hi], r_gy], writes=[r_lo[li]])
                        tcol = (T - OWN0) * 256
                        kb.dma("sp", al_d[(16 + u) * 128:(17 + u) * 128, tcol:tcol + 256], lo[li][:], r_lo[li], reads=[r_lo[li]], writes=[r_al], part=True)

                    if own:
                        groups = []
                        for jq in range(2):
                            j = T * 2 + jq
                            ks = list(range(j))
                            for g0 in range(0, len(ks), 4):
                                groups.append((jq, ks[g0:g0 + 4]))
                        bj_of = {}
                        sb_of = {}

                        def emit_s(n):
                            jq, ks = groups[n]
                            j = T * 2 + jq
                            if ks[0] == 0:
                                bi = mc["bj"] % 2; mc["bj"] += 1
                                bj_of[jq] = bi
                                kb.op("dve", lambda e, bi=bi, j=j: e.tensor_scalar(out=Bj[bi][:, 0:j], in0=NFT[:, 0:j], scalar1=NR[:, j:j + 1],
                                                                                  scalar2=None, op0=ALU.subtract),
                                      reads=[r_NFT, r_NR], writes=[r_Bj[bi]])
                            sbk = mc["s"] % 2; mc["s"] += 1
                            sb_of[n] = sbk
                            for kk, i in enumerate(ks):
                                kb.op("pe", lambda e, sbk=sbk, kk=kk, i=i, jq=jq: e.matmul(ps_s[sbk][:, kk * 128:(kk + 1) * 128],
                                                                                          lhsT=KT[:, i * 128:(i + 1) * 128], rhs=QT[:, jq * 128:(jq + 1) * 128],
                                                                                          start=True, stop=True),
                                      reads=[r_KT, r_QT], writes=[r_pss[sbk]], sig=(kk == len(ks) - 1))

                        def diag_block(jq):
                            j = T * 2 + jq
                            oi = jq % 2
                            kb.op("dve", lambda e: e.tensor_scalar(out=dgn[:], in0=identf[:], scalar1=NFT[:, j:j + 1], scalar2=None, op0=ALU.mult),
                                  reads=[r_idf, r_NFT], writes=[r_dgn])
                            px = mc["mm"] % 3; mc["mm"] += 1
                            kb.op("pe", lambda e: e.matmul(ps_mm[px][:, 0:128], lhsT=ones[:, 128:256], rhs=dgn[:], start=True, stop=True),
                                  reads=[r_on, r_dgn], writes=[r_pmm[px]])
                            kb.op("dve", lambda e: e.tensor_scalar(out=Dm[:], in0=ps_mm[px][:, 0:128], scalar1=NFT[:, j:j + 1], scalar2=-1.0,
                                                                  op0=ALU.subtract, op1=ALU.mult), reads=[r_pmm[px], r_NFT], writes=[r_Dm])
                            kb.op("dve", lambda e: e.tensor_tensor(out=Dm[:], in0=Dm[:], in1=maskneg[:], op=ALU.add), reads=[r_Dm, r_cm], writes=[r_Dm])
                            sbk = mc["s"] % 2; mc["s"] += 1
                            kb.op("pe", lambda e: e.matmul(ps_s[sbk][:, 0:128], lhsT=KT[:, j * 128:(j + 1) * 128], rhs=QT[:, jq * 128:(jq + 1) * 128],
                                                           start=True, stop=True), reads=[r_KT, r_QT], writes=[r_pss[sbk]])
                            kb.op("dve", lambda e: e.scalar_tensor_tensor(out=sdg[:], in0=ps_s[sbk][:, 0:128], scalar=SCALE, in1=Dm[:], op0=ALU.mult, op1=ALU.add),
                                  reads=[r_pss[sbk], r_Dm], writes=[r_sdg])
                            pi = mc["pt"] % NPT; mc["pt"] += 1
                            kb.op("act", lambda e: e.activation(out=PT[pi][:], in_=sdg[:], func=AF.Exp), reads=[r_sdg], writes=[r_PT[pi]])
                            kb.op("pe", lambda e: e.matmul(ps_o[oi][:, 256:385], lhsT=PT[pi][:], rhs=V[:, j, :], start=True, stop=True),
                                  reads=[r_PT[pi], r_V], writes=[r_po[oi]])
                            if j > 0:
                                kb.op("act", lambda e: e.activation(out=rec[:, 1:2], in_=NFT[:, j:j + 1], func=AF.Exp, scale=-1.0, bias=NR[:, j:j + 1]),
                                      reads=[r_NFT, r_NR], writes=[r_rec])
                                kb.op("act", lambda e: e.activation(out=osb[:], in_=ps_o[oi][:, 256:385], func=AF.Identity), reads=[r_po[oi]], writes=[r_osb])
                                kb.op("dve", lambda e: e.scalar_tensor_tensor(out=osb[:], in0=ps_o[oi][:, 0:129], scalar=rec[:, 1:2], in1=osb[:],
                                                                             op0=ALU.mult, op1=ALU.add), reads=[r_po[oi], r_rec, r_osb], writes=[r_osb])
                            else:
                                kb.op("act", lambda e: e.activation(out=osb[:], in_=ps_o[oi][:, 256:385], func=AF.Identity), reads=[r_po[oi]], writes=[r_osb])
                            ai = T % 2
                            kb.op("dve", lambda e: e.reciprocal(out=rec[:, 0:1], in_=osb[:, 128:129]), reads=[r_osb], writes=[r_rec])
                            kb.op("dve", lambda e: e.tensor_scalar(out=ao[:], in0=osb[:, 0:128], scalar1=rec[:, 0:1], scalar2=None, op0=ALU.mult),
                                  reads=[r_osb, r_rec], writes=[r_ao])
                            pt = 0
                            kb.op("pe", lambda e, pt=pt: e.transpose(out=ps_tr[pt][:, 0:128], in_=ao[:], identity=ident[:]),
                                  reads=[r_ao, r_id], writes=[r_ptr[pt]])
                            kb.op("dve", lambda e, pt=pt: e.tensor_copy(out=aoT[ai][:, jq * 128:(jq + 1) * 128], in_=ps_tr[pt][:, 0:128]),
                                  reads=[r_ptr[pt]], writes=[r_aoT[ai]])

                        if groups:
                            emit_s(0)
                        done_diag = set()
                        for n, (jq, ks) in enumerate(groups):
                            j = T * 2 + jq
                            if n + 1 < len(groups):
                                emit_s(n + 1)
                            sbk = sb_of.pop(n)
                            bi = bj_of[jq]
                            oi = jq % 2
                            for kk, i in enumerate(ks):
                                pi = mc["pt"] % NPT; mc["pt"] += 1
                                kb.op("act", lambda e, sbk=sbk, kk=kk, pi=pi, bi=bi, i=i: e.activation(
                                    out=PT[pi][:], in_=ps_s[sbk][:, kk * 128:(kk + 1) * 128], func=AF.Exp,
                                    scale=SCALE, bias=Bj[bi][:, i:i + 1]),
                                    reads=[r_pss[sbk], r_Bj[bi]], writes=[r_PT[pi]])
                                kb.op("pe", lambda e, pi=pi, i=i, oi=oi, j=j: e.matmul(ps_o[oi][:, 0:129], lhsT=PT[pi][:], rhs=V[:, i, :],
                                                                                        start=(i == 0), stop=(i == j - 1)),
                                      reads=[r_PT[pi], r_V], writes=[r_po[oi]])
                                if i == j - 1:
                                    diag_block(jq)
                                    done_diag.add(jq)
                        for jq in range(2):
                            if jq not in done_diag:
                                diag_block(jq)
                        ai = T % 2
                        tcol = (T - OWN0) * 256
                        kb.dma("sp", al_d[u * 128:(u + 1) * 128, tcol:tcol + 256], aoT[ai][:], r_aoT[ai], reads=[r_aoT[ai]], writes=[r_al], part=True)

            barrier()
    for hf in range(nhalf):
        with ExitStack() as sa:
            xb = sbt(sa, "xb", [128, 2, D], BF16); r_xb = res("xb")
            junk = sbt(sa, "junk", [128, D], BF16); r_junk = res("junk")
            hT = sbt(sa, "hT", [128, KC, 256], BF16); r_hT = res("hT")
            alq = sbt(sa, "alq", [128, KC, 256], BF16); r_alq = res("alq")
            mT = sbt(sa, "mT", [128, KC, 256], BF16); r_mT = res("mT")
            NW = 4
            wp = [sbt(sa, "wp%d" % i, [128, KC, 128], BF16) for i in range(NW)]; r_wp = [res("wp%d" % i) for i in range(NW)]
            wo = [sbt(sa, "wo%d" % i, [128, 8, 512], BF16) for i in range(2)]; r_wo = [res("wo%d" % i) for i in range(2)]
            sg = sbt(sa, "sg", [128, 256], F32); r_sg = res("sg")
            t1 = sbt(sa, "t1", [128, 256], F32); r_t1 = res("t1")
            xt = [sbt(sa, "xt%d" % i, [128, 512], F32) for i in range(2)]; r_xt = [res("xt0"), res("xt1")]
            ot = [sbt(sa, "ot%d" % i, [128, 512], F32) for i in range(2)]; r_ot = [res("ot0"), res("ot1")]
            P["tr"] = [pst(sa, "tr%d" % i, [128, 1024], BF16) for i in range(2)]; P["rtr"] = [res("ptr0"), res("ptr1")]
            pacc = [pst(sa, "acc%d" % i, [128, 512], F32) for i in range(4)]; r_pacc = [res("pacc%d" % i) for i in range(4)]
            pout = [pst(sa, "out%d" % i, [128, 512], F32) for i in range(2)]; r_pout = [res("pout0"), res("pout1")]
            kb.dma("sp", grep[:], g1_d, r_grep, reads=([r_mod] if fz is not None else []), writes=[r_grep])
            for q in range(2):
                qi = hf * 2 + q
                t0 = qi * 256
                kb.dma("pool", xb[:], xv[qi], r_xb, writes=[r_xb])
                kb.dma("sp", alq[:], alv[:, :, t0:t0 + 256], r_alq, reads=[r_al], writes=[r_alq])
                norm_T(xb, r_xb, junk, r_junk, lambda kc: hT[:, kc, :], r_hT, 0, C2_SHIFT1)
                slabs = [(c, m) for c in range(KC) for m in range(3)]

                def load_slab(n):
                    c, m = slabs[n]
                    slot = cnt["w"] % NW; cnt["w"] += 1
                    src = wcat_d[c * 128:(c + 1) * 128, m * 32 * 128:(m + 1) * 32 * 128].rearrange("p (k j) -> p k j", j=128)
                    kb.dma("pool", wp[slot][:], src, r_wp[slot], writes=[r_wp[slot]])
                    return slot

                slot_of = {}
                for n in range(min(3, len(slabs))):
                    slot_of[n] = load_slab(n)
                for n, (c, m) in enumerate(slabs):
                    if n + 3 < len(slabs):
                        slot_of[n + 3] = load_slab(n + 3)
                    sl = slot_of.pop(n)
                    if m < 2:
                        for kc in range(KC):
                            kb.op("pe", lambda e, kc=kc, sl=sl, m=m: e.matmul(pacc[m][:, 0:256], lhsT=wp[sl][:, kc, :], rhs=hT[:, kc, :],
                                                                              start=(kc == 0), stop=(kc == KC - 1)),
                                  reads=[r_wp[sl], r_hT], writes=[r_pacc[m]], sig=(kc == KC - 1))
                    else:
                        for br in range(2):
                            for kk in range(16):
                                kc = br * 16 + kk
                                kb.op("pe", lambda e, kc=kc, kk=kk, sl=sl, br=br: e.matmul(pacc[2 + br][:, 0:256], lhsT=wp[sl][:, kc, :], rhs=alq[:, kc, :],
                                                                                        start=(kk == 0), stop=(kk == 15)),
                                      reads=[r_wp[sl], r_alq], writes=[r_pacc[2 + br]], sig=(kk == 15))
                        kb.op("act", lambda e, c=c: e.activation(out=sg[:], in_=pacc[0][:, 0:256], func=AF.Sigmoid,
                                                                bias=cf[:, C2_BGA + c:C2_BGA + c + 1], scale=1.0), reads=[r_pacc[0], r_cf], writes=[r_sg])
                        kb.op("dve", lambda e: e.tensor_tensor(out=t1[:], in0=sg[:], in1=pacc[2][:, 0:256], op=ALU.mult),
                              reads=[r_sg, r_pacc[2]], writes=[r_t1])
                        kb.op("act", lambda e, c=c: e.activation(out=sg[:], in_=pacc[1][:, 0:256], func=AF.Sigmoid,
                                                                bias=cf[:, C2_BGL + c:C2_BGL + c + 1], scale=1.0), reads=[r_pacc[1], r_cf], writes=[r_sg])
                        kb.op("dve", lambda e: e.tensor_tensor(out=sg[:], in0=sg[:], in1=pacc[3][:, 0:256], op=ALU.mult),
                              reads=[r_sg, r_pacc[3]], writes=[r_sg])
                        kb.op("dve", lambda e, c=c: e.tensor_tensor(out=mT[:, c, :], in0=sg[:], in1=t1[:], op=ALU.add),
                              reads=[r_sg, r_t1], writes=[r_mT])
                def load_wo(n, kg):
                    slot = cnt["wo"] % 2; cnt["wo"] += 1
                    src = wout_d[n * 128:(n + 1) * 128, kg * 8 * 512:(kg + 1) * 8 * 512].rearrange("p (k j) -> p k j", j=512)
                    kb.dma("pool", wo[slot][:], src, r_wo[slot], writes=[r_wo[slot]])
                    return slot

                seq = [(n, kg) for n in range(8) for kg in range(4)]
                nxt = load_wo(*seq[0])
                for idx, (n, kg) in enumerate(seq):
                    sl = nxt
                    if idx + 1 < len(seq):
                        nxt = load_wo(*seq[idx + 1])
                    for b in range(2):
                        for kk in range(8):
                            kc = kg * 8 + kk
                            kb.op("pe", lambda e, b=b, kk=kk, kc=kc, sl=sl: e.matmul(pout[b][:], lhsT=mT[:, kc, b * 128:(b + 1) * 128], rhs=wo[sl][:, kk, :],
                                                                                    start=(kc == 0), stop=(kc == KC - 1)),
                                  reads=[r_mT, r_wo[sl]], writes=[r_pout[b]], sig=(kk == 7))
                    if kg == 3:
                        for b in range(2):
                            blk = qi * 2 + b
                            xs = cnt["xo"] % 2; cnt["xo"] += 1
                            kb.dma("sp", xt[xs][:], x_d[blk * 128:(blk + 1) * 128, n * 512:(n + 1) * 512], r_xt[xs], writes=[r_xt[xs]])
                            kb.op("dve", lambda e, b=b, n=n, xs=xs: e.tensor_tensor(out=ot[xs][:], in0=pout[b][:], in1=grep[:, n * 512:(n + 1) * 512], op=ALU.mult),
                                  reads=[r_pout[b], r_grep], writes=[r_ot[xs]])
                            kb.op("dve", lambda e, xs=xs: e.tensor_tensor(out=ot[xs][:], in0=ot[xs][:], in1=xt[xs][:], op=ALU.add),
                                  reads=[r_ot[xs], r_xt[xs]], writes=[r_ot[xs]])
                            kb.dma("sp", y_d[blk * 128:(blk + 1) * 128, n * 512:(n + 1) * 512], ot[xs][:], r_ot[xs], reads=[r_ot[xs]],
                                   writes=[r_y[blk][n]], part=True)
            barrier()
        with ExitStack() as sb_:
            h2T = sbt(sb_, "h2T", [128, KC, 512], BF16); r_h2T = res("h2T")
            Ssc = sbt(sb_, "Ssc", [128, 4, 16, 128], F32); r_S = res("S")
            tau = sbt(sb_, "tau", [128, 32], F32); r_tau = res("tau")
            nbv = sbt(sb_, "nbv", [128, 32], F32); r_nb = res("nb")
            kb.dma("sp", grep[:], g2_d, r_grep, reads=([r_mod] if fz is not None else []), writes=[r_grep])
            with ExitStack() as sq:
                xb = sbt(sq, "xb", [128, 2, D], BF16); r_xb = res("xb")
                junk = sbt(sq, "junk", [128, D], BF16); r_junk = res("junk")
                qT = sbt(sq, "qT", [128, 16, 512], BF16); r_qT = res("qT")
                wp = [sbt(sq, "wq%d" % i, [128, KC, 128], BF16) for i in range(2)]; r_wp = [res("wq0"), res("wq1")]
                t16 = sbt(sq, "t16", [128, 2, 16], F32); r_t16 = res("t16")
                tmp = sbt(sq, "tmp", [128, 128], F32); r_tmp = res("tmp")
                cand = sbt(sq, "cand", [128, 256], F32); r_cand = res("cand")
                tmp2 = sbt(sq, "tmp2", [128, 256], F32); r_tmp2 = res("tmp2")
                b16 = sbt(sq, "b16", [128, 16], F32); r_b16 = res("b16")
                sm = sbt(sq, "sm", [128, 24], F32); r_sm = res("sm")
                P["tr"] = [pst(sq, "tr%d" % i, [128, 1024], BF16) for i in range(2)]; P["rtr"] = [res("ptr0"), res("ptr1")]
                pq = [pst(sq, "pq%d" % i, [128, 512], F32) for i in range(2)]; r_pq = [res("pq0"), res("pq1")]
                for sub in range(2):
                    qi = hf * 2 + sub
                    rys = [r_y[qi * 2 + b][n] for b in range(2) for n in range(8)]
                    kb.dma("pool", xb[:], yv[qi], r_xb, reads=rys, writes=[r_xb])
                    norm_T(xb, r_xb, junk, r_junk, lambda kc, sub=sub: h2T[:, kc, sub * 256:(sub + 1) * 256], r_h2T, KC, C2_SHIFT2)
                def load_wq(hp):
                    kb.dma("pool", wp[hp % 2][:], wq_d[hp * 128:(hp + 1) * 128, :].rearrange("p (k j) -> p k j", j=128), r_wp[hp % 2], writes=[r_wp[hp % 2]])
                load_wq(0)
                for hp in range(16):
                    if hp + 1 < 16:
                        load_wq(hp + 1)
                    for kc in range(KC):
                        kb.op("pe", lambda e, kc=kc, hp=hp: e.matmul(pq[hp % 2][:], lhsT=wp[hp % 2][:, kc, :], rhs=h2T[:, kc, :], start=(kc == 0), stop=(kc == KC - 1)),
                              reads=[r_wp[hp % 2], r_h2T], writes=[r_pq[hp % 2]], sig=(kc == KC - 1))
                    kb.op("act", lambda e, hp=hp: e.activation(out=qT[:, hp, :], in_=pq[hp % 2][:], func=AF.Identity), reads=[r_pq[hp % 2]], writes=[r_qT])
                for blk in range(4):
                    for g4 in range(4):
                        pb = cnt["mm"] % 2; cnt["mm"] += 1
                        for k in range(4):
                            hp = g4 * 4 + k
                            kb.op("pe", lambda e, hp=hp, k=k, blk=blk, pb=pb: e.matmul(pq[pb][:, k * 128:(k + 1) * 128], lhsT=qT[:, hp, blk * 128:(blk + 1) * 128],
                                                                                       rhs=skT[:, hp * 128:(hp + 1) * 128], start=True, stop=True),
                                  reads=[r_qT, r_sk], writes=[r_pq[pb]], sig=(k == 3))
                        kb.op("act", lambda e, blk=blk, g4=g4, pb=pb: e.activation(out=Ssc[:, blk, g4 * 4:(g4 + 1) * 4, :], in_=pq[pb][:].rearrange("p (a b) -> p a b", b=128),
                                                                                   func=AF.Identity), reads=[r_pq[pb]], writes=[r_S])
                for blk in range(4):
                    for h in range(8):
                        colx = blk * 8 + h
                        for p in range(2):
                            sp_ = Ssc[:, blk, 2 * h + p, :]
                            kb.op("dve", lambda e, p=p, sp_=sp_: e.max(out=t16[:, p, 0:8], in_=sp_), reads=[r_S], writes=[r_t16])
                            kb.op("dve", lambda e, p=p, sp_=sp_: e.match_replace(out=tmp[:], in_to_replace=t16[:, p, 0:8], in_values=sp_, imm_value=NEG),
                                  reads=[r_S, r_t16], writes=[r_tmp])
                            kb.op("dve", lambda e, p=p: e.max(out=t16[:, p, 8:16], in_=tmp[:]), reads=[r_tmp], writes=[r_t16])
                        kb.op("dve", lambda e: e.tensor_tensor(out=cand[:].rearrange("p (a b) -> p a b", b=16),
                                                               in0=t16[:, 0, :].unsqueeze(2).broadcast_to([128, 16, 16]),
                                                               in1=t16[:, 1, :].unsqueeze(1).broadcast_to([128, 16, 16]), op=ALU.add),
                              reads=[r_t16], writes=[r_cand])
                        kb.op("dve", lambda e: e.max(out=b16[:, 0:8], in_=cand[:]), reads=[r_cand], writes=[r_b16])
                        kb.op("dve", lambda e: e.match_replace(out=tmp2[:], in_to_replace=b16[:, 0:8], in_values=cand[:], imm_value=NEG),
                              reads=[r_cand, r_b16], writes=[r_tmp2])
                        kb.op("dve", lambda e: e.max(out=b16[:, 8:16], in_=tmp2[:]), reads=[r_tmp2], writes=[r_b16])
                        kb.op("dve", lambda e, colx=colx: e.tensor_copy(out=tau[:, colx:colx + 1], in_=b16[:, 15:16]), reads=[r_b16], writes=[r_tau])
                        kb.op("dve", lambda e: e.tensor_scalar(out=sm[:, 0:1], in0=b16[:, 0:1], scalar1=-1.0, scalar2=None, op0=ALU.mult), reads=[r_b16], writes=[r_sm])
                        kb.op("dve", lambda e: e.memset(sm[:, 1:2], 0.0), writes=[r_sm])
                        kb.op("act", lambda e: e.activation(out=sm[:, 4:20], in_=b16[:], func=AF.Exp, bias=sm[:, 0:1], scale=1.0, accum_out=sm[:, 1:2]),
                              reads=[r_b16, r_sm], writes=[r_sm])
                        kb.op("act", lambda e: e.activation(out=sm[:, 2:3], in_=sm[:, 1:2], func=AF.Ln), reads=[r_sm], writes=[r_sm])
                        kb.op("dve", lambda e, colx=colx: e.tensor_tensor(out=nbv[:, colx:colx + 1], in0=sm[:, 0:1], in1=sm[:, 2:3], op=ALU.subtract),
                              reads=[r_sm], writes=[r_nb])
                        kb.op("act", lambda e, blk=blk, h=h, colx=colx: e.activation(out=Ssc[:, blk, 2 * h, :], in_=Ssc[:, blk, 2 * h, :], func=AF.Exp,
                                                                                    bias=nbv[:, colx:colx + 1], scale=1.0), reads=[r_S, r_nb], writes=[r_S])
                        kb.op("act", lambda e, blk=blk, h=h: e.activation(out=Ssc[:, blk, 2 * h + 1, :], in_=Ssc[:, blk, 2 * h + 1, :], func=AF.Exp),
                              reads=[r_S], writes=[r_S])
                        kb.op("act", lambda e, colx=colx: e.activation(out=tau[:, colx:colx + 1], in_=tau[:, colx:colx + 1], func=AF.Exp,
                                                                      bias=nbv[:, colx:colx + 1], scale=1.0), reads=[r_tau, r_nb], writes=[r_tau])
                        kb.op("dve", lambda e, colx=colx: e.tensor_scalar(out=tau[:, colx:colx + 1], in0=tau[:, colx:colx + 1], scalar1=0.9999, scalar2=None, op0=ALU.mult),
                              reads=[r_tau], writes=[r_tau])
                barrier()
            with ExitStack() as sc_:
                uTs = [sbt(sc_, "uT%d" % i, [128, KC, 128], BF16) for i in range(2)]; r_uT = [res("uT0"), res("uT1")]
                GTs2 = [sbt(sc_, "GTs%d" % i, [128, 8, 512], BF16) for i in range(2)]; r_GTs2 = [res("GTs0"), res("GTs1")]
                Gact = [sbt(sc_, "Gact%d" % i, [128, 8, 512], BF16) for i in range(2)]; r_Gact = [res("Gact0"), res("Gact1")]
                vs = [sbt(sc_, "vs%d" % i, [128, 8, 512], BF16) for i in range(2)]; r_vs = [res("vs0"), res("vs1")]
                cc = [sbt(sc_, "cc%d" % i, [128, 8, 128], F32) for i in range(2)]; r_cc = [res("cc0"), res("cc1")]
                ee = [sbt(sc_, "ee%d" % i, [128, 1024], BF16) for i in range(2)]; r_ee = [res("ee0"), res("ee1")]
                ww = [sbt(sc_, "ww%d" % i, [128, 1024], BF16) for i in range(2)]; r_ww = [res("ww0"), res("ww1")]
                ga = [sbt(sc_, "ga%d" % i, [128, 512], BF16) for i in range(2)]; r_ga = [res("ga0"), res("ga1")]
                oc = [sbt(sc_, "oc%d" % i, [128, 512], F32) for i in range(4)]; r_oc = [res("oc%d" % i) for i in range(4)]
                pgt = [[pst(sc_, "gt%d_%d" % (i, k), [128, 512], F32) for k in range(2)] for i in range(2)]
                r_pgt = [res("pgt0"), res("pgt1")]
                pact = [pst(sc_, "act%d" % i, [128, 512], F32) for i in range(2)]; r_pact = [res("pact0"), res("pact1")]
                pfo = [pst(sc_, "fo%d" % i, [128, 512], F32) for i in range(2)]; r_pfo = [res("pfo0"), res("pfo1")]
                uview = uT_d.rearrange("(i p) (k j) -> i p k j", p=128, j=128)

                def load_u(i):
                    s_ = cnt["u"] % 2; cnt["u"] += 1
                    kb.dma("pool", uTs[s_][:], uview[i], r_uT[s_], writes=[r_uT[s_]])
                    return s_

                def stage1(ch):
                    i0 = ch * 8
                    gt_ = ch % 2
                    for blk in range(4):
                        gb = cnt["gt"] % 2; cnt["gt"] += 1
                        for h in range(8):
                            colx = blk * 8 + h
                            k_ = cnt["ce"] % 2; cnt["ce"] += 1
                            kb.op("dve", lambda e, k_=k_, blk=blk, h=h: e.tensor_tensor(
                                out=cc[k_][:], in0=Ssc[:, blk, 2 * h, i0:i0 + 8].unsqueeze(2).broadcast_to([128, 8, 128]),
                                in1=Ssc[:, blk, 2 * h + 1, :].unsqueeze(1).broadcast_to([128, 8, 128]), op=ALU.mult),
                                reads=[r_S], writes=[r_cc[k_]])
                            kb.op("dve", lambda e, k_=k_, colx=colx: e.scalar_tensor_tensor(out=ww[k_][:], in0=cc[k_][:].rearrange("p a b -> p (a b)"),
                                                                                           scalar=tau[:, colx:colx + 1], in1=cc[k_][:].rearrange("p a b -> p (a b)"),
                                                                                           op0=ALU.is_ge, op1=ALU.mult),
                                  reads=[r_cc[k_], r_tau], writes=[r_ww[k_]])
                            for ii in range(8):
                                kb.op("pe", lambda e, ii=ii, k_=k_, gb=gb, h=h: e.matmul(pgt[gb][ii // 4][:, (ii % 4) * 128:(ii % 4 + 1) * 128],
                                                                                        lhsT=ww[k_][:, ii * 128:(ii + 1) * 128], rhs=ident[:],
                                                                                        start=(h == 0 and ii % 4 == 0), stop=(h == 7 and ii % 4 == 3)),
                                      reads=[r_ww[k_], r_id], writes=[r_pgt[gb]], sig=(ii == 7))
                            yield
                        for k in range(2):
                            dst = GTs2[gt_][:, k * 4:(k + 1) * 4, blk * 128:(blk + 1) * 128]
                            src = pgt[gb][k][:].rearrange("p (a b) -> p a b", b=128)
                            if k == 0:
                                kb.op("act", lambda e, dst=dst, src=src: e.activation(out=dst, in_=src, func=AF.Identity), reads=[r_pgt[gb]], writes=[r_GTs2[gt_]])
                            else:
                                kb.op("dve", lambda e, dst=dst, src=src: e.tensor_copy(out=dst, in_=src), reads=[r_pgt[gb]], writes=[r_GTs2[gt_]])
                        yield

                useq = [(ch, ii) for ch in range(nchunk) for ii in range(8)]
                vseq = [(ch, n) for ch in range(nchunk) for n in range(8)]
                uslot = {}
                vslot = {}

                def load_u_idx(k):
                    if k < len(useq) and k not in uslot:
                        ch_, ii_ = useq[k]
                        uslot[k] = load_u(ch_ * 8 + ii_)

                def load_v_idx(k):
                    if k < len(vseq) and k not in vslot:
                        ch_, n_ = vseq[k]
                        s_ = cnt["vs"] % 2; cnt["vs"] += 1
                        kb.dma("pool", vs[s_][:], vv[:, ch_ * 8:ch_ * 8 + 8, n_ * 512:(n_ + 1) * 512], r_vs[s_], writes=[r_vs[s_]])
                        vslot[k] = s_

                def stage23(ch):
                    gsl = ch % 2
                    gt_ = ch % 2
                    for ii in range(8):
                        ku = ch * 8 + ii
                        load_u_idx(ku)
                        us = uslot.pop(ku)
                        load_u_idx(ku + 1)
                        if ii == 7:
                            load_v_idx(ch * 8)
                        pa = cnt["act"] % 2; cnt["act"] += 1
                        for kc in range(KC):
                            kb.op("pe", lambda e, kc=kc, us=us, pa=pa: e.matmul(pact[pa][:], lhsT=uTs[us][:, kc, :], rhs=h2T[:, kc, :], start=(kc == 0), stop=(kc == KC - 1)),
                                  reads=[r_uT[us], r_h2T], writes=[r_pact[pa]], sig=(kc == KC - 1))
                        g_ = cnt["ga"] % 2; cnt["ga"] += 1
                        kb.op("act", lambda e, pa=pa, g_=g_: e.activation(out=ga[g_][:], in_=pact[pa][:], func=AF.Gelu), reads=[r_pact[pa]], writes=[r_ga[g_]])
                        kb.op("pool", lambda e, ii=ii, g_=g_: e.tensor_tensor(out=Gact[gsl][:, ii, :], in0=ga[g_][:], in1=GTs2[gt_][:, ii, :], op=ALU.mult),
                              reads=[r_ga[g_], r_GTs2[gt_]], writes=[r_Gact[gsl]])
                        yield
                    for n in range(8):
                        kv = ch * 8 + n
                        load_v_idx(kv)
                        vsl = vslot.pop(kv)
                        if n + 1 < 8:
                            load_v_idx(kv + 1)
                        for blk in range(4):
                            po = cnt["out"] % 2; cnt["out"] += 1
                            for ii in range(8):
                                kb.op("pe", lambda e, ii=ii, blk=blk, vsl=vsl, po=po: e.matmul(pfo[po][:], lhsT=Gact[gsl][:, ii, blk * 128:(blk + 1) * 128], rhs=vs[vsl][:, ii, :],
                                                                                              start=(ii == 0), stop=(ii == 7)),
                                      reads=[r_Gact[gsl], r_vs[vsl]], writes=[r_pfo[po]], sig=(ii == 7))
                            o_ = cnt["ot"] % 4; cnt["ot"] += 1
                            kb.op("dve", lambda e, po=po, o_=o_, n=n: e.tensor_tensor(out=oc[o_][:], in0=pfo[po][:], in1=grep[:, n * 512:(n + 1) * 512], op=ALU.mult),
                                  reads=[r_pfo[po], r_grep], writes=[r_oc[o_]])
                            gblk = hf * 4 + blk
                            kb.dma("pool", y_d[gblk * 128:(gblk + 1) * 128, n * 512:(n + 1) * 512], oc[o_][:], r_oc[o_], reads=[r_oc[o_], r_y[gblk][n]],
                                   writes=[r_y[gblk][n]], accum_op=ALU.add)
                        yield

                load_u_idx(0)
                for _ in stage1(0):
                    pass
                for ch in range(nchunk):
                    g23 = stage23(ch)
                    g1n = stage1(ch + 1) if ch + 1 < nchunk else iter(())
                    alive1 = True
                    for _ in g23:
                        for _k in range(3):
                            if alive1:
                                try:
                                    next(g1n)
                                except StopIteration:
                                    alive1 = False
                    if alive1:
                        for _ in g1n:
                            pass
                barrier()
    kb.finish([r for row in r_y for r in row] + [r_al, r_mod])
    return nc


def prep_l2_shared(inp):
    w_in = inp["w_in"][0]
    wga = w_in[:, LY_END:LY_END + D].reshape(KC, 128, KC, 128).transpose(2, 1, 0, 3)
    wgl = w_in[:, GA_END:GA_END + D].reshape(KC, 128, KC, 128).transpose(2, 1, 0, 3)
    wao = inp["w_attn_o"][0].reshape(16, 128, KC, 128).transpose(2, 1, 0, 3)
    wlo = inp["w_lru_o"][0].reshape(16, 128, KC, 128).transpose(2, 1, 0, 3)
    wcat = np.ascontiguousarray(np.concatenate([wga, wgl, wao, wlo], axis=2)).reshape(KC * 128, 96 * 128)
    wout = np.ascontiguousarray(inp["w_out"][0].reshape(KC, 128, 8, 512).transpose(2, 1, 0, 3)).reshape(8 * 128, KC * 512)
    wq = np.ascontiguousarray(inp["peer_wq"][0].reshape(KC, 128, 16, 128).transpose(2, 1, 0, 3)).reshape(16 * 128, KC * 128)
    skT = np.ascontiguousarray(inp["peer_subkeys"][0].reshape(16, 128, 128).transpose(2, 0, 1)).reshape(128, 16 * 128)
    uT = np.ascontiguousarray(inp["peer_u"][0].reshape(128, 128, KC, 128).transpose(0, 3, 2, 1)).reshape(128 * 128, KC * 128)
    v = np.ascontiguousarray(inp["peer_v"][0])
    return {"wcat": wcat, "wout": wout, "wq": wq, "skT": skT, "uT": uT, "v": v,
            "ident": np.eye(128, dtype=np.float32).astype(NPBF)}


def prep_l2(inp, mod, al_all, shared=None):
    if shared is None:
        shared = prep_l2_shared(inp)
    x = inp["x"][0]
    b_in = inp["b_in"][0]
    cf = np.zeros((128, C2_N), np.float32)
    cf[:, C2_SHIFT1:C2_SHIFT1 + KC] = _colT(mod[0:D])
    cf[:, C2_SCALE1:C2_SCALE1 + KC] = _colT(mod[D:2 * D])
    cf[:, C2_G1:C2_G1 + KC] = _colT(inp["norm_mix_g"][0])
    cf[:, C2_SHIFT2:C2_SHIFT2 + KC] = _colT(mod[3 * D:4 * D])
    cf[:, C2_SCALE2:C2_SCALE2 + KC] = _colT(mod[4 * D:5 * D])
    cf[:, C2_G2:C2_G2 + KC] = _colT(inp["norm_ffn_g"][0])
    cf[:, C2_BGA:C2_BGA + KC] = _colT(b_in[LY_END:LY_END + D])
    cf[:, C2_BGL:C2_BGL + KC] = _colT(b_in[GA_END:GA_END + D])
    g1rep = np.ascontiguousarray(np.broadcast_to(mod[None, 2 * D:3 * D], (128, D)))
    g2rep = np.ascontiguousarray(np.broadcast_to(mod[None, 5 * D:6 * D], (128, D)))
    maps = []
    for r in range(NCORE):
        ts = slice(r * TOK2, (r + 1) * TOK2)
        al = np.empty((KC * 128, TOK2), NPBF)
        for kc in range(16):
            al[kc * 128:(kc + 1) * 128] = al_all[kc // 2, (kc % 2) * 128:(kc % 2 + 1) * 128, ts]
            al[(16 + kc) * 128:(17 + kc) * 128] = al_all[kc // 2, (2 + kc % 2) * 128:(3 + kc % 2) * 128, ts]
        m = {"x": np.ascontiguousarray(x[ts]), "cf": cf, "g1rep": g1rep, "g2rep": g2rep, "alT": al}
        m.update(shared)
        maps.append(m)
    return maps


def prep_fused_shared(inp):
    sh = prep_l2_shared(inp)
    w_in = inp["w_in"][0]
    b_in = inp["b_in"][0]
    cfm = np.zeros((128, 2 + 16 * CF_UW), np.float32)
    cfm[:, 0] = inp["q_norm_g"][0]
    cfm[:, 1] = inp["k_norm_g"][0]
    bvrep = np.zeros((128, 16 * 129), np.float32)
    wslab = np.zeros((16 * 128, KC * WCOLS), np.float32)
    wax = np.zeros((128, 16 * 256), np.float32)
    for h in range(16):
        hs = slice(h * 128, (h + 1) * 128)
        base = 2 + h * CF_UW
        cols = [b_in[hs], b_in[Q_END + h * 128:Q_END + (h + 1) * 128],
                b_in[F_END + h * 128:F_END + (h + 1) * 128], b_in[LX_END + h * 128:LX_END + (h + 1) * 128],
                inp["conv_w"][0][0, hs], inp["conv_w"][0][1, hs], inp["conv_w"][0][2, hs], inp["conv_w"][0][3, hs],
                inp["conv_b"][0][hs], inp["lru_ba"][0][hs], inp["lru_bx"][0][hs], inp["lru_lambda"][0][hs]]
        for k, cvec in enumerate(cols):
            cfm[:, base + k] = cvec
        bvrep[:, h * 129:h * 129 + 128] = b_in[None, K_END + h * 128:K_END + (h + 1) * 128]
        bvrep[:, h * 129 + 128] = b_in[V_END + h]
        wcat = np.concatenate([w_in[:, hs], w_in[:, Q_END + h * 128:Q_END + (h + 1) * 128],
                               w_in[:, F_END + h * 128:F_END + (h + 1) * 128], w_in[:, LX_END + h * 128:LX_END + (h + 1) * 128],
                               w_in[:, K_END + h * 128:K_END + (h + 1) * 128], w_in[:, V_END + h:V_END + h + 1]], axis=1)
        wslab[h * 128:(h + 1) * 128] = wcat.reshape(KC, 128, WCOLS).transpose(1, 0, 2).reshape(128, KC * WCOLS)
        wax[:, h * 256:h * 256 + 128] = inp["lru_wa"][0][h]
        wax[:, h * 256 + 128:h * 256 + 256] = inp["lru_wx"][0][h]
    cf = np.zeros((128, C2_N), np.float32)
    cf[:, C2_G1:C2_G1 + KC] = _colT(inp["norm_mix_g"][0])
    cf[:, C2_G2:C2_G2 + KC] = _colT(inp["norm_ffn_g"][0])
    cf[:, C2_BGA:C2_BGA + KC] = _colT(b_in[LY_END:LY_END + D])
    cf[:, C2_BGL:C2_BGL + KC] = _colT(b_in[GA_END:GA_END + D])
    c1 = l1_consts()
    sh.update({"cfm": cfm, "bvrep": bvrep, "wslab": wslab, "wax": wax, "cf": cf,
               "cT": _colT(inp["c"][0]), "w_ada": np.ascontiguousarray(inp["w_ada"][0]), "b_ada": np.ascontiguousarray(inp["b_ada"][0][None]),
               "cmask": c1["cmask"], "identf": c1["identf"], "tri": c1["tri"], "ones": c1["ones"]})
    return sh


def prep_fused_core(inp, shared, nreal_tiles, ntiles=NT):
    x = inp["x"][0]
    pad = ntiles - nreal_tiles
    xpad = np.zeros((ntiles * 256, D), np.float32)
    xpad[pad * 256:] = x[0:nreal_tiles * 256]
    tmask = np.zeros((128, ntiles), np.float32)
    tmask[:, pad:] = 1.0
    m = {"xpad": xpad, "tmask": tmask}
    m.update(shared)
    return m


_CACHE = {}


def kernel(**inputs):
    inp = {k: np.asarray(v) for k, v in inputs.items()}
    cores = list(range(NCORE))
    if "fused" not in _CACHE:
        _CACHE["fused"] = build_l2(2, 16, fz=dict(nunits=16, ntiles=NT))
    shared = prep_fused_shared(inp)
    maps = [prep_fused_core(inp, shared, 4 * (r + 1)) for r in cores]
    res = run_bass_kernel_spmd(_CACHE["fused"], maps, core_ids=cores)
    y = np.concatenate([np.asarray(r["y"]) for r in res.results], axis=0)
    return y[None].astype(np.float32)
```
